# Optimizing a Trainium2 kernel written in Bass

```python
import jax, jax.numpy as jnp
from jax import lax
import numpy as np


D_MODEL = 2048
BATCH = 4
SEQ = 4096
DEPTH = 2
DEC_BATCH = 2
DEC_SEQ = 4096
PAST_LEN = 128

HEAD_DIM = 128
EPS = 1e-6
NEG = -1e30
D_FF = 5632
ROPE_THETA = 500000.0
ROPE_DIM = HEAD_DIM // 4
N_EVEN = (DEPTH + 1) // 2
N_ODD = DEPTH // 2
A_HEADS = 8
A_BRANCHES = ((128, 1), (512, 4), (2048, 16))
B_HEADS = 8
RET_CHUNK = 128
RET_ROPE_THETA = 10000.0
C_HEADS = 4
C_DK = 128
C_DV = 256
GLA_RANK = 16
GLA_TAU = 16.0
GLA_CHUNK = 64
D_HEADS = 8
MLA_Q_RANK = 448
MLA_KV_RANK = 160
MLA_NOPE = 128
MLA_ROPE = 64
MLA_V = 128
MLA_ROPE_THETA = 10000.0
MLA_Q_BLOCK = 128

EVEN_SIZES = (A_HEADS * HEAD_DIM,) * 3 + (B_HEADS * HEAD_DIM,) * 4
ODD_SIZES = (C_HEADS * C_DK, C_HEADS * C_DK, C_HEADS * C_DV, C_HEADS * C_DV, GLA_RANK, GLA_RANK, MLA_Q_RANK, MLA_KV_RANK, MLA_ROPE)
EVEN_IN = sum(EVEN_SIZES)
ODD_IN = sum(ODD_SIZES)
EVEN_MIX = A_HEADS * HEAD_DIM + B_HEADS * HEAD_DIM
ODD_MIX = C_HEADS * C_DV + D_HEADS * MLA_V

kernel_name = 'hybrid_dilated_retention_gla_mla_encoder'


def _offsets(sizes):
    out, acc = [], 0
    for sz in sizes[:-1]:
        acc += sz
        out.append(acc)
    return out


def rmsnorm(x, g):
    xf = x.astype(jnp.float32)
    y = xf * lax.rsqrt(jnp.mean(xf * xf, axis=-1, keepdims=True) + EPS)
    return (y * g.astype(jnp.float32)).astype(x.dtype)


def swiglu(x, w_gate, w_up, w_down):
    return (jax.nn.silu(x @ w_gate) * (x @ w_up)) @ w_down


def rope(x, pos, theta, rot_dim):
    half = rot_dim // 2
    inv = jnp.power(jnp.float32(theta), -jnp.arange(half, dtype=jnp.float32) * (2.0 / rot_dim))
    ang = pos[:, None] * inv[None, :]
    cos, sin = jnp.cos(ang), jnp.sin(ang)
    xf = x.astype(jnp.float32)
    x1, x2 = xf[..., :half], xf[..., half:rot_dim]
    out = jnp.concatenate([x1 * cos - x2 * sin, x1 * sin + x2 * cos, xf[..., rot_dim:]], axis=-1)
    return out.astype(x.dtype)


def split_heads(t, n_heads):
    b, s, w = t.shape
    return t.reshape(b, s, n_heads, w // n_heads).transpose(0, 2, 1, 3)


def merge_heads(t):
    b, h, s, d = t.shape
    return t.transpose(0, 2, 1, 3).reshape(b, s, h * d)


def dilated_branch(q, k, v, window, dilation):
    b, h, s, d = q.shape
    half = window // (2 * dilation)
    L = s // dilation
    nb = -(-L // half)
    Lp = nb * half

    def to_sub(t):
        t = t.reshape(b, h, L, dilation, d).transpose(0, 1, 3, 2, 4)
        t = jnp.pad(t, ((0, 0), (0, 0), (0, 0), (0, Lp - L), (0, 0)))
        return t.reshape(b, h, dilation, nb, half, d)

    def band(t):
        tp = jnp.pad(t, ((0, 0), (0, 0), (0, 0), (1, 1), (0, 0), (0, 0)))
        return jnp.concatenate([tp[:, :, :, :-2], tp[:, :, :, 1:-1], tp[:, :, :, 2:]], axis=4)

    qs = to_sub(q)
    ks = band(to_sub(k))
    vs = band(to_sub(v)).astype(jnp.float32)
    sc = jnp.einsum('bhrnqd,bhrnkd->bhrnqk', qs, ks).astype(jnp.float32) * (d ** -0.5)
    qi = jnp.arange(nb)[:, None] * half + jnp.arange(half)[None, :]
    ki = jnp.arange(nb)[:, None] * half - half + jnp.arange(3 * half)[None, :]
    kib = ki[:, None, :]
    valid = (jnp.abs(kib - qi[:, :, None]) <= half) & (kib >= 0) & (kib < L)
    sc = jnp.where(valid, sc, NEG)
    m = jnp.max(sc, axis=-1, keepdims=True)
    p = jnp.exp(sc - m)
    den = jnp.sum(p, axis=-1, keepdims=True)
    num = jnp.einsum('bhrnqk,bhrnkd->bhrnqd', p, vs)

    def from_sub(t):
        x_dim = t.shape[-1]
        t = t.reshape(b, h, dilation, Lp, x_dim)[:, :, :, :L]
        return t.transpose(0, 1, 3, 2, 4).reshape(b, h, s, x_dim)

    return from_sub(num), from_sub(m), from_sub(den)


def dilated_attention(q, k, v):
    parts = [dilated_branch(q, k, v, w, r) for (w, r) in A_BRANCHES]
    m_all = parts[0][1]
    for part in parts[1:]:
        m_all = jnp.maximum(m_all, part[1])
    num = parts[0][0] * jnp.exp(parts[0][1] - m_all)
    den = parts[0][2] * jnp.exp(parts[0][1] - m_all)
    for part in parts[1:]:
        scale = jnp.exp(part[1] - m_all)
        num = num + part[0] * scale
        den = den + part[2] * scale
    return num / den


def retention_dir(q, k, v, log_gamma, strict):
    b, h, s, d = q.shape
    c = RET_CHUNK
    n = s // c
    qc = q.reshape(b, h, n, c, d)
    kc = k.reshape(b, h, n, c, d)
    vc = v.reshape(b, h, n, c, v.shape[-1])
    idx = jnp.arange(c, dtype=jnp.float32)
    rel = idx[:, None] - idx[None, :]
    mask = (rel > 0) if strict else (rel >= 0)
    lg = log_gamma[:, None, None]
    decay = jnp.where(mask, jnp.exp(lg * jnp.where(mask, rel, 0.0)), 0.0)
    att = jnp.einsum('bhnqd,bhnkd->bhnqk', qc, kc) * decay[None, :, None]
    out = jnp.einsum('bhnqk,bhnkd->bhnqd', att, vc)
    q_dec = jnp.exp(log_gamma[:, None] * (idx + 1.0))
    k_dec = jnp.exp(log_gamma[:, None] * (c - 1.0 - idx))
    chunk_kv = jnp.einsum('bhnkd,bhnke->nbhde', kc * k_dec[None, :, None, :, None], vc)
    chunk_decay = jnp.exp(log_gamma * c)[None, :, None, None]

    def step(state, kv):
        return chunk_decay * state + kv, state

    _, prev = lax.scan(step, jnp.zeros((b, h, d, vc.shape[-1]), jnp.float32), chunk_kv)
    out = out + jnp.einsum('bhnqd,nbhde->bhnqe', qc * q_dec[None, :, None, :, None], prev)
    return out.reshape(b, h, s, vc.shape[-1])


def mixer_even(h, w_in, w_out, ret_decay):
    b, s, _ = h.shape
    f32 = jnp.float32
    pos = jnp.arange(s, dtype=f32)
    qa, ka, va, qb, kb, vb, gb = jnp.split(h @ w_in, _offsets(EVEN_SIZES), axis=-1)
    qa = rope(split_heads(qa, A_HEADS), pos, ROPE_THETA, ROPE_DIM)
    ka = rope(split_heads(ka, A_HEADS), pos, ROPE_THETA, ROPE_DIM)
    oa = dilated_attention(qa, ka, split_heads(va, A_HEADS))
    qb = rope(split_heads(qb, B_HEADS), pos, RET_ROPE_THETA, HEAD_DIM).astype(f32)
    kb = rope(split_heads(kb, B_HEADS), pos, RET_ROPE_THETA, HEAD_DIM).astype(f32) * (HEAD_DIM ** -0.5)
    vb = split_heads(vb, B_HEADS).astype(f32)
    log_gamma = -jnp.exp(ret_decay.astype(f32))
    flip = lambda t: jnp.flip(t, axis=2)
    ob = retention_dir(qb, kb, vb, log_gamma[0], False) + flip(retention_dir(flip(qb), flip(kb), flip(vb), log_gamma[1], True))
    mu = jnp.mean(ob, axis=-1, keepdims=True)
    var = jnp.mean(jnp.square(ob - mu), axis=-1, keepdims=True)
    ob = (ob - mu) * lax.rsqrt(var + EPS)
    ob = ob * jax.nn.silu(split_heads(gb, B_HEADS).astype(f32))
    mixed = jnp.concatenate([merge_heads(oa).astype(h.dtype), merge_heads(ob).astype(h.dtype)], axis=-1)
    return mixed @ w_out


def gla_dir(q, k, v, log_a, strict):
    b, h, s, dk = q.shape
    dv = v.shape[-1]
    c = GLA_CHUNK
    n = s // c

    def chunks(t):
        return t.reshape(b, h, n, c, t.shape[-1]).transpose(2, 0, 1, 3, 4)

    idx = jnp.arange(c)
    mask = (idx[:, None] > idx[None, :]) if strict else (idx[:, None] >= idx[None, :])
    mask = mask[:, :, None]

    def step(state, inp):
        qc, kc, vc, ac = inp
        cum = jnp.cumsum(ac, axis=-2)
        last = cum[:, :, -1:, :]
        o_inter = jnp.einsum('bhtd,bhde->bhte', qc * jnp.exp(cum), state)
        diff = cum[:, :, :, None, :] - cum[:, :, None, :, :]
        gate = jnp.where(mask, jnp.exp(jnp.where(mask, diff, 0.0)), 0.0)
        att = jnp.einsum('bhtd,bhsd,bhtsd->bhts', qc, kc, gate)
        o_intra = jnp.einsum('bhts,bhse->bhte', att, vc)
        new_state = jnp.exp(last[:, :, 0, :, None]) * state + jnp.einsum('bhsd,bhse->bhde', kc * jnp.exp(last - cum), vc)
        return new_state, o_inter + o_intra

    _, out = lax.scan(step, jnp.zeros((b, h, dk, dv), jnp.float32), (chunks(q), chunks(k), chunks(v), chunks(log_a)))
    return out.transpose(1, 2, 0, 3, 4).reshape(b, h, s, dv)


def mla_attention(q_nope, q_rope, k_nope, k_rope, v):
    b, h, s, _ = q_nope.shape
    nq = s // MLA_Q_BLOCK
    scale = (MLA_NOPE + MLA_ROPE) ** -0.5

    def blocks(t):
        return t.reshape(b, h, nq, MLA_Q_BLOCK, t.shape[-1]).transpose(2, 0, 1, 3, 4)

    def attend(qs):
        qn, qr = qs
        sc = (jnp.einsum('bhqd,bhkd->bhqk', qn, k_nope) + jnp.einsum('bhqd,bkd->bhqk', qr, k_rope)).astype(jnp.float32) * scale
        p = jax.nn.softmax(sc, axis=-1)
        return jnp.einsum('bhqk,bhkd->bhqd', p.astype(v.dtype), v)

    out = lax.map(attend, (blocks(q_nope), blocks(q_rope)))
    return out.transpose(1, 2, 0, 3, 4).reshape(b, h, s, v.shape[-1])


def mixer_odd(h, w_in, w_out, gla_gate_w2, gla_gate_b, gla_norm_g, mla_q_norm_g, mla_w_uq, mla_kv_norm_g, mla_w_ukv):
    b, s, _ = h.shape
    f32 = jnp.float32
    pos = jnp.arange(s, dtype=f32)
    qc, kc, vc, rc, af, ab, cq, ckv, kr = jnp.split(h @ w_in, _offsets(ODD_SIZES), axis=-1)
    def log_gate(low, w2, bias):
        z = (low @ w2 + bias).astype(f32)
        return split_heads(jax.nn.log_sigmoid(z) / GLA_TAU, C_HEADS)
    la_f = log_gate(af, gla_gate_w2[0], gla_gate_b[0])
    la_b = log_gate(ab, gla_gate_w2[1], gla_gate_b[1])
    q = split_heads(qc, C_HEADS).astype(f32) * (C_DK ** -0.5)
    k = split_heads(kc, C_HEADS).astype(f32)
    v = split_heads(vc, C_HEADS).astype(f32)
    flip = lambda t: jnp.flip(t, axis=2)
    oc = gla_dir(q, k, v, la_f, False) + flip(gla_dir(flip(q), flip(k), flip(v), flip(la_b), True))
    oc = oc * lax.rsqrt(jnp.mean(oc * oc, axis=-1, keepdims=True) + EPS) * gla_norm_g.astype(f32)
    oc = oc * jax.nn.silu(split_heads(rc, C_HEADS).astype(f32))
    qf = split_heads(rmsnorm(cq, mla_q_norm_g) @ mla_w_uq, D_HEADS)
    q_nope = qf[..., :MLA_NOPE]
    q_rope = rope(qf[..., MLA_NOPE:], pos, MLA_ROPE_THETA, MLA_ROPE)
    kvf = split_heads(rmsnorm(ckv, mla_kv_norm_g) @ mla_w_ukv, D_HEADS)
    k_nope, v_m = kvf[..., :MLA_NOPE], kvf[..., MLA_NOPE:]
    k_rope = rope(kr, pos, MLA_ROPE_THETA, MLA_ROPE)
    od = mla_attention(q_nope, q_rope, k_nope, k_rope, v_m)
    mixed = jnp.concatenate([merge_heads(oc).astype(h.dtype), merge_heads(od).astype(h.dtype)], axis=-1)
    return mixed @ w_out


def trunk(x, norm_g, final_norm_g, ffn_w_gate, ffn_w_up, ffn_w_down, ab_w_in, ab_w_out, ret_decay,
          cd_w_in, cd_w_out, gla_gate_w2, gla_gate_b, gla_norm_g, mla_q_norm_g, mla_w_uq, mla_kv_norm_g, mla_w_ukv):
    h = x
    for i in range(DEPTH):
        j = i // 2
        h = h + 0.5 * swiglu(rmsnorm(h, norm_g[i, 0]), ffn_w_gate[i, 0], ffn_w_up[i, 0], ffn_w_down[i, 0])
        hn = rmsnorm(h, norm_g[i, 1])
        if i % 2 == 0:
            h = h + mixer_even(hn, ab_w_in[j], ab_w_out[j], ret_decay[j])
        else:
            h = h + mixer_odd(hn, cd_w_in[j], cd_w_out[j], gla_gate_w2[j], gla_gate_b[j], gla_norm_g[j],
                              mla_q_norm_g[j], mla_w_uq[j], mla_kv_norm_g[j], mla_w_ukv[j])
        h = h + 0.5 * swiglu(rmsnorm(h, norm_g[i, 2]), ffn_w_gate[i, 1], ffn_w_up[i, 1], ffn_w_down[i, 1])
    return rmsnorm(h, final_norm_g)


def setup_inputs(seed: int = 0) -> dict:
    key = jax.random.key(seed)
    ks = jax.random.split(key, 24)
    f32 = jnp.float32

    def w(k, shape, fan_in):
        return jax.random.normal(k, shape, f32) * (fan_in ** -0.5)

    def gain(k, shape):
        return 1.0 + 0.02 * jax.random.normal(k, shape, f32)

    base = np.log(-np.log1p(-np.power(2.0, -5.0 - np.arange(B_HEADS)))).astype(np.float32)
    ret_decay = jnp.asarray(base)[None, None, :] + 0.05 * jax.random.normal(ks[8], (N_EVEN, 2, B_HEADS), f32)
    return {
        'x_prompt': jax.random.normal(ks[0], (BATCH, SEQ, D_MODEL), f32),
        'x_sample': jax.random.normal(ks[1], (DEC_BATCH, DEC_SEQ, D_MODEL), f32),
        'norm_g': gain(ks[2], (DEPTH, 3, D_MODEL)),
        'final_norm_g': gain(ks[3], (D_MODEL,)),
        'ffn_w_gate': w(ks[4], (DEPTH, 2, D_MODEL, D_FF), D_MODEL),
        'ffn_w_up': w(ks[5], (DEPTH, 2, D_MODEL, D_FF), D_MODEL),
        'ffn_w_down': w(ks[6], (DEPTH, 2, D_FF, D_MODEL), D_FF),
        'ab_w_in': w(ks[7], (N_EVEN, D_MODEL, EVEN_IN), D_MODEL),
        'ab_w_out': w(ks[9], (N_EVEN, EVEN_MIX, D_MODEL), EVEN_MIX),
        'ret_decay': ret_decay,
        'cd_w_in': w(ks[10], (N_ODD, D_MODEL, ODD_IN), D_MODEL),
        'cd_w_out': w(ks[11], (N_ODD, ODD_MIX, D_MODEL), ODD_MIX),
        'gla_gate_w2': w(ks[12], (N_ODD, 2, GLA_RANK, C_HEADS * C_DK), GLA_RANK),
        'gla_gate_b': 0.1 * jax.random.normal(ks[13], (N_ODD, 2, C_HEADS * C_DK), f32),
        'gla_norm_g': gain(ks[14], (N_ODD, C_DV)),
        'mla_q_norm_g': gain(ks[15], (N_ODD, MLA_Q_RANK)),
        'mla_w_uq': w(ks[16], (N_ODD, MLA_Q_RANK, D_HEADS * (MLA_NOPE + MLA_ROPE)), MLA_Q_RANK),
        'mla_kv_norm_g': gain(ks[17], (N_ODD, MLA_KV_RANK)),
        'mla_w_ukv': w(ks[18], (N_ODD, MLA_KV_RANK, D_HEADS * (MLA_NOPE + MLA_V)), MLA_KV_RANK),
    }


def reference(x_prompt, x_sample, norm_g, final_norm_g, ffn_w_gate, ffn_w_up, ffn_w_down, ab_w_in, ab_w_out,
              ret_decay, cd_w_in, cd_w_out, gla_gate_w2, gla_gate_b, gla_norm_g, mla_q_norm_g, mla_w_uq,
              mla_kv_norm_g, mla_w_ukv):
    weights = (norm_g, final_norm_g, ffn_w_gate, ffn_w_up, ffn_w_down, ab_w_in, ab_w_out, ret_decay,
               cd_w_in, cd_w_out, gla_gate_w2, gla_gate_b, gla_norm_g, mla_q_norm_g, mla_w_uq,
               mla_kv_norm_g, mla_w_ukv)
    y_prompt = trunk(x_prompt, *weights)
    y_sample = trunk(x_sample, *weights)
    return (y_prompt, y_sample)
```

```python
import contextlib
import math
import numpy as np
import concourse.bass as bass
import concourse.mybir as mybir
from concourse.bass_utils import run_bass_kernel_spmd

F32 = mybir.dt.float32
BF16 = mybir.dt.bfloat16
AF = mybir.ActivationFunctionType
ALU = mybir.AluOpType

SAME_ENGINE_SYNC = True
N_DMA_SEMS = 8
EPS = 1e-6

FULL_CFG = dict(S=4096, D=2048, FF=5632, AH=8, BH=8, CH=4, DH=8, QR=448, KVR=160, T=512)

C_ID, C_ONE, C_MEAN, C_POS, C_NEG, C_IDX, C_CMA = 0, 128, 256, 384, 512, 640, 1152
C_COLF, C_COLB, C_CC = 1664, 1665, 1666
C_PA, C_PB, C_PM, C_MB, C_MG = 1668, 1796, 1924, 2052, 2308
NCOLS = 2564


class Buf:
    __slots__ = ("w", "r", "excl")

    def __init__(self, excl=False):
        self.w = None
        self.r = []
        self.excl = excl


def bufs(n):
    return [Buf() for _ in range(n)]


class Sched:
    ENGS = ("tensor", "vector", "scalar", "gpsimd", "sync")
    QUEUES = ("sync", "gpsimd")

    def __init__(self, nc, stack):
        self.nc = nc
        self.lists = {e: [] for e in self.ENGS}
        self.sem = {e: stack.enter_context(nc.semaphore("s_" + e)) for e in self.ENGS}
        self.count = {e: 0 for e in self.ENGS}
        self.waited = {e: {} for e in self.ENGS}
        self.dsem, self.dval, self.dnext = {}, {}, {}
        for q in self.QUEUES:
            self.dsem[q] = [stack.enter_context(nc.semaphore("d_%s%d" % (q, i))) for i in range(N_DMA_SEMS)]
            self.dval[q] = [0] * N_DMA_SEMS
            self.dnext[q] = 0
        self.n_ops = 0

    def _wait(self, eng, tok):
        key, val, teng, sem = tok
        if teng == eng and (eng == "tensor" or not SAME_ENGINE_SYNC):
            return
        if self.waited[eng].get(key, 0) >= val:
            return
        self.waited[eng][key] = val
        self.lists[eng].append(("wait", sem, val))

    def _deps(self, eng, reads, writes):
        for b in reads:
            if b.w is not None:
                self._wait(eng, b.w)
            if b.excl:
                for t in b.r:
                    if t[2] != eng:
                        self._wait(eng, t)
        for b in writes:
            if b.w is not None:
                self._wait(eng, b.w)
            for t in b.r:
                self._wait(eng, t)

    @staticmethod
    def _commit(tok, reads, writes):
        for b in reads:
            b.r.append(tok)
        for b in writes:
            b.w = tok
            b.r = []

    def op(self, eng, fn, reads=(), writes=()):
        self._deps(eng, reads, writes)
        self.count[eng] += 1
        tok = ("e_" + eng, self.count[eng], eng, self.sem[eng])
        self.lists[eng].append(("inst", fn, self.sem[eng], 1))
        self._commit(tok, reads, writes)
        self.n_ops += 1
        return tok

    def dma(self, q, fn, reads=(), writes=()):
        i = self.dnext[q]
        self.dnext[q] = (i + 1) % N_DMA_SEMS
        sem = self.dsem[q][i]
        key = "d_%s%d" % (q, i)
        if self.dval[q][i] > 0:
            self._wait(q, (key, self.dval[q][i], "dma", sem))
        self._deps(q, reads, writes)
        self.dval[q][i] += 16
        tok = (key, self.dval[q][i], "dma", sem)
        self.lists[q].append(("inst", fn, sem, 16))
        self._commit(tok, reads, writes)
        self.n_ops += 1
        return tok

    def barrier(self):
        for e in self.ENGS:
            for x in self.ENGS:
                if x != e and self.count[x] > 0:
                    self._wait(e, ("e_" + x, self.count[x], x, self.sem[x]))
            for q in self.QUEUES:
                for i in range(N_DMA_SEMS):
                    if self.dval[q][i] > 0:
                        self._wait(e, ("d_%s%d" % (q, i), self.dval[q][i], "dma", self.dsem[q][i]))

    def flush(self):
        lists = self.lists
        self.lists = {e: [] for e in self.ENGS}

        def run(e, name):
            for ent in lists[name]:
                if ent[0] == "wait":
                    e.wait_ge(ent[1], ent[2])
                else:
                    ent[1](e).then_inc(ent[2], ent[3])

        with self.nc.Block() as block:
            @block.tensor
            def _(e):
                run(e, "tensor")

            @block.vector
            def _(e):
                run(e, "vector")

            @block.scalar
            def _(e):
                run(e, "scalar")

            @block.gpsimd
            def _(e):
                run(e, "gpsimd")

            @block.sync
            def _(e):
                run(e, "sync")

    def finish(self):
        self.barrier()
        self.flush()


def act(S, out, in_, func, r, w, scale=None, bias=None):
    kw = {}
    if scale is not None:
        kw["scale"] = scale
    if bias is not None:
        kw["bias"] = bias
    return S.op("scalar", lambda e: e.activation(out=out, in_=in_, func=func, **kw), r, w)


def tt(S, eng, out, a, b, op, r, w):
    return S.op(eng, lambda e: e.tensor_tensor(out=out, in0=a, in1=b, op=op), r, w)


def ts(S, eng, out, a, s1, op0, r, w, s2=None, op1=None):
    if op1 is None:
        return S.op(eng, lambda e: e.tensor_scalar(out=out, in0=a, scalar1=s1, scalar2=None, op0=op0), r, w)
    return S.op(eng, lambda e: e.tensor_scalar(out=out, in0=a, scalar1=s1, scalar2=s2, op0=op0, op1=op1), r, w)


def stt(S, out, a, scalar, b, op0, op1, r, w):
    return S.op("vector", lambda e: e.scalar_tensor_tensor(out=out, in0=a, scalar=scalar, in1=b, op0=op0, op1=op1), r, w)


def cp(S, eng, out, in_, r, w):
    if eng == "scalar":
        return S.op("scalar", lambda e: e.activation(out=out, in_=in_, func=AF.Copy), r, w)
    return S.op(eng, lambda e: e.tensor_copy(out=out, in_=in_), r, w)


def mm(S, out, pairs, r, w, start=True, stop=True):
    pairs = list(pairs)

    def fn(e):
        n = len(pairs)
        inst = None
        for i, (l, rr) in enumerate(pairs):
            inst = e.matmul(out, l, rr, start=(start and i == 0), stop=(stop and i == n - 1))
        return inst

    return S.op("tensor", fn, r, w)


def tr(S, out, in_, ident, r, w):
    return S.op("tensor", lambda e: e.transpose(out, in_, ident), r, w)


def dma(S, q, out, in_, r, w, slow=False):
    if slow:
        return S.dma(q, lambda e: e.dma_start(out=out, in_=in_, allow_slow_non_contiguous=True), r, w)
    return S.dma(q, lambda e: e.dma_start(out=out, in_=in_), r, w)


def memset(S, eng, ap, val, w):
    return S.op(eng, lambda e: e.memset(ap, val), (), w)


def rows_of(n):
    out = []
    o = 0
    while o < n:
        out.append((o, min(128, n - o)))
        o += 128
    return out


def build(cfg):
    S_, D, FF, AH, BH, CH, DH, QR, KVR, T = (cfg[k] for k in ("S", "D", "FF", "AH", "BH", "CH", "DH", "QR", "KVR", "T"))
    DC, FC, NT, NCH = D // 128, FF // 128, S_ // T, S_ // 128
    MIXE, MIXO = (AH + BH) * 128, CH * 256 + DH * 128
    MEC, MOC = MIXE // 128, MIXO // 128
    XC = max(DC, MEC, MOC)
    EVEN_IN = 3 * AH * 128 + 4 * BH * 128
    ODD_IN = 2 * CH * 128 + 2 * CH * 256 + 32 + QR + KVR + 64
    QCH, KVCH = rows_of(QR), rows_of(KVR)
    TB = T // 128
    FH = FC // 2
    assert FC % 4 == 0

    nc = bass.Bass("TRN2", target_bir_lowering=False)

    def din(name, shape):
        return nc.dram_tensor(name, list(shape), F32, kind="ExternalInput").ap()

    def dscr(name, shape, dt):
        return nc.dram_tensor(name, list(shape), dt, kind="Internal").ap()

    x = din("x", [S_, D])
    norm_g = din("norm_g", [2, 3, D])
    final_g = din("final_norm_g", [D])
    wg = din("ffn_w_gate", [2, 2, D, FF])
    wu = din("ffn_w_up", [2, 2, D, FF])
    wd = din("ffn_w_down", [2, 2, FF, D])
    ab_in = din("ab_w_in", [1, D, EVEN_IN])
    ab_out = din("ab_w_out", [1, MIXE, D])
    ret_decay = din("ret_decay", [1, 2, BH])
    cd_in = din("cd_w_in", [1, D, ODD_IN])
    cd_out = din("cd_w_out", [1, MIXO, D])
    gate_w2 = din("gla_gate_w2", [1, 2, 16, CH * 128])
    gate_b = din("gla_gate_b", [1, 2, CH * 128])
    gla_g = din("gla_norm_g", [1, 256])
    q_g = din("mla_q_norm_g", [1, QR])
    w_uq = din("mla_w_uq", [1, QR, DH * 192])
    kv_g = din("mla_kv_norm_g", [1, KVR])
    w_ukv = din("mla_w_ukv", [1, KVR, DH * 256])
    cst = din("cst", [128, NCOLS])
    ropeA = din("ropeA", [2, 128, S_])
    ropeB = din("ropeB", [2, 128, S_])
    ropeM = din("ropeM", [2, 64, S_])
    y = nc.dram_tensor("y", [S_, D], F32, kind="ExternalOutput").ap()

    hT = dscr("hT", [D, S_], F32)
    mixT = dscr("mixT", [XC * 128, S_], BF16)
    qaT = dscr("qaT", [AH * 128, S_], BF16)
    kaT = dscr("kaT", [AH * 128, S_], BF16)
    va = dscr("va", [S_, AH * 128], BF16)
    qbT = dscr("qbT", [BH * 128, S_], BF16)
    kbT = dscr("kbT", [BH * 128, S_], BF16)
    vb = dscr("vb", [S_, BH * 128], BF16)
    gbT = dscr("gbT", [BH * 128, S_], BF16)
    qcT = dscr("qcT", [CH * 128, S_], BF16)
    kcT = dscr("kcT", [CH * 128, S_], BF16)
    vc = dscr("vc", [S_, CH * 256], BF16)
    rcT = dscr("rcT", [CH * 256, S_], BF16)
    lfT = dscr("lfT", [CH * 128, S_], F32)
    lbT = dscr("lbT", [CH * 128, S_], F32)
    qnT = dscr("qnT", [DH * 128, S_], BF16)
    qrT = dscr("qrT", [DH * 64, S_], BF16)
    knT = dscr("knT", [DH * 128, S_], BF16)
    krT = dscr("krT", [64, S_], BF16)
    vm = dscr("vm", [S_, DH * 128], BF16)
    NBA = 4 * (FF // 256) + 48
    WSB = 48
    wscr_l = [dscr("wscr%d" % i, [WSB, 128, XC * 512], BF16) for i in range((NBA + WSB - 1) // WSB)]

    class _W:
        def __getitem__(self, b):
            return wscr_l[b // WSB][b % WSB]
    wscr = _W()
    wdscr = dscr("wdscr", [8 * DC, 128, FH * 128], BF16)
    wkeys, wdkeys = {}, {}

    with contextlib.ExitStack() as gst:
        S = Sched(nc, gst)

        uniq = [0]

        def sb(stack, name, shape, dt):
            uniq[0] += 1
            return stack.enter_context(nc.sbuf_tensor("%s_%d" % (name, uniq[0]), list(shape), dt))

        cf = sb(gst, "cf", [128, NCOLS], F32)
        cb = sb(gst, "cb", [128, NCOLS], BF16)
        ng = sb(gst, "ng", [128, 6, DC], F32)
        fg = sb(gst, "fg", [128, DC], F32)
        lg = sb(gst, "lg", [128, 2 * BH], F32)
        B_c = Buf()
        PS = [gst.enter_context(nc.psum_tensor("ps%d" % i, [128, 512], F32)) for i in range(8)]
        BPS = [Buf(excl=True) for _ in range(8)]

        dma(S, "sync", cf[:], cst[:, :], (), [B_c])
        cp(S, "vector", cb[:], cf[:], [B_c], [B_c])
        gtmp = sb(gst, "gtmp", [128, 128], F32)
        ftmp = sb(gst, "ftmp", [128, 128], F32)
        dma(S, "sync", gtmp[0:6 * DC, :], norm_g.rearrange("a b (c p) -> (a b c) p", p=128), (), [B_c])
        dma(S, "sync", ftmp[0:DC, :], final_g.rearrange("(c p) -> c p", p=128), (), [B_c])
        tr(S, PS[7][:, 0:6 * DC], gtmp[0:6 * DC, :], cf[0:6 * DC, C_ID:C_ID + 6 * DC], [B_c], [BPS[7]])
        cp(S, "vector", ng[:].rearrange("p k c -> p (k c)"), PS[7][:, 0:6 * DC], [BPS[7]], [B_c])
        tr(S, PS[7][:, 0:DC], ftmp[0:DC, :], cf[0:DC, C_ID:C_ID + DC], [B_c], [BPS[7]])
        cp(S, "vector", fg[:], PS[7][:, 0:DC], [BPS[7]], [B_c])
        dma(S, "sync", lg[:], ret_decay.rearrange("a b c -> (a b c)").partition_broadcast(128), (), [B_c])
        act(S, lg[:], lg[:], AF.Exp, [B_c], [B_c])
        act(S, lg[:], lg[:], AF.Copy, [B_c], [B_c], scale=-1.0)

        ident_f = cf[:, C_ID:C_ID + 128]
        ones_f = cf[:, C_ONE:C_ONE + 128]
        mean_f = cf[:, C_MEAN:C_MEAN + 128]
        ident_b = cb[:, C_ID:C_ID + 128]
        ones_b = cb[:, C_ONE:C_ONE + 128]

        def token_phase(pidx):
            with contextlib.ExitStack() as st:
                hT_t = sb(st, "hT_t", [128, DC, T], F32)
                xn = sb(st, "xn", [128, XC, T], BF16)
                aT = sb(st, "aT", [128, FH, T], BF16)
                wA = [sb(st, "wA%d" % i, [128, XC, 512], BF16) for i in range(2)]
                wD = [sb(st, "wD%d" % i, [128, FH, 128], BF16) for i in range(2)]
                sq = [sb(st, "sq%d" % i, [128, T], BF16) for i in range(2)]
                rstd = sb(st, "rstd", [128, T], F32)
                sg = [sb(st, "sg%d" % i, [128, T], F32) for i in range(2)]
                xio = [sb(st, "xio%d" % i, [128, D], F32) for i in range(2)] if pidx != 1 else None
                stg = [sb(st, "stg%d" % i, [128, T], BF16) for i in range(4)]
                stgf = [sb(st, "stgf%d" % i, [128, T], F32) for i in range(2)] if pidx == 1 else None
                t1b = [sb(st, "t1b%d" % i, [128, T], F32) for i in range(2)]
                t2b = [sb(st, "t2b%d" % i, [128, T], F32) for i in range(2)]
                xbf = [sb(st, "xbf%d" % i, [128, T], BF16) for i in range(2)]
                rope_t = sb(st, "rope_t", [128, 4 if pidx == 0 else 2, T], F32)
                B_h, B_xn, B_a = bufs(DC), bufs(XC), bufs(FH)
                B_wA, B_wD, B_sq, B_sg, B_xio = bufs(2), bufs(2), bufs(2), bufs(2), bufs(2)
                B_stg, B_stgf, B_t1, B_t2, B_xbf = bufs(4), bufs(2), bufs(2), bufs(2), bufs(2)
                B_rstd, B_rope = Buf(), Buf()
                cnt = {"wA": 0, "wD": 0, "stg": 0, "stgf": 0, "rp": 0, "xio": 0}
                if pidx == 1:
                    w2 = sb(st, "w2", [16, 2, CH * 128], BF16)
                    nbias = sb(st, "nbias", [128, 2, CH], F32)
                    wuq_s = sb(st, "wuq_s", [128, len(QCH), DH * 192], BF16)
                    wukv_s = sb(st, "wukv_s", [128, len(KVCH), DH * 256], BF16)
                    qg_s = sb(st, "qg_s", [128, len(QCH)], F32)
                    kvg_s = sb(st, "kvg_s", [128, len(KVCH)], F32)
                    lowf = sb(st, "lowf", [16, 2, T], BF16)
                    cq_s = sb(st, "cq_s", [128, len(QCH), T], F32)
                    cqn = sb(st, "cqn", [128, len(QCH), T], BF16)
                    ckv_s = sb(st, "ckv_s", [128, len(KVCH), T], F32)
                    ckvn = sb(st, "ckvn", [128, len(KVCH), T], BF16)
                    B_od = Buf()
                    B_low, B_cq, B_cqn, B_ckv, B_ckvn = Buf(), Buf(), Buf(), Buf(), Buf()
                    rstd2 = sb(st, "rstd2", [128, T], F32)
                    B_rstd2 = Buf()
                    dma(S, "gpsimd", w2[:], gate_w2[0].rearrange("a r n -> r a n"), (), [B_od])
                    dma(S, "sync", nbias[:], gate_b[0].rearrange("a (c p) -> p a c", p=128), (), [B_od], slow=True)
                    ts(S, "vector", nbias[:], nbias[:], -1.0, ALU.mult, [B_od], [B_od])
                    for ci, (o, rws) in enumerate(QCH):
                        dma(S, "gpsimd", wuq_s[0:rws, ci, :], w_uq[0, o:o + rws, :], (), [B_od])
                        dma(S, "sync", qg_s[0:rws, ci:ci + 1], q_g[0, o:o + rws].rearrange("(p a) -> p a", a=1), (), [B_od])
                    for ci, (o, rws) in enumerate(KVCH):
                        dma(S, "gpsimd", wukv_s[0:rws, ci, :], w_ukv[0, o:o + rws, :], (), [B_od])
                        dma(S, "sync", kvg_s[0:rws, ci:ci + 1], kv_g[0, o:o + rws].rearrange("(p a) -> p a", a=1), (), [B_od])

                def load_wA(src2d, kc, ncols, col0, dst_col0=0, key=None):
                    i = cnt["wA_cur"]
                    if key is None:
                        key = (src2d.tensor.name, src2d.offset, col0, ncols, dst_col0)
                    if key not in wkeys:
                        assert len(wkeys) < NBA
                        wkeys[key] = (len(wkeys), Buf(), cnt["tile"])
                    blk, bb, t_created = wkeys[key]
                    sview = wscr[blk].rearrange("p (c f) -> p c f", f=512)[:, 0:kc, dst_col0:dst_col0 + ncols]
                    if t_created == cnt["tile"]:
                        dma(S, "gpsimd", wA[i][:, 0:kc, dst_col0:dst_col0 + ncols],
                            src2d[:, col0:col0 + ncols].rearrange("(c p) f -> p c f", p=128), (), [B_wA[i]])
                        dma(S, "sync", sview, wA[i][:, 0:kc, dst_col0:dst_col0 + ncols], [B_wA[i]], [bb])
                    else:
                        dma(S, "gpsimd", wA[i][:, 0:kc, dst_col0:dst_col0 + ncols], sview, [bb], [B_wA[i]])

                def next_wA():
                    cnt["wA"] += 1
                    cnt["wA_cur"] = cnt["wA"] % 2
                    return cnt["wA_cur"]

                def sumsq_chunk(c):
                    i = c % 2
                    act(S, sq[i][:], hT_t[:, c, :], AF.Square, [B_h[c]], [B_sq[i]])
                    mm(S, PS[6][:], [(ones_b, sq[i][:])], [B_sq[i], B_c], [BPS[6]], start=(c == 0), stop=(c == DC - 1))
                    if c == DC - 1:
                        cnt["ss"] = True

                def rms_to_xn(gcol):
                    if not cnt.get("ss"):
                        for c in range(DC):
                            sumsq_chunk(c)
                    cnt["ss"] = False
                    act(S, rstd[:], PS[6][:], AF.Ln, [BPS[6]], [B_rstd], scale=1.0 / D, bias=EPS)
                    act(S, rstd[:], rstd[:], AF.Exp, [B_rstd], [B_rstd], scale=-0.5)

                def norm_apply(gap_fn, out_fn, out_bufs):
                    for c in range(DC):
                        stt(S, out_fn(c), hT_t[:, c, :], gap_fn(c), rstd[:], ALU.mult, ALU.mult,
                            [B_h[c], B_rstd, B_c], [out_bufs[c]])
                    if out_bufs is B_xn:
                        cnt["fresh"] = True

                def mm_xn(out, pair_fn, kc, rd, wbuf):
                    if cnt.get("fresh"):
                        cnt["fresh"] = False
                        for c in range(kc):
                            mm(S, out, [pair_fn(c)], rd + [B_xn[c]], [wbuf], start=(c == 0), stop=(c == kc - 1))
                    else:
                        mm(S, out, [pair_fn(c) for c in range(kc)], rd + B_xn[0:kc], [wbuf])

                def ffn(l, w):
                    rms_to_xn(None)
                    gk = l * 3 + (0 if w == 0 else 2)
                    norm_apply(lambda c: ng[:, gk, c:c + 1], lambda c: xn[:, c, :], B_xn)
                    for half in range(2):
                        f0 = half * FH
                        for j in range(FH // 2):
                            i = next_wA()
                            load_wA(wg[l, w], DC, 256, (f0 + 2 * j) * 128, 0, key=("gu", l, w, half, j))
                            load_wA(wu[l, w], DC, 256, (f0 + 2 * j) * 128, 256, key=("gu", l, w, half, j))
                            for fi in range(2):
                                f = 2 * j + fi
                                pg, pu = f % 2, 2 + f % 2
                                mm_xn(PS[pg][:, 0:T], (lambda c, i=i, fi=fi: (wA[i][:, c, fi * 128:(fi + 1) * 128], xn[:, c, :])), DC,
                                      [B_wA[i]], BPS[pg])
                                mm(S, PS[pu][:, 0:T], [(wA[i][:, c, 256 + fi * 128:256 + (fi + 1) * 128], xn[:, c, :]) for c in range(DC)],
                                   [B_wA[i]] + B_xn[0:DC], [BPS[pu]])
                                act(S, sg[f % 2][:], PS[pg][:, 0:T], AF.Silu, [BPS[pg]], [B_sg[f % 2]])
                                tt(S, "vector", aT[:, f, :], sg[f % 2][:], PS[pu][:, 0:T], ALU.mult, [B_sg[f % 2], BPS[pu]], [B_a[f]])
                        for dcn in range(DC):
                            cnt["wD"] += 1
                            i = cnt["wD"] % 2
                            key = (l, w, half, dcn)
                            if key not in wdkeys:
                                blk = len(wdkeys)
                                wdkeys[key] = (blk, Buf())
                                dma(S, "gpsimd", wD[i][:], wd[l, w][f0 * 128:(f0 + FH) * 128, dcn * 128:(dcn + 1) * 128].rearrange("(f p) d -> p f d", p=128),
                                    (), [B_wD[i]])
                                dma(S, "sync", wdscr[blk].rearrange("p (f d) -> p f d", d=128), wD[i][:], [B_wD[i]], [wdkeys[key][1]])
                            else:
                                blk, bb = wdkeys[key]
                                dma(S, "gpsimd", wD[i][:], wdscr[blk].rearrange("p (f d) -> p f d", d=128), [bb], [B_wD[i]])
                            pb = 4 + dcn % 2
                            mm(S, PS[pb][:, 0:T], [(wD[i][:, f, :], aT[:, f, :]) for f in range(FH)], [B_wD[i]] + B_a, [BPS[pb]])
                            stt(S, hT_t[:, dcn, :], PS[pb][:, 0:T], 0.5, hT_t[:, dcn, :], ALU.mult, ALU.add, [BPS[pb]], [B_h[dcn]])
                            if half == 1:
                                if dcn >= 2:
                                    sumsq_chunk(dcn - 2)
                                if dcn == DC - 1:
                                    for c_ in range(max(DC - 2, 0), DC):
                                        sumsq_chunk(c_)

                def out_proj(w_out2d, mc):
                    for cb0 in range(0, D, 512):
                        ncol = min(512, D - cb0)
                        i = next_wA()
                        load_wA(w_out2d, mc, ncol, cb0)
                        for dd in range(ncol // 128):
                            dcn = cb0 // 128 + dd
                            pb = 4 + dcn % 2
                            mm(S, PS[pb][:, 0:T], [(wA[i][:, m, dd * 128:(dd + 1) * 128], xn[:, m, :]) for m in range(mc)],
                               [B_wA[i]] + B_xn[0:mc], [BPS[pb]])
                            tt(S, "vector", hT_t[:, dcn, :], hT_t[:, dcn, :], PS[pb][:, 0:T], ALU.add, [BPS[pb]], [B_h[dcn]])
                            if dcn >= 2:
                                sumsq_chunk(dcn - 2)
                            if dcn == DC - 1:
                                for c_ in range(max(DC - 2, 0), DC):
                                    sumsq_chunk(c_)

                def get_stg():
                    cnt["stg"] += 1
                    return cnt["stg"] % 4

                def rope_fm(ps_i, rows, ctab, stab, perm, out_dram):
                    cnt["rp"] += 1
                    k = cnt["rp"] % 2
                    RP = cfg.get("rp", 9)
                    if RP < 1:
                        return
                    cp(S, "scalar", xbf[k][0:rows, :], PS[ps_i][0:rows, 0:T], [BPS[ps_i]], [B_xbf[k]])
                    if RP < 2:
                        return
                    mm(S, PS[7][0:rows, 0:T], [(perm, xbf[k][0:rows, :])], [B_xbf[k], B_c], [BPS[7]])
                    if RP < 3:
                        return
                    tt(S, "vector", t1b[k][0:rows, :], PS[ps_i][0:rows, 0:T], ctab, ALU.mult, [BPS[ps_i], B_rope], [B_t1[k]])
                    tt(S, "vector", t2b[k][0:rows, :], PS[7][0:rows, 0:T], stab, ALU.mult, [BPS[7], B_rope], [B_t2[k]])
                    if RP < 4:
                        return
                    si = get_stg()
                    tt(S, "vector", stg[si][0:rows, :], t1b[k][0:rows, :], t2b[k][0:rows, :], ALU.add, [B_t1[k], B_t2[k]], [B_stg[si]])
                    if RP < 5:
                        return
                    dma(S, "sync", out_dram, stg[si][0:rows, :], [B_stg[si]], ())

                def proj_fm(i, col, ncols_out, kc, ps_i):
                    mm_xn(PS[ps_i][0:ncols_out, 0:T], (lambda c: (wA[i][:, c, col:col + ncols_out], xn[:, c, :])), kc,
                          [B_wA[i]], BPS[ps_i])

                def proj_tm(w2d, col0, ncols, out_dram2d, t0):
                    for cbk in range(0, ncols, 512):
                        nb_ = min(512, ncols - cbk)
                        i = next_wA()
                        load_wA(w2d, DC, nb_, col0 + cbk)
                        for b in range(TB):
                            pb = 4 + b % 2
                            mm_xn(PS[pb][:, 0:nb_], (lambda c, i=i, b=b, nb_=nb_: (xn[:, c, b * 128:(b + 1) * 128], wA[i][:, c, 0:nb_])), DC,
                                  [B_wA[i]], BPS[pb])
                            si = get_stg()
                            cp(S, "scalar", stg[si][:, 0:nb_], PS[pb][:, 0:nb_], [BPS[pb]], [B_stg[si]])
                            dma(S, "sync", out_dram2d[t0 + b * 128:t0 + (b + 1) * 128, cbk:cbk + nb_], stg[si][:, 0:nb_], [B_stg[si]], ())

                def sect_fm(w2d, col0, nfeat, handler):
                    for cbk in range(0, nfeat, 512):
                        nb_ = min(512, nfeat - cbk)
                        i = next_wA()
                        load_wA(w2d, DC, nb_, col0 + cbk)
                        for (o, rws) in rows_of(nb_):
                            ps_i = (cnt["rp"] + o // 128) % 2
                            ps_i = 0 if (o // 128) % 2 == 0 else 1
                            proj_fm(i, o, rws, DC, ps_i)
                            handler(ps_i, cbk + o, rws)

                DBG = cfg.get("dbg", 99)
                for t in range(min(NT, cfg.get("ntiles", NT))):
                    t0 = t * T
                    tsl = slice(t0, t0 + T)
                    cnt["tile"] = (pidx, t)
                    if pidx == 0:
                        for b in range(TB):
                            cnt["xio"] += 1
                            k = cnt["xio"] % 2
                            dma(S, "sync", xio[k][:], x[t0 + b * 128:t0 + (b + 1) * 128, :], (), [B_xio[k]])
                            for c0 in range(0, DC, 4):
                                nn = min(4, DC - c0)
                                for cc in range(nn):
                                    c = c0 + cc
                                    tr(S, PS[7][:, cc * 128:(cc + 1) * 128], xio[k][:, c * 128:(c + 1) * 128], ident_f,
                                       [B_xio[k], B_c], [BPS[7]])
                                cp(S, "vector", hT_t[:, c0:c0 + nn, b * 128:(b + 1) * 128],
                                   PS[7][:, 0:nn * 128].rearrange("p (c t) -> p c t", t=128), [BPS[7]], B_h[c0:c0 + nn])
                    else:
                        dma(S, "sync", hT_t[:], hT[:, tsl].rearrange("(c p) t -> p c t", p=128), (), B_h)
                        mc = MEC if pidx == 1 else MOC
                        dma(S, "sync", xn[:, 0:mc, :], mixT[0:mc * 128, tsl].rearrange("(c p) t -> p c t", p=128), (), B_xn[0:mc])
                        out_proj(ab_out[0] if pidx == 1 else cd_out[0], mc)
                    if DBG < 2:
                        continue
                    if pidx == 0:
                        ffn(0, 0)
                    elif pidx == 1:
                        ffn(0, 1)
                        ffn(1, 0)
                    else:
                        ffn(1, 1)
                    if pidx == 2:
                        rms_to_xn(None)
                        norm_apply(lambda c: fg[:, c:c + 1], lambda c: hT_t[:, c, :], B_h)
                        for b in range(TB):
                            cnt["xio"] += 1
                            k = cnt["xio"] % 2
                            for c0 in range(0, DC, 4):
                                nn = min(4, DC - c0)
                                for cc in range(nn):
                                    c = c0 + cc
                                    tr(S, PS[7][:, cc * 128:(cc + 1) * 128], hT_t[:, c, b * 128:(b + 1) * 128], ident_f,
                                       [B_h[c], B_c], [BPS[7]])
                                cp(S, "vector", xio[k][:, c0 * 128:(c0 + nn) * 128], PS[7][:, 0:nn * 128], [BPS[7]], [B_xio[k]])
                            dma(S, "sync", y[t0 + b * 128:t0 + (b + 1) * 128, :], xio[k][:], [B_xio[k]], ())
                        continue
                    if DBG < 3:
                        continue
                    dma(S, "sync", hT[:, tsl].rearrange("(c p) t -> p c t", p=128), hT_t[:], B_h, ())
                    rms_to_xn(None)
                    gk = pidx * 3 + 1
                    norm_apply(lambda c: ng[:, gk, c:c + 1], lambda c: xn[:, c, :], B_xn)
                    if DBG < 4:
                        continue
                    if pidx == 0:
                        w2d = ab_in[0]
                        dma(S, "sync", rope_t[:, 0:2, :], ropeA[:, :, tsl].rearrange("a p t -> p a t"), (), [B_rope])
                        dma(S, "sync", rope_t[:, 2:4, :], ropeB[:, :, tsl].rearrange("a p t -> p a t"), (), [B_rope])
                        PA = cb[:, C_PA:C_PA + 128]
                        PB = cb[:, C_PB:C_PB + 128]
                        o_qa, o_ka, o_va = 0, AH * 128, 2 * AH * 128
                        o_qb = 3 * AH * 128
                        o_kb, o_vb, o_gb = o_qb + BH * 128, o_qb + 2 * BH * 128, o_qb + 3 * BH * 128
                        def h_silu(p, fo, r, dst=gbT):
                            si = get_stg()
                            act(S, stg[si][0:r, :], PS[p][0:r, 0:T], AF.Silu, [BPS[p]], [B_stg[si]])
                            dma(S, "sync", dst[fo:fo + r, tsl], stg[si][0:r, :], [B_stg[si]], ())
                        sects = [
                            lambda: sect_fm(w2d, o_qa, AH * 128, lambda p, fo, r: rope_fm(p, r, rope_t[0:r, 0, :], rope_t[0:r, 1, :], PA, qaT[fo:fo + r, tsl])),
                            lambda: sect_fm(w2d, o_ka, AH * 128, lambda p, fo, r: rope_fm(p, r, rope_t[0:r, 0, :], rope_t[0:r, 1, :], PA, kaT[fo:fo + r, tsl])),
                            lambda: proj_tm(w2d, o_va, AH * 128, va, t0),
                            lambda: sect_fm(w2d, o_qb, BH * 128, lambda p, fo, r: rope_fm(p, r, rope_t[0:r, 2, :], rope_t[0:r, 3, :], PB, qbT[fo:fo + r, tsl])),
                            lambda: sect_fm(w2d, o_kb, BH * 128, lambda p, fo, r: rope_fm(p, r, rope_t[0:r, 2, :], rope_t[0:r, 3, :], PB, kbT[fo:fo + r, tsl])),
                            lambda: proj_tm(w2d, o_vb, BH * 128, vb, t0),
                            lambda: sect_fm(w2d, o_gb, BH * 128, h_silu),
                        ]
                        for f_ in sects[:cfg.get("nsect", 7)]:
                            f_()
                    else:
                        w2d = cd_in[0]
                        dma(S, "sync", rope_t[0:64, 0:2, :], ropeM[:, :, tsl].rearrange("a p t -> p a t"), (), [B_rope])
                        PM = cb[0:64, C_PM:C_PM + 64]
                        o_qc, o_kc, o_vc = 0, CH * 128, 2 * CH * 128
                        o_rc = o_vc + CH * 256
                        o_af = o_rc + CH * 256
                        o_ab, o_cq = o_af + 16, o_af + 32
                        o_ckv = o_cq + QR
                        o_kr = o_ckv + KVR

                        def h_plain(dst):
                            def hh(p, fo, r):
                                si = get_stg()
                                cp(S, "scalar", stg[si][0:r, :], PS[p][0:r, 0:T], [BPS[p]], [B_stg[si]])
                                dma(S, "sync", dst[fo:fo + r, tsl], stg[si][0:r, :], [B_stg[si]], ())
                            return hh
                        def h_silu2(p, fo, r):
                            si = get_stg()
                            act(S, stg[si][0:r, :], PS[p][0:r, 0:T], AF.Silu, [BPS[p]], [B_stg[si]])
                            dma(S, "sync", rcT[fo:fo + r, tsl], stg[si][0:r, :], [B_stg[si]], ())

                        blk = []
                        for (cbk, nb_) in ((0, 32 + QR), (32 + QR, KVR + 64)):
                            assert nb_ <= 512
                            i = next_wA()
                            load_wA(w2d, DC, nb_, o_af + cbk)
                            blk.append((cbk, nb_, i))

                        def small_proj(goff, rows, ps_i):
                            for (cbk, nb_, i) in blk:
                                if cbk <= goff and goff + rows <= cbk + nb_:
                                    proj_fm(i, goff - cbk, rows, DC, ps_i)
                                    return
                            raise AssertionError("straddle %d %d" % (goff, rows))

                        for d_ in range(2):
                            small_proj(16 * d_, 16, d_)
                            cp(S, "scalar", lowf[:, d_, :], PS[d_][0:16, 0:T], [BPS[d_]], [B_low])

                        def latent_raw(goff, chs, raw, B_raw, ps_sum):
                            for ci, (o, rws) in enumerate(chs):
                                pr = ci % 2
                                small_proj(goff + o, rws, pr)
                                cp(S, "scalar", raw[0:rws, ci, :], PS[pr][0:rws, 0:T], [BPS[pr]], [B_raw])
                                act(S, sq[ci % 2][0:rws, :], PS[pr][0:rws, 0:T], AF.Square, [BPS[pr]], [B_sq[ci % 2]])
                                mm(S, PS[ps_sum][:, 0:T], [(ones_b[0:rws, :], sq[ci % 2][0:rws, :])], [B_sq[ci % 2], B_c], [BPS[ps_sum]],
                                   start=(ci == 0), stop=(ci == len(chs) - 1))

                        def latent_norm(chs, nfeat, raw, B_raw, nrm, B_nrm, gtile, ps_sum, rs_t, B_rs):
                            act(S, rs_t[:], PS[ps_sum][:, 0:T], AF.Ln, [BPS[ps_sum]], [B_rs], scale=1.0 / nfeat, bias=EPS)
                            act(S, rs_t[:], rs_t[:], AF.Exp, [B_rs], [B_rs], scale=-0.5)
                            for ci, (o, rws) in enumerate(chs):
                                stt(S, nrm[0:rws, ci, :], raw[0:rws, ci, :], gtile[0:rws, ci:ci + 1], rs_t[0:rws, :], ALU.mult, ALU.mult,
                                    [B_raw, B_rs, B_od], [B_nrm])

                        latent_raw(32, QCH, cq_s, B_cq, 2)
                        latent_raw(32 + QR, KVCH, ckv_s, B_ckv, 3)
                        small_proj(32 + QR + KVR, 64, 1)
                        rope_fm(1, 64, rope_t[0:64, 0, :], rope_t[0:64, 1, :], PM, krT[0:64, tsl])
                        latent_norm(QCH, QR, cq_s, B_cq, cqn, B_cqn, qg_s, 2, rstd, B_rstd)
                        latent_norm(KVCH, KVR, ckv_s, B_ckv, ckvn, B_ckvn, kvg_s, 3, rstd2, B_rstd2)

                        sect_fm(w2d, o_qc, CH * 128, h_plain(qcT))
                        sect_fm(w2d, o_kc, CH * 128, h_plain(kcT))
                        proj_tm(w2d, o_vc, CH * 256, vc, t0)
                        sect_fm(w2d, o_rc, CH * 256, h_silu2)

                        for h in range(DH):
                            mm(S, PS[0][:, 0:T], [(wuq_s[0:rws, ci, h * 192:h * 192 + 128], cqn[0:rws, ci, :]) for ci, (o, rws) in enumerate(QCH)],
                               [B_od, B_cqn], [BPS[0]])
                            si = get_stg()
                            cp(S, "scalar", stg[si][:], PS[0][:, 0:T], [BPS[0]], [B_stg[si]])
                            dma(S, "sync", qnT[h * 128:(h + 1) * 128, tsl], stg[si][:], [B_stg[si]], ())
                            mm(S, PS[1][0:64, 0:T], [(wuq_s[0:rws, ci, h * 192 + 128:h * 192 + 192], cqn[0:rws, ci, :]) for ci, (o, rws) in enumerate(QCH)],
                               [B_od, B_cqn], [BPS[1]])
                            rope_fm(1, 64, rope_t[0:64, 0, :], rope_t[0:64, 1, :], PM, qrT[h * 64:(h + 1) * 64, tsl])
                        for h in range(DH):
                            pk = 2 + h % 2
                            mm(S, PS[pk][:, 0:T], [(wukv_s[0:rws, ci, h * 256:h * 256 + 128], ckvn[0:rws, ci, :]) for ci, (o, rws) in enumerate(KVCH)],
                               [B_od, B_ckvn], [BPS[pk]])
                            si = get_stg()
                            cp(S, "scalar", stg[si][:], PS[pk][:, 0:T], [BPS[pk]], [B_stg[si]])
                            dma(S, "sync", knT[h * 128:(h + 1) * 128, tsl], stg[si][:], [B_stg[si]], ())
                        for b in range(TB):
                            for h0 in range(0, DH, 4):
                                nh = min(4, DH - h0)
                                pb = 4 + b % 2
                                for hh in range(nh):
                                    h = h0 + hh
                                    mm(S, PS[pb][:, hh * 128:(hh + 1) * 128],
                                       [(ckvn[0:rws, ci, b * 128:(b + 1) * 128], wukv_s[0:rws, ci, h * 256 + 128:h * 256 + 256]) for ci, (o, rws) in enumerate(KVCH)],
                                       [B_od, B_ckvn], [BPS[pb]])
                                si = get_stg()
                                cp(S, "scalar", stg[si][:, 0:nh * 128], PS[pb][:, 0:nh * 128], [BPS[pb]], [B_stg[si]])
                                dma(S, "sync", vm[t0 + b * 128:t0 + (b + 1) * 128, h0 * 128:(h0 + nh) * 128], stg[si][:, 0:nh * 128], [B_stg[si]], ())
                        for d_ in range(2):
                            dstT = lfT if d_ == 0 else lbT
                            for hc in range(CH):
                                pgt = (d_ * CH + hc) % 2
                                mm(S, PS[pgt][:, 0:T], [(w2[:, d_, hc * 128:(hc + 1) * 128], lowf[:, d_, :])], [B_od, B_low], [BPS[pgt]])
                                k = cnt["stgf"] = cnt["stgf"] + 1
                                k %= 2
                                act(S, stgf[k][:], PS[pgt][:, 0:T], AF.Exp, [BPS[pgt], B_od], [B_stgf[k]], scale=-1.0, bias=nbias[:, d_, hc:hc + 1])
                                act(S, stgf[k][:], stgf[k][:], AF.Ln, [B_stgf[k]], [B_stgf[k]], scale=1.0, bias=1.0)
                                dma(S, "sync", dstT[hc * 128:(hc + 1) * 128, tsl], stgf[k][:], [B_stgf[k]], ())
                S.barrier()
                S.flush()

        def mixer_even():
            scale = 128.0 ** -0.5
            with contextlib.ExitStack() as st:
                NB = S_ // 128
                qT2 = [sb(st, "a_q%d" % j, [128, S_], BF16) for j in range(2)]
                kT2 = [sb(st, "a_k%d" % j, [128, S_], BF16) for j in range(2)]
                vbr2 = [[sb(st, "a_v%d_%d" % (i, j), [128, NB, 128], BF16) for i in range(3)] for j in range(2)]
                B_q2, B_k2, B_v2 = bufs(2), bufs(2), [bufs(3) for _ in range(2)]
                accn = sb(st, "a_n", [128, S_], F32)
                accd = sb(st, "a_d", [128, S_], F32)
                eb = [sb(st, "a_e%d" % i, [128, 256], BF16) for i in range(3)]
                em = [sb(st, "a_em%d" % i, [128, 256], BF16) for i in range(3)]
                obf = sb(st, "a_o", [128, S_], BF16)
                junk = sb(st, "a_junk", [128, 4], F32)
                B_junk = Buf()
                B_n, B_d, B_o = Buf(), Buf(), Buf()
                B_e, B_em = bufs(3), bufs(3)
                MB = cb[:, C_MB:C_MB + 256]
                SB_ = (0, 1, 6)
                it = 0
                def load_head(h):
                    j = h % 2
                    hs = slice(h * 128, (h + 1) * 128)
                    dma(S, "sync", qT2[j][:], qaT[hs, :], (), [B_q2[j]])
                    dma(S, "sync", kT2[j][:], kaT[hs, :], (), [B_k2[j]])
                    for bi, dil in enumerate((1, 4, 16)):
                        L = S_ // dil
                        nb = L // 128
                        for r in range(dil):
                            dma(S, "sync", vbr2[j][bi][:, r * nb:(r + 1) * nb, :],
                                va[r::dil, hs].rearrange("(b p) e -> p b e", p=128), (), [B_v2[j][bi]])

                load_head(0)
                for h in range(AH):
                    hs = slice(h * 128, (h + 1) * 128)
                    qT, kT, vbr = qT2[h % 2], kT2[h % 2], vbr2[h % 2]
                    B_q, B_k, B_v = B_q2[h % 2], B_k2[h % 2], B_v2[h % 2]
                    if h + 1 < AH:
                        load_head(h + 1)
                    B_nr = {(bi, r): Buf() for bi, dil in enumerate((1, 4, 16)) for r in range(dil)}
                    B_dr = {(bi, r): Buf() for bi, dil in enumerate((1, 4, 16)) for r in range(dil)}
                    memset(S, "gpsimd", accn[:], 0.0, [B_n] + [B_nr[(0, 0)]])
                    memset(S, "gpsimd", accd[:], 0.0, [B_d] + [B_dr[(0, 0)]])
                    blocks = []
                    for bi, dil in enumerate((1, 4, 16)):
                        L = S_ // dil
                        nb = L // 128
                        for b in range(nb):
                            for r in range(dil):
                                blocks.append((bi, dil, L, nb, r, b))

                    def geom(blk):
                        bi, dil, L, nb, r, b = blk
                        j0 = 128 * b
                        qlo, qhi = max(0, j0 - 64), min(L, j0 + 192)
                        return j0, qlo, qhi, qhi - qlo, qlo - (j0 - 64)

                    def issue_scores(x):
                        bi, dil, L, nb, r, b = blocks[x]
                        j0, qlo, qhi, n, flo = geom(blocks[x])
                        kc = kT[:, r + dil * j0:r + dil * (j0 + 127) + 1:dil]
                        qc = qT[:, r + dil * qlo:r + dil * (qhi - 1) + 1:dil]
                        sbk = SB_[(it + x) % 3]
                        mm(S, PS[sbk][:, 0:n], [(kc, qc)], [B_k, B_q], [BPS[sbk]])

                    issue_scores(0)
                    issue_scores(1)
                    for x, blk in enumerate(blocks):
                        bi, dil, L, nb, r, b = blk
                        j0, qlo, qhi, n, flo = geom(blk)
                        k3 = (it + x) % 3
                        sbk = SB_[k3]
                        k = (it + x) % 2
                        if x + 2 < len(blocks):
                            issue_scores(x + 2)
                        act(S, eb[k3][:, 0:n], PS[sbk][:, 0:n], AF.Exp, [BPS[sbk]], [B_e[k3]], scale=scale)
                        tt(S, "gpsimd", em[k3][:, 0:n], eb[k3][:, 0:n], MB[:, flo:flo + n], ALU.mult, [B_e[k3], B_c], [B_em[k3]])
                        mm(S, PS[2 + k][:, 0:n], [(vbr[bi][:, r * nb + b, :], em[k3][:, 0:n])], [B_v[bi], B_em[k3]], [BPS[2 + k]])
                        mm(S, PS[4 + k][:, 0:n], [(ones_b, em[k3][:, 0:n])], [B_c, B_em[k3]], [BPS[4 + k]])
                        cs = slice(r + dil * qlo, r + dil * (qhi - 1) + 1, dil)
                        if x > 0 and blocks[x - 1][0] != bi:
                            pbi = blocks[x - 1][0]
                            pdil = (1, 4, 16)[pbi]
                            S.op("vector", lambda e: e.memset(junk[:, 0:1], 0.0),
                                 [B_nr[(pbi, rr)] for rr in range(pdil)] + [B_dr[(pbi, rr)] for rr in range(pdil)],
                                 [B_nr[(bi, rr)] for rr in range(dil)] + [B_dr[(bi, rr)] for rr in range(dil)] + [B_junk])
                        tt(S, "vector", accn[:, cs], accn[:, cs], PS[2 + k][:, 0:n], ALU.add, [BPS[2 + k]], [B_nr[(bi, r)]])
                        tt(S, "vector", accd[:, cs], accd[:, cs], PS[4 + k][:, 0:n], ALU.add, [BPS[4 + k]], [B_dr[(bi, r)]])
                    it += len(blocks)
                    S.op("vector", lambda e: e.memset(junk[:, 0:1], 0.0),
                         [B_nr[(2, rr)] for rr in range(16)] + [B_dr[(2, rr)] for rr in range(16)], [B_n, B_d, B_junk])
                    S.op("vector", lambda e: e.reciprocal(out=accd[:], in_=accd[:]), [B_d], [B_d])
                    tt(S, "gpsimd", obf[:], accn[:], accd[:], ALU.mult, [B_n, B_d], [B_o])
                    dma(S, "sync", mixT[hs, :], obf[:], [B_o], ())
                S.barrier()
                S.flush()

        def mixer_ret():
            lns = math.log(128.0 ** -0.5)
            NGR = S_ // 512
            with contextlib.ExitStack() as st:
                PSb = [PS[i][:].bitcast(BF16) for i in range(8)]

                class Slot:
                    pass
                slots = []
                for si in range(2):
                    L = Slot()
                    n = "b%d_" % si
                    L.qT = sb(st, n + "q", [128, S_], BF16)
                    L.kT = sb(st, n + "k", [128, S_], BF16)
                    L.gT = sb(st, n + "g", [128, S_], BF16)
                    L.v = sb(st, n + "v", [128, NCH, 128], BF16)
                    L.SfA = sb(st, n + "sf", [128, NCH, 128], BF16)
                    L.SbA = sb(st, n + "sb", [128, NCH, 128], BF16)
                    L.Sf2 = [sb(st, n + "sfr%d" % i, [128, 128], F32) for i in range(2)]
                    L.kdall = [sb(st, n + "kdall%d" % i, [128, NCH, 128], BF16) for i in range(2)]
                    L.DT = sb(st, n + "dt", [128, 128], F32)
                    L.DT2 = sb(st, n + "dt2", [128, 128], F32)
                    L.qdf = sb(st, n + "qdf", [128, 512], F32)
                    L.qdb = sb(st, n + "qdb", [128, 512], F32)
                    L.kd = sb(st, n + "kd", [128, 4], F32)
                    L.qf = [sb(st, n + "qf%d" % i, [128, 512], BF16) for i in range(2)]
                    L.qb_ = [sb(st, n + "qb%d" % i, [128, 512], BF16) for i in range(2)]
                    L.A = [sb(st, n + "A%d" % i, [128, 128], BF16) for i in range(2)]
                    L.o = sb(st, n + "o", [128, 512], F32)
                    L.o2 = sb(st, n + "o2", [128, 512], F32)
                    L.msq = sb(st, n + "msq", [128, 512], F32)
                    L.var = sb(st, n + "var", [128, 512], F32)
                    L.res = sb(st, n + "res", [128, 512], F32)
                    L.ob = [sb(st, n + "ob%d" % i, [128, 512], BF16) for i in range(2)]
                    (L.B_q, L.B_k, L.B_g, L.B_v, L.B_sfa, L.B_sba, L.B_hc, L.B_o, L.B_o2, L.B_msq, L.B_var, L.B_res) = (Buf() for _ in range(12))
                    L.B_sf2, L.B_kdall, L.B_A, L.B_ob, L.B_qf, L.B_qb = bufs(2), bufs(2), bufs(2), bufs(2), bufs(2), bufs(2)
                    L.pt, L.po = si, (2 + 2 * si, 3 + 2 * si)
                    L.pk = L.po[0]
                    slots.append(L)

                def head_gen(h, L):
                    hs = slice(h * 128, (h + 1) * 128)
                    lgf, lgb = lg[:, h:h + 1], lg[:, BH + h:BH + h + 1]
                    dma(S, "sync", L.qT[:], qbT[hs, :], (), [L.B_q])
                    dma(S, "sync", L.kT[:], kbT[hs, :], (), [L.B_k])
                    dma(S, "sync", L.gT[:], gbT[hs, :], (), [L.B_g])
                    dma(S, "sync", L.v[:], vb[:, hs].rearrange("(b p) e -> p b e", p=128), (), [L.B_v])
                    yield
                    act(S, L.DT[:], cf[:, C_POS:C_POS + 128], AF.Exp, [B_c], [L.B_hc], scale=lgf, bias=lns)
                    act(S, L.DT2[:], cf[:, C_NEG:C_NEG + 128], AF.Exp, [B_c], [L.B_hc], scale=lgb)
                    tt(S, "vector", L.DT[:], L.DT[:], L.DT2[:], ALU.mult, [L.B_hc], [L.B_hc])
                    act(S, L.qdf[:], cf[:, C_IDX:C_IDX + 512], AF.Exp, [B_c], [L.B_hc], scale=lgf, bias=lns)
                    act(S, L.qdb[:], cf[:, C_CMA:C_CMA + 512], AF.Exp, [B_c], [L.B_hc], scale=lgb, bias=lns)
                    act(S, L.kd[:, 0:1], cf[:, C_COLF:C_COLF + 1], AF.Exp, [B_c], [L.B_hc], scale=lgf)
                    act(S, L.kd[:, 1:2], cf[:, C_COLB:C_COLB + 1], AF.Exp, [B_c], [L.B_hc], scale=lgb)
                    act(S, L.kd[:, 2:3], cf[:, C_CC:C_CC + 1], AF.Exp, [B_c], [L.B_hc], scale=lgf)
                    act(S, L.kd[:, 3:4], cf[:, C_CC:C_CC + 1], AF.Exp, [B_c], [L.B_hc], scale=lgb)
                    yield
                    for direction in (0, 1):
                        SA, B_sa = (L.SfA, L.B_sfa) if direction == 0 else (L.SbA, L.B_sba)
                        first = 0 if direction == 0 else NCH - 1
                        for g8 in range(NCH // 8):
                            for j in range(8):
                                i = g8 * 8 + j
                                tr(S, PSb[L.pt][:, j * 128:(j + 1) * 128], L.kT[:, i * 128:(i + 1) * 128], ident_b, [L.B_k, B_c], [BPS[L.pt]])
                            ts(S, "vector", L.kdall[direction][:, g8 * 8:(g8 + 1) * 8, :], PSb[L.pt][:, 0:1024].rearrange("p (c d) -> p c d", d=128),
                               L.kd[:, direction:direction + 1], ALU.mult, [BPS[L.pt], L.B_hc], [L.B_kdall[direction]])
                            yield
                        memset(S, "vector", L.Sf2[0][:], 0.0, [L.B_sf2[0]])
                        memset(S, "gpsimd", SA[:, first, :], 0.0, [B_sa])
                        order = range(0, NCH - 1) if direction == 0 else range(NCH - 1, 0, -1)
                        for s_, i in enumerate(order):
                            mm(S, PS[L.pk][:, 0:128], [(L.kdall[direction][:, i, :], L.v[:, i, :])], [L.B_kdall[direction], L.B_v], [BPS[L.pk]])
                            stt(S, L.Sf2[(s_ + 1) % 2][:], L.Sf2[s_ % 2][:], L.kd[:, 2 + direction:3 + direction], PS[L.pk][:, 0:128], ALU.mult, ALU.add,
                                [BPS[L.pk], L.B_hc, L.B_sf2[s_ % 2]], [L.B_sf2[(s_ + 1) % 2]])
                            nxt = i + 1 if direction == 0 else i - 1
                            cp(S, "gpsimd", SA[:, nxt, :], L.Sf2[(s_ + 1) % 2][:], [L.B_sf2[(s_ + 1) % 2]], [B_sa])
                            yield

                    def front(g):
                        gs = slice(g * 512, (g + 1) * 512)
                        q2 = g % 2
                        tt(S, "gpsimd", L.qf[q2][:], L.qT[:, gs], L.qdf[:], ALU.mult, [L.B_q, L.B_hc], [L.B_qf[q2]])
                        tt(S, "gpsimd", L.qb_[q2][:], L.qT[:, gs], L.qdb[:], ALU.mult, [L.B_q, L.B_hc], [L.B_qb[q2]])
                        po = L.po[g % 2]
                        for ci in range(4):
                            i = g * 4 + ci
                            cs = slice(i * 128, (i + 1) * 128)
                            k = i % 2
                            mm(S, PS[L.pt][:, 0:128], [(L.kT[:, cs], L.qT[:, cs])], [L.B_k, L.B_q], [BPS[L.pt]])
                            tt(S, "vector", L.A[k][:], PS[L.pt][:, 0:128], L.DT[:], ALU.mult, [BPS[L.pt], L.B_hc], [L.B_A[k]])
                            mm(S, PS[po][:, ci * 128:(ci + 1) * 128],
                               [(L.v[:, i, :], L.A[k][:]), (L.SfA[:, i, :], L.qf[q2][:, ci * 128:(ci + 1) * 128]), (L.SbA[:, i, :], L.qb_[q2][:, ci * 128:(ci + 1) * 128])],
                               [L.B_v, L.B_A[k], L.B_sfa, L.B_sba, L.B_qf[q2], L.B_qb[q2]], [BPS[po]])
                            yield

                    def post(g):
                        gs = slice(g * 512, (g + 1) * 512)
                        po = L.po[g % 2]
                        cp(S, "scalar", L.o[:], PS[po][:], [BPS[po]], [L.B_o])
                        act(S, L.o2[:], PS[po][:], AF.Square, [BPS[po]], [L.B_o2])
                        yield
                        mm(S, PS[6][:], [(mean_f, L.o[:])], [L.B_o, B_c], [BPS[6]])
                        mm(S, PS[7][:], [(mean_f, L.o2[:])], [L.B_o2, B_c], [BPS[7]])
                        act(S, L.msq[:], PS[6][:], AF.Square, [BPS[6]], [L.B_msq])
                        tt(S, "vector", L.var[:], PS[7][:], L.msq[:], ALU.subtract, [BPS[7], L.B_msq], [L.B_var])
                        tt(S, "vector", L.res[:], L.o[:], PS[6][:], ALU.subtract, [L.B_o, BPS[6]], [L.B_res])
                        yield
                        act(S, L.var[:], L.var[:], AF.Ln, [L.B_var], [L.B_var], scale=1.0, bias=EPS)
                        act(S, L.var[:], L.var[:], AF.Exp, [L.B_var], [L.B_var], scale=-0.5)
                        tt(S, "gpsimd", L.res[:], L.res[:], L.var[:], ALU.mult, [L.B_res, L.B_var], [L.B_res])
                        kk = g % 2
                        tt(S, "gpsimd", L.ob[kk][:], L.res[:], L.gT[:, gs], ALU.mult, [L.B_res, L.B_g], [L.B_ob[kk]])
                        dma(S, "sync", mixT[AH * 128 + h * 128:AH * 128 + (h + 1) * 128, gs], L.ob[kk][:], [L.B_ob[kk]], ())
                        yield

                    yield from front(0)
                    for g in range(NGR):
                        if g + 1 < NGR:
                            yield from front(g + 1)
                        yield from post(g)

                for h0 in range(0, BH, 2):
                    gens = [head_gen(h0 + j, slots[j]) for j in range(min(2, BH - h0))]
                    while gens:
                        for g_ in list(gens):
                            try:
                                next(g_)
                            except StopIteration:
                                gens.remove(g_)
                S.barrier()
                S.flush()

        def mixer_gla():
            sc = 128.0 ** -0.5
            with contextlib.ExitStack() as st:
                NG = S_ // 512
                v = sb(st, "c_v", [128, NCH, 256], BF16)
                qF = sb(st, "c_qF", [128, S_], BF16)
                kF = sb(st, "c_kF", [128, S_], BF16)
                qB = sb(st, "c_qB", [128, S_], BF16)
                qB2 = sb(st, "c_qB2", [128, S_], BF16)
                kB = sb(st, "c_kB", [128, S_], BF16)
                eFl = sb(st, "c_eFl", [128, NCH], F32)
                eEt = sb(st, "c_eEt", [128, NCH], F32)
                SfA = sb(st, "c_sf", [128, NCH, 256], BF16)
                SbA = sb(st, "c_sb", [128, NCH, 256], BF16)
                Sr2 = [sb(st, "c_sr%d" % i, [128, 256], F32) for i in range(2)]
                kve = [sb(st, "c_kve%d" % i, [128, 256], F32) for i in range(2)]
                kdall = [sb(st, "c_kdall%d" % i, [128, NCH, 128], BF16) for i in range(2)]
                B_sr2, B_kve, B_kdall = bufs(2), bufs(2), bufs(2)
                qt = [sb(st, "c_qt%d" % i, [128, 512], BF16) for i in range(2)]
                kt = [sb(st, "c_kt%d" % i, [128, 512], BF16) for i in range(2)]
                lt = [sb(st, "c_lt%d" % i, [128, 2, 512], F32) for i in range(2)]
                Lc = sb(st, "c_Lc", [128, 2, 512], F32)
                X1 = sb(st, "c_X1", [128, 512], F32)
                X2 = sb(st, "c_X2", [128, 512], F32)
                X3 = sb(st, "c_X3", [128, 512], F32)
                kdec = [sb(st, "c_kdec%d" % i, [128, 128], BF16) for i in range(2)]
                A2 = [sb(st, "c_A%d" % i, [128, 256], BF16) for i in range(2)]
                rt = sb(st, "c_rt", [128, 2, 512], BF16)
                o = sb(st, "c_o", [128, 2, 512], F32)
                o2 = sb(st, "c_o2", [128, 512], F32)
                rs = sb(st, "c_rs", [128, 512], F32)
                ob = [sb(st, "c_ob%d" % i, [128, 512], BF16) for i in range(2)]
                gn = sb(st, "c_gn", [128, 2], F32)
                B_v, B_qF, B_kF, B_qB, B_qB2, B_kB, B_eFl, B_eEt = (Buf() for _ in range(8))
                B_sfa, B_sba, B_sr, B_st, B_L, B_X1, B_X2, B_X3, B_rt, B_o, B_o2, B_rs, B_gn = (Buf() for _ in range(13))
                B_qt, B_kt, B_lt, B_kdec, B_A2, B_ob = bufs(2), bufs(2), bufs(2), bufs(2), bufs(2), bufs(2)
                PSb = [PS[i][:].bitcast(BF16) for i in range(8)]
                MG = cb[:, C_MG:C_MG + 256]
                dma(S, "sync", gn[:], gla_g[0].rearrange("(c p) -> p c", p=128), (), [B_gn], slow=True)
                it = 0
                for hc in range(CH):
                    hs = slice(hc * 128, (hc + 1) * 128)
                    dma(S, "sync", v[:], vc[:, hc * 256:(hc + 1) * 256].rearrange("(b p) e -> p b e", p=128), (), [B_v])
                    for g in range(NG):
                        gs = slice(g * 512, (g + 1) * 512)
                        k = g % 2
                        dma(S, "sync", qt[k][:], qcT[hs, gs], (), [B_qt[k]])
                        dma(S, "sync", kt[k][:], kcT[hs, gs], (), [B_kt[k]])
                        dma(S, "sync", lt[k][:, 0, :], lfT[hs, gs], (), [B_lt[k]])
                        dma(S, "sync", lt[k][:, 1, :], lbT[hs, gs], (), [B_lt[k]])
                        for d_ in range(2):
                            for ci in range(4):
                                cs = slice(ci * 128, (ci + 1) * 128)
                                S.op("vector", (lambda e, d_=d_, cs=cs, k=k: e.tensor_tensor_scan(
                                    out=Lc[:, d_, cs], data0=cf[:, C_ONE:C_ONE + 128], data1=lt[k][:, d_, cs], initial=0.0,
                                    op0=ALU.mult, op1=ALU.add)), [B_lt[k], B_c], [B_L])
                        act(S, X1[:], Lc[:, 0, :], AF.Exp, [B_L], [B_X1], scale=-1.0 / 16)
                        cp(S, "gpsimd", eFl[:, g * 4:(g + 1) * 4], X1[:, 127::128], [B_X1], [B_eFl])
                        stt(S, qF[:, gs], qt[k][:], sc, X1[:], ALU.mult, ALU.mult, [B_qt[k], B_X1], [B_qF])
                        act(S, X2[:], Lc[:, 0, :], AF.Exp, [B_L], [B_X2], scale=1.0 / 16)
                        tt(S, "gpsimd", kF[:, gs], kt[k][:], X2[:], ALU.mult, [B_kt[k], B_X2], [B_kF])
                        act(S, eEt[:, g * 4:(g + 1) * 4], Lc[:, 1, 127::128], AF.Exp, [B_L], [B_eEt], scale=-1.0 / 16)
                        tt(S, "gpsimd", X3[:], Lc[:, 1, :], lt[k][:, 1, :], ALU.subtract, [B_L, B_lt[k]], [B_X3])
                        act(S, X1[:], X3[:], AF.Exp, [B_X3], [B_X1], scale=-1.0 / 16)
                        tt(S, "gpsimd", kB[:, gs], kt[k][:], X1[:], ALU.mult, [B_kt[k], B_X1], [B_kB])
                        act(S, X2[:], X3[:], AF.Exp, [B_X3], [B_X2], scale=1.0 / 16)
                        stt(S, qB[:, gs], qt[k][:], sc, X2[:], ALU.mult, ALU.mult, [B_qt[k], B_X2], [B_qB])
                        for ci in range(4):
                            cs = slice(ci * 128, (ci + 1) * 128)
                            ts(S, "vector", X3[:, cs], X3[:, cs], -1.0, ALU.mult, [B_X3, B_L], [B_X3],
                               s2=Lc[:, 1, ci * 128 + 127:ci * 128 + 128], op1=ALU.add)
                        act(S, X1[:], X3[:], AF.Exp, [B_X3], [B_X1], scale=-1.0 / 16)
                        stt(S, qB2[:, gs], qt[k][:], sc, X1[:], ALU.mult, ALU.mult, [B_qt[k], B_X1], [B_qB2])
                    for direction in (0, 1):
                        SA, B_sa = (SfA, B_sfa) if direction == 0 else (SbA, B_sba)
                        KX, B_kx = (kF, B_kF) if direction == 0 else (kB, B_kB)
                        first = 0 if direction == 0 else NCH - 1
                        for g8 in range(NCH // 8):
                            k = it % 2
                            it += 1
                            for j in range(8):
                                i = g8 * 8 + j
                                tr(S, PSb[k][:, j * 128:(j + 1) * 128], KX[:, i * 128:(i + 1) * 128], ident_b, [B_kx, B_c], [BPS[k]])
                            cp(S, "scalar", kdall[direction][:, g8 * 8:(g8 + 1) * 8, :], PSb[k][:, 0:1024].rearrange("p (c d) -> p c d", d=128),
                               [BPS[k]], [B_kdall[direction]])
                        memset(S, "vector", Sr2[0][:], 0.0, [B_sr2[0]])
                        memset(S, "gpsimd", SA[:, first, :], 0.0, [B_sa])
                        order = range(0, NCH - 1) if direction == 0 else range(NCH - 1, 0, -1)
                        for s_, i in enumerate(order):
                            k = it % 2
                            it += 1
                            a_, b_ = s_ % 2, (s_ + 1) % 2
                            mm(S, PS[2 + k][:, 0:256], [(kdall[direction][:, i, :], v[:, i, :])], [B_kdall[direction], B_v], [BPS[2 + k]])
                            if direction == 0:
                                act(S, kve[k][:], PS[2 + k][:, 0:256], AF.Copy, [BPS[2 + k], B_eFl], [B_kve[k]], scale=eFl[:, i:i + 1])
                                stt(S, Sr2[b_][:], Sr2[a_][:], eFl[:, i:i + 1], kve[k][:], ALU.mult, ALU.add, [B_sr2[a_], B_eFl, B_kve[k]], [B_sr2[b_]])
                                nxt = i + 1
                            else:
                                stt(S, Sr2[b_][:], Sr2[a_][:], eEt[:, i:i + 1], PS[2 + k][:, 0:256], ALU.mult, ALU.add,
                                    [B_sr2[a_], BPS[2 + k], B_eEt], [B_sr2[b_]])
                                nxt = i - 1
                            cp(S, "gpsimd", SA[:, nxt, :], Sr2[b_][:], [B_sr2[b_]], [B_sa])
                    def front(g):
                        nonlocal it
                        for ci in range(4):
                            i = g * 4 + ci
                            cs = slice(i * 128, (i + 1) * 128)
                            k = it % 2
                            it += 1
                            mm(S, PS[k][:, 0:128], [(kF[:, cs], qF[:, cs])], [B_kF, B_qF], [BPS[k]])
                            mm(S, PS[k][:, 128:256], [(kB[:, cs], qB[:, cs])], [B_kB, B_qB], [BPS[k]])
                            tt(S, "vector", A2[k][:], PS[k][:, 0:256], MG, ALU.mult, [BPS[k], B_c], [B_A2[k]])
                            for ec in range(2):
                                es = slice(ec * 128, (ec + 1) * 128)
                                pb = 2 + 2 * (g % 2) + ec
                                mm(S, PS[pb][:, ci * 128:(ci + 1) * 128],
                                   [(v[:, i, es], A2[k][:, 0:128]), (v[:, i, es], A2[k][:, 128:256]),
                                    (SfA[:, i, es], qF[:, cs]), (SbA[:, i, es], qB2[:, cs])],
                                   [B_v, B_A2[k], B_sfa, B_sba, B_qF, B_qB2], [BPS[pb]])

                    def post(g):
                        gs = slice(g * 512, (g + 1) * 512)
                        dma(S, "sync", rt[:], rcT[hc * 256:(hc + 1) * 256, gs].rearrange("(c p) t -> p c t", p=128), (), [B_rt])
                        for ec in range(2):
                            pb = 2 + 2 * (g % 2) + ec
                            cp(S, "scalar", o[:, ec, :], PS[pb][:], [BPS[pb]], [B_o])
                            act(S, o2[:], PS[pb][:], AF.Square, [BPS[pb]], [B_o2])
                            mm(S, PS[6][:], [(ones_f, o2[:])], [B_o2, B_c], [BPS[6]], start=(ec == 0), stop=(ec == 1))
                        act(S, rs[:], PS[6][:], AF.Ln, [BPS[6]], [B_rs], scale=1.0 / 256, bias=EPS)
                        act(S, rs[:], rs[:], AF.Exp, [B_rs], [B_rs], scale=-0.5)
                        for ec in range(2):
                            stt(S, o[:, ec, :], o[:, ec, :], gn[:, ec:ec + 1], rs[:], ALU.mult, ALU.mult, [B_rs, B_gn], [B_o])
                            kk = (g * 2 + ec) % 2
                            tt(S, "gpsimd", ob[kk][:], o[:, ec, :], rt[:, ec, :], ALU.mult, [B_o, B_rt], [B_ob[kk]])
                            dma(S, "sync", mixT[hc * 256 + ec * 128:hc * 256 + (ec + 1) * 128, gs], ob[kk][:], [B_ob[kk]], ())

                    front(0)
                    for g in range(NG):
                        if g + 1 < NG:
                            front(g + 1)
                        post(g)
                S.barrier()
                S.flush()

        def mixer_mla():
            scale = 192.0 ** -0.5
            with contextlib.ExitStack() as st:
                NG = S_ // 512
                kr = sb(st, "d_kr", [64, S_], BF16)
                qn2 = [sb(st, "d_qn%d" % j, [128, S_], BF16) for j in range(2)]
                kn2 = [sb(st, "d_kn%d" % j, [128, S_], BF16) for j in range(2)]
                qr2 = [sb(st, "d_qr%d" % j, [64, S_], BF16) for j in range(2)]
                v2 = [sb(st, "d_v%d" % j, [128, NCH, 128], BF16) for j in range(2)]
                B_qn2, B_kn2, B_qr2, B_v2 = bufs(2), bufs(2), bufs(2), bufs(2)
                NE = 6
                E = [sb(st, "d_E%d" % i, [128, 512], BF16) for i in range(NE)]
                rd = sb(st, "d_rd", [128, 512], F32)
                ob = [sb(st, "d_ob%d" % i, [128, 512], BF16) for i in range(2)]
                B_kr, B_rd = Buf(), Buf()
                B_E, B_ob = bufs(NE), bufs(2)
                dma(S, "sync", kr[:], krT[:, :], (), [B_kr])
                it = 0
                def load_head(h):
                    j = h % 2
                    hs = slice(h * 128, (h + 1) * 128)
                    dma(S, "sync", qn2[j][:], qnT[hs, :], (), [B_qn2[j]])
                    dma(S, "sync", kn2[j][:], knT[hs, :], (), [B_kn2[j]])
                    dma(S, "sync", qr2[j][:], qrT[h * 64:(h + 1) * 64, :], (), [B_qr2[j]])
                    dma(S, "sync", v2[j][:], vm[:, hs].rearrange("(b p) e -> p b e", p=128), (), [B_v2[j]])

                load_head(0)
                for h in range(DH):
                    hs = slice(h * 128, (h + 1) * 128)
                    qn, kn, qr, v = qn2[h % 2], kn2[h % 2], qr2[h % 2], v2[h % 2]
                    B_qn, B_kn, B_qr, B_v = B_qn2[h % 2], B_kn2[h % 2], B_qr2[h % 2], B_v2[h % 2]
                    if h + 1 < DH:
                        load_head(h + 1)
                    blocks = [(g, kb_) for g in range(NG) for kb_ in range(NCH)]

                    def issue_scores(bi_):
                        g, kb_ = blocks[bi_]
                        gs = slice(g * 512, (g + 1) * 512)
                        ks = slice(kb_ * 128, (kb_ + 1) * 128)
                        sbk = (it0 + bi_) % 4
                        mm(S, PS[sbk][:], [(kn[:, ks], qn[:, gs]), (kr[:, ks], qr[:, gs])], [B_kn, B_qn, B_kr, B_qr], [BPS[sbk]])

                    it0 = it
                    issue_scores(0)
                    issue_scores(1)
                    for bi_, (g, kb_) in enumerate(blocks):
                        gs = slice(g * 512, (g + 1) * 512)
                        po, pd = 4 + g % 2, 6 + g % 2
                        sbk = (it0 + bi_) % 4
                        k = (it0 + bi_) % NE
                        if bi_ + 2 < len(blocks):
                            issue_scores(bi_ + 2)
                        act(S, E[k][:], PS[sbk][:], AF.Exp, [BPS[sbk]], [B_E[k]], scale=scale)
                        mm(S, PS[po][:], [(v[:, kb_, :], E[k][:])], [B_v, B_E[k]], [BPS[po]], start=(kb_ == 0), stop=(kb_ == NCH - 1))
                        mm(S, PS[pd][:], [(ones_b, E[k][:])], [B_c, B_E[k]], [BPS[pd]], start=(kb_ == 0), stop=(kb_ == NCH - 1))
                        if kb_ == NCH - 1:
                            S.op("vector", (lambda e, pd=pd: e.reciprocal(out=rd[:], in_=PS[pd][:])), [BPS[pd]], [B_rd])
                            kk = g % 2
                            tt(S, "vector", ob[kk][:], PS[po][:], rd[:], ALU.mult, [BPS[po], B_rd], [B_ob[kk]])
                            dma(S, "sync", mixT[CH * 256 + h * 128:CH * 256 + (h + 1) * 128, gs], ob[kk][:], [B_ob[kk]], ())
                    it += len(blocks)
                S.barrier()
                S.flush()

        S.barrier()
        S.flush()
        stages = [lambda: token_phase(0), mixer_even, mixer_ret, lambda: token_phase(1), mixer_gla, mixer_mla, lambda: token_phase(2)]
        for f in stages[:cfg.get("stages", 7)]:
            f()
        S.finish()
    return nc


def _consts(S_):
    c = np.zeros((128, NCOLS), np.float32)
    p = np.arange(128)[:, None].astype(np.float32)
    f = np.arange(128)[None, :].astype(np.float32)
    c[:, C_ID:C_ID + 128] = np.eye(128, dtype=np.float32)
    c[:, C_ONE:C_ONE + 128] = 1.0
    c[:, C_MEAN:C_MEAN + 128] = 1.0 / 128
    c[:, C_POS:C_POS + 128] = np.maximum(f - p, 0)
    c[:, C_NEG:C_NEG + 128] = np.maximum(p - f, 0)
    a4 = np.tile(np.arange(128, dtype=np.float32), 4)[None, :]
    c[:, C_IDX:C_IDX + 512] = a4 + 1.0
    c[:, C_CMA:C_CMA + 512] = 128.0 - a4
    c[:, C_COLF] = 127.0 - p[:, 0]
    c[:, C_COLB] = p[:, 0]
    c[:, C_CC] = 128.0

    def perm(n, half, rot):
        m = np.zeros((128, 128), np.float32)
        for i in range(half):
            m[i, i + half] = 1.0
            m[i + half, i] = 1.0
        return m
    c[:, C_PA:C_PA + 128] = perm(128, 16, 32)
    c[:, C_PB:C_PB + 128] = perm(128, 64, 128)
    c[:, C_PM:C_PM + 128] = perm(128, 32, 64)
    f2 = np.arange(256)[None, :].astype(np.float32)
    c[:, C_MB:C_MB + 256] = ((f2 - p >= 0) & (f2 - p <= 128)).astype(np.float32)
    c[:, C_MG:C_MG + 128] = (p <= f).astype(np.float32)
    c[:, C_MG + 128:C_MG + 256] = (p > f).astype(np.float32)

    def rope_tab(theta, rot, rows):
        half = rot // 2
        inv = np.power(np.float32(theta), -np.arange(half, dtype=np.float32) * np.float32(2.0 / rot)).astype(np.float32)
        pos = np.arange(S_, dtype=np.float32)
        ang = (pos[:, None] * inv[None, :]).astype(np.float32)
        cs, sn = np.cos(ang).astype(np.float32), np.sin(ang).astype(np.float32)
        C = np.ones((rows, S_), np.float32)
        Sn = np.zeros((rows, S_), np.float32)
        C[0:half] = cs.T
        C[half:rot] = cs.T
        Sn[0:half] = -sn.T
        Sn[half:rot] = sn.T
        return np.ascontiguousarray(np.stack([C, Sn]))
    return c, rope_tab(500000.0, 32, 128), rope_tab(10000.0, 128, 128), rope_tab(10000.0, 64, 64)


_NC_CACHE = {}


def run_cfg(cfg, seqs, weights, n_cores):
    key = tuple(sorted(cfg.items()))
    if key not in _NC_CACHE:
        _NC_CACHE[key] = build(cfg)
    nc = _NC_CACHE[key]
    cst, rA, rB, rM = _consts(cfg["S"])
    base = {k: np.ascontiguousarray(np.asarray(v, dtype=np.float32)) for k, v in weights.items()}
    base.update(cst=cst, ropeA=rA, ropeB=rB, ropeM=rM)
    in_maps = []
    for i in range(n_cores):
        m = dict(base)
        m["x"] = np.ascontiguousarray(seqs[i % len(seqs)])
        in_maps.append(m)
    res = run_bass_kernel_spmd(nc, in_maps, core_ids=list(range(n_cores)))
    return [res.results[i]["y"] for i in range(len(seqs))]


def kernel(x_prompt, x_sample, **weights):
    xp = np.asarray(x_prompt, dtype=np.float32)
    xs = np.asarray(x_sample, dtype=np.float32)
    seqs = [xp[i] for i in range(xp.shape[0])] + [xs[i] for i in range(xs.shape[0])]
    outs = run_cfg(FULL_CFG, seqs, weights, 8)
    yp = np.stack(outs[:xp.shape[0]]).astype(np.float32)
    ys = np.stack(outs[xp.shape[0]:]).astype(np.float32)
    return (yp, ys)
```

```python
import contextlib
import math
import numpy as np
import concourse.bass as bass
import concourse.mybir as mybir
from concourse.bass_utils import run_bass_kernel_spmd

F32 = mybir.dt.float32
BF16 = mybir.dt.bfloat16
AF = mybir.ActivationFunctionType
ALU = mybir.AluOpType

SAME_ENGINE_SYNC = True
N_DMA_SEMS = 8
EPS = 1e-6

FULL_CFG = dict(S=4096, D=2048, FF=5632, AH=8, BH=8, CH=4, DH=8, QR=448, KVR=160, T=512)

C_ID, C_ONE, C_MEAN, C_POS, C_NEG, C_IDX, C_CMA = 0, 128, 256, 384, 512, 640, 1152
C_COLF, C_COLB, C_CC = 1664, 1665, 1666
C_PA, C_PB, C_PM, C_MB, C_MG = 1668, 1796, 1924, 2052, 2308
NCOLS = 2564


class Buf:
    __slots__ = ("w", "r", "excl")

    def __init__(self, excl=False):
        self.w = None
        self.r = []
        self.excl = excl


def bufs(n):
    return [Buf() for _ in range(n)]


class Sched:
    ENGS = ("tensor", "vector", "scalar", "gpsimd", "sync")
    QUEUES = ("sync", "gpsimd")

    def __init__(self, nc, stack):
        self.nc = nc
        self.lists = {e: [] for e in self.ENGS}
        self.sem = {e: stack.enter_context(nc.semaphore("s_" + e)) for e in self.ENGS}
        self.count = {e: 0 for e in self.ENGS}
        self.waited = {e: {} for e in self.ENGS}
        self.dsem, self.dval, self.dnext = {}, {}, {}
        for q in self.QUEUES:
            self.dsem[q] = [stack.enter_context(nc.semaphore("d_%s%d" % (q, i))) for i in range(N_DMA_SEMS)]
            self.dval[q] = [0] * N_DMA_SEMS
            self.dnext[q] = 0
        self.n_ops = 0

    def _wait(self, eng, tok):
        key, val, teng, sem = tok
        if teng == eng and (eng == "tensor" or not SAME_ENGINE_SYNC):
            return
        if self.waited[eng].get(key, 0) >= val:
            return
        self.waited[eng][key] = val
        self.lists[eng].append(("wait", sem, val))

    def _deps(self, eng, reads, writes):
        for b in reads:
            if b.w is not None:
                self._wait(eng, b.w)
            if b.excl:
                for t in b.r:
                    if t[2] != eng:
                        self._wait(eng, t)
        for b in writes:
            if b.w is not None:
                self._wait(eng, b.w)
            for t in b.r:
                self._wait(eng, t)

    @staticmethod
    def _commit(tok, reads, writes):
        for b in reads:
            b.r.append(tok)
        for b in writes:
            b.w = tok
            b.r = []

    def op(self, eng, fn, reads=(), writes=()):
        self._deps(eng, reads, writes)
        self.count[eng] += 1
        tok = ("e_" + eng, self.count[eng], eng, self.sem[eng])
        self.lists[eng].append(("inst", fn, self.sem[eng], 1))
        self._commit(tok, reads, writes)
        self.n_ops += 1
        return tok

    def dma(self, q, fn, reads=(), writes=()):
        i = self.dnext[q]
        self.dnext[q] = (i + 1) % N_DMA_SEMS
        sem = self.dsem[q][i]
        key = "d_%s%d" % (q, i)
        if self.dval[q][i] > 0:
            self._wait(q, (key, self.dval[q][i], "dma", sem))
        self._deps(q, reads, writes)
        self.dval[q][i] += 16
        tok = (key, self.dval[q][i], "dma", sem)
        self.lists[q].append(("inst", fn, sem, 16))
        self._commit(tok, reads, writes)
        self.n_ops += 1
        return tok

    def barrier(self):
        for e in self.ENGS:
            for x in self.ENGS:
                if x != e and self.count[x] > 0:
                    self._wait(e, ("e_" + x, self.count[x], x, self.sem[x]))
            for q in self.QUEUES:
                for i in range(N_DMA_SEMS):
                    if self.dval[q][i] > 0:
                        self._wait(e, ("d_%s%d" % (q, i), self.dval[q][i], "dma", self.dsem[q][i]))

    def flush(self):
        lists = self.lists
        self.lists = {e: [] for e in self.ENGS}

        def run(e, name):
            for ent in lists[name]:
                if ent[0] == "wait":
                    e.wait_ge(ent[1], ent[2])
                else:
                    ent[1](e).then_inc(ent[2], ent[3])

        with self.nc.Block() as block:
            @block.tensor
            def _(e):
                run(e, "tensor")

            @block.vector
            def _(e):
                run(e, "vector")

            @block.scalar
            def _(e):
                run(e, "scalar")

            @block.gpsimd
            def _(e):
                run(e, "gpsimd")

            @block.sync
            def _(e):
                run(e, "sync")

    def finish(self):
        self.barrier()
        self.flush()


def act(S, out, in_, func, r, w, scale=None, bias=None):
    kw = {}
    if scale is not None:
        kw["scale"] = scale
    if bias is not None:
        kw["bias"] = bias
    return S.op("scalar", lambda e: e.activation(out=out, in_=in_, func=func, **kw), r, w)


def tt(S, eng, out, a, b, op, r, w):
    return S.op(eng, lambda e: e.tensor_tensor(out=out, in0=a, in1=b, op=op), r, w)


def ts(S, eng, out, a, s1, op0, r, w, s2=None, op1=None):
    if op1 is None:
        return S.op(eng, lambda e: e.tensor_scalar(out=out, in0=a, scalar1=s1, scalar2=None, op0=op0), r, w)
    return S.op(eng, lambda e: e.tensor_scalar(out=out, in0=a, scalar1=s1, scalar2=s2, op0=op0, op1=op1), r, w)


def stt(S, out, a, scalar, b, op0, op1, r, w):
    return S.op("vector", lambda e: e.scalar_tensor_tensor(out=out, in0=a, scalar=scalar, in1=b, op0=op0, op1=op1), r, w)


def cp(S, eng, out, in_, r, w):
    if eng == "scalar":
        return S.op("scalar", lambda e: e.activation(out=out, in_=in_, func=AF.Copy), r, w)
    return S.op(eng, lambda e: e.tensor_copy(out=out, in_=in_), r, w)


def mm(S, out, pairs, r, w, start=True, stop=True):
    pairs = list(pairs)

    def fn(e):
        n = len(pairs)
        inst = None
        for i, (l, rr) in enumerate(pairs):
            inst = e.matmul(out, l, rr, start=(start and i == 0), stop=(stop and i == n - 1))
        return inst

    return S.op("tensor", fn, r, w)


def tr(S, out, in_, ident, r, w):
    return S.op("tensor", lambda e: e.transpose(out, in_, ident), r, w)


def dma(S, q, out, in_, r, w, slow=False):
    if slow:
        return S.dma(q, lambda e: e.dma_start(out=out, in_=in_, allow_slow_non_contiguous=True), r, w)
    return S.dma(q, lambda e: e.dma_start(out=out, in_=in_), r, w)


def memset(S, eng, ap, val, w):
    return S.op(eng, lambda e: e.memset(ap, val), (), w)


def rows_of(n):
    out = []
    o = 0
    while o < n:
        out.append((o, min(128, n - o)))
        o += 128
    return out


def build(cfg):
    S_, D, FF, AH, BH, CH, DH, QR, KVR, T = (cfg[k] for k in ("S", "D", "FF", "AH", "BH", "CH", "DH", "QR", "KVR", "T"))
    DC, FC, NT, NCH = D // 128, FF // 128, S_ // T, S_ // 128
    MIXE, MIXO = (AH + BH) * 128, CH * 256 + DH * 128
    MEC, MOC = MIXE // 128, MIXO // 128
    XC = max(DC, MEC, MOC)
    EVEN_IN = 3 * AH * 128 + 4 * BH * 128
    ODD_IN = 2 * CH * 128 + 2 * CH * 256 + 32 + QR + KVR + 64
    QCH, KVCH = rows_of(QR), rows_of(KVR)
    TB = T // 128
    FH = FC // 2
    assert FC % 4 == 0

    nc = bass.Bass("TRN2", target_bir_lowering=False)

    def din(name, shape):
        return nc.dram_tensor(name, list(shape), F32, kind="ExternalInput").ap()

    def dscr(name, shape, dt):
        return nc.dram_tensor(name, list(shape), dt, kind="Internal").ap()

    x = din("x", [S_, D])
    norm_g = din("norm_g", [2, 3, D])
    final_g = din("final_norm_g", [D])
    wg = din("ffn_w_gate", [2, 2, D, FF])
    wu = din("ffn_w_up", [2, 2, D, FF])
    wd = din("ffn_w_down", [2, 2, FF, D])
    ab_in = din("ab_w_in", [1, D, EVEN_IN])
    ab_out = din("ab_w_out", [1, MIXE, D])
    ret_decay = din("ret_decay", [1, 2, BH])
    cd_in = din("cd_w_in", [1, D, ODD_IN])
    cd_out = din("cd_w_out", [1, MIXO, D])
    gate_w2 = din("gla_gate_w2", [1, 2, 16, CH * 128])
    gate_b = din("gla_gate_b", [1, 2, CH * 128])
    gla_g = din("gla_norm_g", [1, 256])
    q_g = din("mla_q_norm_g", [1, QR])
    w_uq = din("mla_w_uq", [1, QR, DH * 192])
    kv_g = din("mla_kv_norm_g", [1, KVR])
    w_ukv = din("mla_w_ukv", [1, KVR, DH * 256])
    cst = din("cst", [128, NCOLS])
    ropeA = din("ropeA", [2, 128, S_])
    ropeB = din("ropeB", [2, 128, S_])
    ropeM = din("ropeM", [2, 64, S_])
    y = nc.dram_tensor("y", [S_, D], F32, kind="ExternalOutput").ap()

    hT = dscr("hT", [D, S_], F32)
    mixT = dscr("mixT", [XC * 128, S_], BF16)
    qaT = dscr("qaT", [AH * 128, S_], BF16)
    kaT = dscr("kaT", [AH * 128, S_], BF16)
    va = dscr("va", [S_, AH * 128], BF16)
    qbT = dscr("qbT", [BH * 128, S_], BF16)
    kbT = dscr("kbT", [BH * 128, S_], BF16)
    vb = dscr("vb", [S_, BH * 128], BF16)
    gbT = dscr("gbT", [BH * 128, S_], BF16)
    qcT = dscr("qcT", [CH * 128, S_], BF16)
    kcT = dscr("kcT", [CH * 128, S_], BF16)
    vc = dscr("vc", [S_, CH * 256], BF16)
    rcT = dscr("rcT", [CH * 256, S_], BF16)
    lfT = dscr("lfT", [CH * 128, S_], F32)
    lbT = dscr("lbT", [CH * 128, S_], F32)
    qnT = dscr("qnT", [DH * 128, S_], BF16)
    qrT = dscr("qrT", [DH * 64, S_], BF16)
    knT = dscr("knT", [DH * 128, S_], BF16)
    krT = dscr("krT", [64, S_], BF16)
    vm = dscr("vm", [S_, DH * 128], BF16)
    NBA = 4 * (FF // 256) + 48
    WSB = 48
    wscr_l = [dscr("wscr%d" % i, [WSB, 128, XC * 512], BF16) for i in range((NBA + WSB - 1) // WSB)]

    class _W:
        def __getitem__(self, b):
            return wscr_l[b // WSB][b % WSB]
    wscr = _W()
    wdscr = dscr("wdscr", [8 * DC, 128, FH * 128], BF16)
    wkeys, wdkeys = {}, {}

    with contextlib.ExitStack() as gst:
        S = Sched(nc, gst)

        uniq = [0]

        def sb(stack, name, shape, dt):
            uniq[0] += 1
            return stack.enter_context(nc.sbuf_tensor("%s_%d" % (name, uniq[0]), list(shape), dt))

        cf = sb(gst, "cf", [128, NCOLS], F32)
        cb = sb(gst, "cb", [128, NCOLS], BF16)
        ng = sb(gst, "ng", [128, 6, DC], F32)
        fg = sb(gst, "fg", [128, DC], F32)
        lg = sb(gst, "lg", [128, 2 * BH], F32)
        B_c = Buf()
        PS = [gst.enter_context(nc.psum_tensor("ps%d" % i, [128, 512], F32)) for i in range(8)]
        BPS = [Buf(excl=True) for _ in range(8)]

        dma(S, "sync", cf[:], cst[:, :], (), [B_c])
        cp(S, "vector", cb[:], cf[:], [B_c], [B_c])
        gtmp = sb(gst, "gtmp", [128, 128], F32)
        ftmp = sb(gst, "ftmp", [128, 128], F32)
        dma(S, "sync", gtmp[0:6 * DC, :], norm_g.rearrange("a b (c p) -> (a b c) p", p=128), (), [B_c])
        dma(S, "sync", ftmp[0:DC, :], final_g.rearrange("(c p) -> c p", p=128), (), [B_c])
        tr(S, PS[7][:, 0:6 * DC], gtmp[0:6 * DC, :], cf[0:6 * DC, C_ID:C_ID + 6 * DC], [B_c], [BPS[7]])
        cp(S, "vector", ng[:].rearrange("p k c -> p (k c)"), PS[7][:, 0:6 * DC], [BPS[7]], [B_c])
        tr(S, PS[7][:, 0:DC], ftmp[0:DC, :], cf[0:DC, C_ID:C_ID + DC], [B_c], [BPS[7]])
        cp(S, "vector", fg[:], PS[7][:, 0:DC], [BPS[7]], [B_c])
        dma(S, "sync", lg[:], ret_decay.rearrange("a b c -> (a b c)").partition_broadcast(128), (), [B_c])
        act(S, lg[:], lg[:], AF.Exp, [B_c], [B_c])
        act(S, lg[:], lg[:], AF.Copy, [B_c], [B_c], scale=-1.0)

        ident_f = cf[:, C_ID:C_ID + 128]
        ones_f = cf[:, C_ONE:C_ONE + 128]
        mean_f = cf[:, C_MEAN:C_MEAN + 128]
        ident_b = cb[:, C_ID:C_ID + 128]
        ones_b = cb[:, C_ONE:C_ONE + 128]

        def token_phase(pidx):
            with contextlib.ExitStack() as st:
                hT_t = sb(st, "hT_t", [128, DC, T], F32)
                xn = sb(st, "xn", [128, XC, T], BF16)
                aT = sb(st, "aT", [128, FH, T], BF16)
                wA = [sb(st, "wA%d" % i, [128, XC, 512], BF16) for i in range(2)]
                wD = [sb(st, "wD%d" % i, [128, FH, 128], BF16) for i in range(2)]
                sq = [sb(st, "sq%d" % i, [128, T], BF16) for i in range(2)]
                rstd = sb(st, "rstd", [128, T], F32)
                sg = [sb(st, "sg%d" % i, [128, T], F32) for i in range(2)]
                xio = [sb(st, "xio%d" % i, [128, D], F32) for i in range(2)] if pidx != 1 else None
                stg = [sb(st, "stg%d" % i, [128, T], BF16) for i in range(4)]
                stgf = [sb(st, "stgf%d" % i, [128, T], F32) for i in range(2)] if pidx == 1 else None
                t1b = [sb(st, "t1b%d" % i, [128, T], F32) for i in range(2)]
                t2b = [sb(st, "t2b%d" % i, [128, T], F32) for i in range(2)]
                xbf = [sb(st, "xbf%d" % i, [128, T], BF16) for i in range(2)]
                rope_t = sb(st, "rope_t", [128, 4 if pidx == 0 else 2, T], F32)
                B_h, B_xn, B_a = bufs(DC), bufs(XC), bufs(FH)
                B_wA, B_wD, B_sq, B_sg, B_xio = bufs(2), bufs(2), bufs(2), bufs(2), bufs(2)
                B_stg, B_stgf, B_t1, B_t2, B_xbf = bufs(4), bufs(2), bufs(2), bufs(2), bufs(2)
                B_rstd, B_rope = Buf(), Buf()
                cnt = {"wA": 0, "wD": 0, "stg": 0, "stgf": 0, "rp": 0, "xio": 0}
                if pidx == 1:
                    w2 = sb(st, "w2", [16, 2, CH * 128], BF16)
                    nbias = sb(st, "nbias", [128, 2, CH], F32)
                    wuq_s = sb(st, "wuq_s", [128, len(QCH), DH * 192], BF16)
                    wukv_s = sb(st, "wukv_s", [128, len(KVCH), DH * 256], BF16)
                    qg_s = sb(st, "qg_s", [128, len(QCH)], F32)
                    kvg_s = sb(st, "kvg_s", [128, len(KVCH)], F32)
                    lowf = sb(st, "lowf", [16, 2, T], BF16)
                    cq_s = sb(st, "cq_s", [128, len(QCH), T], F32)
                    cqn = sb(st, "cqn", [128, len(QCH), T], BF16)
                    ckv_s = sb(st, "ckv_s", [128, len(KVCH), T], F32)
                    ckvn = sb(st, "ckvn", [128, len(KVCH), T], BF16)
                    B_od = Buf()
                    B_low, B_cq, B_cqn, B_ckv, B_ckvn = Buf(), Buf(), Buf(), Buf(), Buf()
                    rstd2 = sb(st, "rstd2", [128, T], F32)
                    B_rstd2 = Buf()
                    dma(S, "gpsimd", w2[:], gate_w2[0].rearrange("a r n -> r a n"), (), [B_od])
                    dma(S, "sync", nbias[:], gate_b[0].rearrange("a (c p) -> p a c", p=128), (), [B_od], slow=True)
                    ts(S, "vector", nbias[:], nbias[:], -1.0, ALU.mult, [B_od], [B_od])
                    for ci, (o, rws) in enumerate(QCH):
                        dma(S, "gpsimd", wuq_s[0:rws, ci, :], w_uq[0, o:o + rws, :], (), [B_od])
                        dma(S, "sync", qg_s[0:rws, ci:ci + 1], q_g[0, o:o + rws].rearrange("(p a) -> p a", a=1), (), [B_od])
                    for ci, (o, rws) in enumerate(KVCH):
                        dma(S, "gpsimd", wukv_s[0:rws, ci, :], w_ukv[0, o:o + rws, :], (), [B_od])
                        dma(S, "sync", kvg_s[0:rws, ci:ci + 1], kv_g[0, o:o + rws].rearrange("(p a) -> p a", a=1), (), [B_od])

                def load_wA(src2d, kc, ncols, col0, dst_col0=0, key=None):
                    i = cnt["wA_cur"]
                    if key is None:
                        key = (src2d.tensor.name, src2d.offset, col0, ncols, dst_col0)
                    if key not in wkeys:
                        assert len(wkeys) < NBA
                        wkeys[key] = (len(wkeys), Buf(), cnt["tile"])
                    blk, bb, t_created = wkeys[key]
                    sview = wscr[blk].rearrange("p (c f) -> p c f", f=512)[:, 0:kc, dst_col0:dst_col0 + ncols]
                    if t_created == cnt["tile"]:
                        dma(S, "gpsimd", wA[i][:, 0:kc, dst_col0:dst_col0 + ncols],
                            src2d[:, col0:col0 + ncols].rearrange("(c p) f -> p c f", p=128), (), [B_wA[i]])
                        if key[0] != "gu":
                            dma(S, "sync", sview, wA[i][:, 0:kc, dst_col0:dst_col0 + ncols], [B_wA[i]], [bb])
                        elif dst_col0 != 0:
                            dma(S, "sync", wscr[blk].rearrange("p (c f) -> p c f", f=512)[:, 0:kc, :], wA[i][:, 0:kc, :], [B_wA[i]], [bb])
                    elif key[0] == "gu":
                        if dst_col0 == 0:
                            dma(S, "gpsimd", wA[i][:, 0:kc, :], wscr[blk].rearrange("p (c f) -> p c f", f=512)[:, 0:kc, :], [bb], [B_wA[i]])
                    else:
                        dma(S, "gpsimd", wA[i][:, 0:kc, dst_col0:dst_col0 + ncols], sview, [bb], [B_wA[i]])

                def next_wA():
                    cnt["wA"] += 1
                    cnt["wA_cur"] = cnt["wA"] % 2
                    return cnt["wA_cur"]

                def sumsq_chunk(c):
                    i = c % 2
                    act(S, sq[i][:], hT_t[:, c, :], AF.Square, [B_h[c]], [B_sq[i]])
                    mm(S, PS[6][:], [(ones_b, sq[i][:])], [B_sq[i], B_c], [BPS[6]], start=(c == 0), stop=(c == DC - 1))
                    if c == DC - 1:
                        cnt["ss"] = True

                def rms_to_xn(gcol):
                    if not cnt.get("ss"):
                        for c in range(DC):
                            sumsq_chunk(c)
                    cnt["ss"] = False
                    act(S, rstd[:], PS[6][:], AF.Ln, [BPS[6]], [B_rstd], scale=1.0 / D, bias=EPS)
                    act(S, rstd[:], rstd[:], AF.Exp, [B_rstd], [B_rstd], scale=-0.5)

                def norm_apply(gap_fn, out_fn, out_bufs):
                    for c in range(DC):
                        stt(S, out_fn(c), hT_t[:, c, :], gap_fn(c), rstd[:], ALU.mult, ALU.mult,
                            [B_h[c], B_rstd, B_c], [out_bufs[c]])
                    if out_bufs is B_xn:
                        cnt["fresh"] = True

                def mm_xn(out, pair_fn, kc, rd, wbuf):
                    if cnt.get("fresh"):
                        cnt["fresh"] = False
                        for c in range(kc):
                            mm(S, out, [pair_fn(c)], rd + [B_xn[c]], [wbuf], start=(c == 0), stop=(c == kc - 1))
                    else:
                        mm(S, out, [pair_fn(c) for c in range(kc)], rd + B_xn[0:kc], [wbuf])

                def ffn(l, w):
                    rms_to_xn(None)
                    gk = l * 3 + (0 if w == 0 else 2)
                    norm_apply(lambda c: ng[:, gk, c:c + 1], lambda c: xn[:, c, :], B_xn)
                    for half in range(2):
                        f0 = half * FH
                        for j in range(FH // 2):
                            i = next_wA()
                            load_wA(wg[l, w], DC, 256, (f0 + 2 * j) * 128, 0, key=("gu", l, w, half, j))
                            load_wA(wu[l, w], DC, 256, (f0 + 2 * j) * 128, 256, key=("gu", l, w, half, j))
                            for fi in range(2):
                                f = 2 * j + fi
                                pg, pu = f % 2, 2 + f % 2
                                mm_xn(PS[pg][:, 0:T], (lambda c, i=i, fi=fi: (wA[i][:, c, fi * 128:(fi + 1) * 128], xn[:, c, :])), DC,
                                      [B_wA[i]], BPS[pg])
                                mm(S, PS[pu][:, 0:T], [(wA[i][:, c, 256 + fi * 128:256 + (fi + 1) * 128], xn[:, c, :]) for c in range(DC)],
                                   [B_wA[i]] + B_xn[0:DC], [BPS[pu]])
                                act(S, sg[f % 2][:], PS[pg][:, 0:T], AF.Silu, [BPS[pg]], [B_sg[f % 2]])
                                tt(S, "vector", aT[:, f, :], sg[f % 2][:], PS[pu][:, 0:T], ALU.mult, [B_sg[f % 2], BPS[pu]], [B_a[f]])
                        for dcn in range(DC):
                            cnt["wD"] += 1
                            i = cnt["wD"] % 2
                            key = (l, w, half, dcn)
                            if key not in wdkeys:
                                blk = len(wdkeys)
                                wdkeys[key] = (blk, Buf())
                                dma(S, "gpsimd", wD[i][:], wd[l, w][f0 * 128:(f0 + FH) * 128, dcn * 128:(dcn + 1) * 128].rearrange("(f p) d -> p f d", p=128),
                                    (), [B_wD[i]])
                                dma(S, "sync", wdscr[blk].rearrange("p (f d) -> p f d", d=128), wD[i][:], [B_wD[i]], [wdkeys[key][1]])
                            else:
                                blk, bb = wdkeys[key]
                                dma(S, "gpsimd", wD[i][:], wdscr[blk].rearrange("p (f d) -> p f d", d=128), [bb], [B_wD[i]])
                            pb = 4 + dcn % 2
                            mm(S, PS[pb][:, 0:T], [(wD[i][:, f, :], aT[:, f, :]) for f in range(FH)], [B_wD[i]] + B_a, [BPS[pb]])
                            stt(S, hT_t[:, dcn, :], PS[pb][:, 0:T], 0.5, hT_t[:, dcn, :], ALU.mult, ALU.add, [BPS[pb]], [B_h[dcn]])
                            if half == 1:
                                if dcn >= 2:
                                    sumsq_chunk(dcn - 2)
                                if dcn == DC - 1:
                                    for c_ in range(max(DC - 2, 0), DC):
                                        sumsq_chunk(c_)

                def mix_where(mc):
                    if mc <= FH:
                        return aT, B_a
                    return xn, B_xn

                def issue_h_load(t_):
                    sl_ = slice(t_ * T, (t_ + 1) * T)
                    dma(S, "sync", hT_t[:], hT[:, sl_].rearrange("(c p) t -> p c t", p=128), (), B_h)

                def issue_mix_load(t_):
                    mc_ = MEC if pidx == 1 else MOC
                    sl_ = slice(t_ * T, (t_ + 1) * T)
                    mb_, Bm_ = mix_where(mc_)
                    dma(S, "sync", mb_[:, 0:mc_, :], mixT[0:mc_ * 128, sl_].rearrange("(c p) t -> p c t", p=128), (), Bm_[0:mc_])

                def out_proj(w_out2d, mc):
                    mixbuf, B_mix = mix_where(mc)
                    for cb0 in range(0, D, 512):
                        ncol = min(512, D - cb0)
                        i = next_wA()
                        load_wA(w_out2d, mc, ncol, cb0)
                        for dd in range(ncol // 128):
                            dcn = cb0 // 128 + dd
                            pb = 4 + dcn % 2
                            mm(S, PS[pb][:, 0:T], [(wA[i][:, m, dd * 128:(dd + 1) * 128], mixbuf[:, m, :]) for m in range(mc)],
                               [B_wA[i]] + B_mix[0:mc], [BPS[pb]])
                            tt(S, "vector", hT_t[:, dcn, :], hT_t[:, dcn, :], PS[pb][:, 0:T], ALU.add, [BPS[pb]], [B_h[dcn]])
                            if dcn >= 2:
                                sumsq_chunk(dcn - 2)
                            if dcn == DC - 1:
                                for c_ in range(max(DC - 2, 0), DC):
                                    sumsq_chunk(c_)

                def get_stg():
                    cnt["stg"] += 1
                    return cnt["stg"] % 4

                def rope_fm(ps_i, rows, ctab, stab, perm, out_dram):
                    cnt["rp"] += 1
                    k = cnt["rp"] % 2
                    RP = cfg.get("rp", 9)
                    if RP < 1:
                        return
                    cp(S, "scalar", xbf[k][0:rows, :], PS[ps_i][0:rows, 0:T], [BPS[ps_i]], [B_xbf[k]])
                    if RP < 2:
                        return
                    mm(S, PS[7][0:rows, 0:T], [(perm, xbf[k][0:rows, :])], [B_xbf[k], B_c], [BPS[7]])
                    if RP < 3:
                        return
                    tt(S, "vector", t1b[k][0:rows, :], PS[ps_i][0:rows, 0:T], ctab, ALU.mult, [BPS[ps_i], B_rope], [B_t1[k]])
                    tt(S, "vector", t2b[k][0:rows, :], PS[7][0:rows, 0:T], stab, ALU.mult, [BPS[7], B_rope], [B_t2[k]])
                    if RP < 4:
                        return
                    si = get_stg()
                    tt(S, "vector", stg[si][0:rows, :], t1b[k][0:rows, :], t2b[k][0:rows, :], ALU.add, [B_t1[k], B_t2[k]], [B_stg[si]])
                    if RP < 5:
                        return
                    dma(S, "sync", out_dram, stg[si][0:rows, :], [B_stg[si]], ())

                def proj_fm(i, col, ncols_out, kc, ps_i):
                    mm_xn(PS[ps_i][0:ncols_out, 0:T], (lambda c: (wA[i][:, c, col:col + ncols_out], xn[:, c, :])), kc,
                          [B_wA[i]], BPS[ps_i])

                def proj_tm(w2d, col0, ncols, out_dram2d, t0):
                    for cbk in range(0, ncols, 512):
                        nb_ = min(512, ncols - cbk)
                        i = next_wA()
                        load_wA(w2d, DC, nb_, col0 + cbk)
                        for b in range(TB):
                            pb = 4 + b % 2
                            mm_xn(PS[pb][:, 0:nb_], (lambda c, i=i, b=b, nb_=nb_: (xn[:, c, b * 128:(b + 1) * 128], wA[i][:, c, 0:nb_])), DC,
                                  [B_wA[i]], BPS[pb])
                            si = get_stg()
                            cp(S, "scalar", stg[si][:, 0:nb_], PS[pb][:, 0:nb_], [BPS[pb]], [B_stg[si]])
                            dma(S, "sync", out_dram2d[t0 + b * 128:t0 + (b + 1) * 128, cbk:cbk + nb_], stg[si][:, 0:nb_], [B_stg[si]], ())

                def sect_fm(w2d, col0, nfeat, handler):
                    for cbk in range(0, nfeat, 512):
                        nb_ = min(512, nfeat - cbk)
                        i = next_wA()
                        load_wA(w2d, DC, nb_, col0 + cbk)
                        for (o, rws) in rows_of(nb_):
                            ps_i = (cnt["rp"] + o // 128) % 2
                            ps_i = 0 if (o // 128) % 2 == 0 else 1
                            proj_fm(i, o, rws, DC, ps_i)
                            handler(ps_i, cbk + o, rws)

                DBG = cfg.get("dbg", 99)
                for t in range(min(NT, cfg.get("ntiles", NT))):
                    t0 = t * T
                    tsl = slice(t0, t0 + T)
                    cnt["tile"] = (pidx, t)
                    if pidx == 0:
                        for b in range(TB):
                            cnt["xio"] += 1
                            k = cnt["xio"] % 2
                            dma(S, "sync", xio[k][:], x[t0 + b * 128:t0 + (b + 1) * 128, :], (), [B_xio[k]])
                            for c0 in range(0, DC, 4):
                                nn = min(4, DC - c0)
                                for cc in range(nn):
                                    c = c0 + cc
                                    tr(S, PS[7][:, cc * 128:(cc + 1) * 128], xio[k][:, c * 128:(c + 1) * 128], ident_f,
                                       [B_xio[k], B_c], [BPS[7]])
                                cp(S, "vector", hT_t[:, c0:c0 + nn, b * 128:(b + 1) * 128],
                                   PS[7][:, 0:nn * 128].rearrange("p (c t) -> p c t", t=128), [BPS[7]], B_h[c0:c0 + nn])
                    else:
                        mc = MEC if pidx == 1 else MOC
                        if not cnt.get("h_pref"):
                            issue_h_load(t)
                        if not cnt.get("mix_pref"):
                            issue_mix_load(t)
                        cnt["h_pref"] = cnt["mix_pref"] = False
                        out_proj(ab_out[0] if pidx == 1 else cd_out[0], mc)
                    if DBG < 2:
                        continue
                    if pidx == 0:
                        ffn(0, 0)
                    elif pidx == 1:
                        ffn(0, 1)
                        ffn(1, 0)
                    else:
                        ffn(1, 1)
                    if pidx == 2 and t + 1 < min(NT, cfg.get("ntiles", NT)) and mix_where(MOC)[0] is aT:
                        issue_mix_load(t + 1)
                        cnt["mix_pref"] = True
                    if pidx == 2:
                        rms_to_xn(None)
                        norm_apply(lambda c: fg[:, c:c + 1], lambda c: hT_t[:, c, :], B_h)
                        for b in range(TB):
                            cnt["xio"] += 1
                            k = cnt["xio"] % 2
                            for c0 in range(0, DC, 4):
                                nn = min(4, DC - c0)
                                for cc in range(nn):
                                    c = c0 + cc
                                    tr(S, PS[7][:, cc * 128:(cc + 1) * 128], hT_t[:, c, b * 128:(b + 1) * 128], ident_f,
                                       [B_h[c], B_c], [BPS[7]])
                                cp(S, "vector", xio[k][:, c0 * 128:(c0 + nn) * 128], PS[7][:, 0:nn * 128], [BPS[7]], [B_xio[k]])
                            dma(S, "sync", y[t0 + b * 128:t0 + (b + 1) * 128, :], xio[k][:], [B_xio[k]], ())
                        continue
                    if DBG < 3:
                        continue
                    dma(S, "sync", hT[:, tsl].rearrange("(c p) t -> p c t", p=128), hT_t[:], B_h, ())
                    rms_to_xn(None)
                    gk = pidx * 3 + 1
                    norm_apply(lambda c: ng[:, gk, c:c + 1], lambda c: xn[:, c, :], B_xn)
                    if pidx == 1 and t + 1 < min(NT, cfg.get("ntiles", NT)) and mix_where(MEC)[0] is aT:
                        issue_h_load(t + 1)
                        issue_mix_load(t + 1)
                        cnt["h_pref"] = cnt["mix_pref"] = True
                    if DBG < 4:
                        continue
                    if pidx == 0:
                        w2d = ab_in[0]
                        dma(S, "sync", rope_t[:, 0:2, :], ropeA[:, :, tsl].rearrange("a p t -> p a t"), (), [B_rope])
                        dma(S, "sync", rope_t[:, 2:4, :], ropeB[:, :, tsl].rearrange("a p t -> p a t"), (), [B_rope])
                        PA = cb[:, C_PA:C_PA + 128]
                        PB = cb[:, C_PB:C_PB + 128]
                        o_qa, o_ka, o_va = 0, AH * 128, 2 * AH * 128
                        o_qb = 3 * AH * 128
                        o_kb, o_vb, o_gb = o_qb + BH * 128, o_qb + 2 * BH * 128, o_qb + 3 * BH * 128
                        def h_silu(p, fo, r, dst=gbT):
                            si = get_stg()
                            act(S, stg[si][0:r, :], PS[p][0:r, 0:T], AF.Silu, [BPS[p]], [B_stg[si]])
                            dma(S, "sync", dst[fo:fo + r, tsl], stg[si][0:r, :], [B_stg[si]], ())
                        sects = [
                            lambda: sect_fm(w2d, o_qa, AH * 128, lambda p, fo, r: rope_fm(p, r, rope_t[0:r, 0, :], rope_t[0:r, 1, :], PA, qaT[fo:fo + r, tsl])),
                            lambda: sect_fm(w2d, o_ka, AH * 128, lambda p, fo, r: rope_fm(p, r, rope_t[0:r, 0, :], rope_t[0:r, 1, :], PA, kaT[fo:fo + r, tsl])),
                            lambda: proj_tm(w2d, o_va, AH * 128, va, t0),
                            lambda: sect_fm(w2d, o_qb, BH * 128, lambda p, fo, r: rope_fm(p, r, rope_t[0:r, 2, :], rope_t[0:r, 3, :], PB, qbT[fo:fo + r, tsl])),
                            lambda: sect_fm(w2d, o_kb, BH * 128, lambda p, fo, r: rope_fm(p, r, rope_t[0:r, 2, :], rope_t[0:r, 3, :], PB, kbT[fo:fo + r, tsl])),
                            lambda: proj_tm(w2d, o_vb, BH * 128, vb, t0),
                            lambda: sect_fm(w2d, o_gb, BH * 128, h_silu),
                        ]
                        for f_ in sects[:cfg.get("nsect", 7)]:
                            f_()
                    else:
                        w2d = cd_in[0]
                        dma(S, "sync", rope_t[0:64, 0:2, :], ropeM[:, :, tsl].rearrange("a p t -> p a t"), (), [B_rope])
                        PM = cb[0:64, C_PM:C_PM + 64]
                        o_qc, o_kc, o_vc = 0, CH * 128, 2 * CH * 128
                        o_rc = o_vc + CH * 256
                        o_af = o_rc + CH * 256
                        o_ab, o_cq = o_af + 16, o_af + 32
                        o_ckv = o_cq + QR
                        o_kr = o_ckv + KVR

                        def h_plain(dst):
                            def hh(p, fo, r):
                                si = get_stg()
                                cp(S, "scalar", stg[si][0:r, :], PS[p][0:r, 0:T], [BPS[p]], [B_stg[si]])
                                dma(S, "sync", dst[fo:fo + r, tsl], stg[si][0:r, :], [B_stg[si]], ())
                            return hh
                        def h_silu2(p, fo, r):
                            si = get_stg()
                            act(S, stg[si][0:r, :], PS[p][0:r, 0:T], AF.Silu, [BPS[p]], [B_stg[si]])
                            dma(S, "sync", rcT[fo:fo + r, tsl], stg[si][0:r, :], [B_stg[si]], ())

                        blk = []
                        for (cbk, nb_) in ((0, 32 + QR), (32 + QR, KVR + 64)):
                            assert nb_ <= 512
                            i = next_wA()
                            load_wA(w2d, DC, nb_, o_af + cbk)
                            blk.append((cbk, nb_, i))

                        def small_proj(goff, rows, ps_i):
                            for (cbk, nb_, i) in blk:
                                if cbk <= goff and goff + rows <= cbk + nb_:
                                    proj_fm(i, goff - cbk, rows, DC, ps_i)
                                    return
                            raise AssertionError("straddle %d %d" % (goff, rows))

                        for d_ in range(2):
                            small_proj(16 * d_, 16, d_)
                            cp(S, "scalar", lowf[:, d_, :], PS[d_][0:16, 0:T], [BPS[d_]], [B_low])

                        def latent_raw(goff, chs, raw, B_raw, ps_sum):
                            for ci, (o, rws) in enumerate(chs):
                                pr = ci % 2
                                small_proj(goff + o, rws, pr)
                                cp(S, "scalar", raw[0:rws, ci, :], PS[pr][0:rws, 0:T], [BPS[pr]], [B_raw])
                                act(S, sq[ci % 2][0:rws, :], PS[pr][0:rws, 0:T], AF.Square, [BPS[pr]], [B_sq[ci % 2]])
                                mm(S, PS[ps_sum][:, 0:T], [(ones_b[0:rws, :], sq[ci % 2][0:rws, :])], [B_sq[ci % 2], B_c], [BPS[ps_sum]],
                                   start=(ci == 0), stop=(ci == len(chs) - 1))

                        def latent_norm(chs, nfeat, raw, B_raw, nrm, B_nrm, gtile, ps_sum, rs_t, B_rs):
                            act(S, rs_t[:], PS[ps_sum][:, 0:T], AF.Ln, [BPS[ps_sum]], [B_rs], scale=1.0 / nfeat, bias=EPS)
                            act(S, rs_t[:], rs_t[:], AF.Exp, [B_rs], [B_rs], scale=-0.5)
                            for ci, (o, rws) in enumerate(chs):
                                stt(S, nrm[0:rws, ci, :], raw[0:rws, ci, :], gtile[0:rws, ci:ci + 1], rs_t[0:rws, :], ALU.mult, ALU.mult,
                                    [B_raw, B_rs, B_od], [B_nrm])

                        latent_raw(32, QCH, cq_s, B_cq, 2)
                        latent_raw(32 + QR, KVCH, ckv_s, B_ckv, 3)
                        small_proj(32 + QR + KVR, 64, 1)
                        rope_fm(1, 64, rope_t[0:64, 0, :], rope_t[0:64, 1, :], PM, krT[0:64, tsl])
                        latent_norm(QCH, QR, cq_s, B_cq, cqn, B_cqn, qg_s, 2, rstd, B_rstd)
                        latent_norm(KVCH, KVR, ckv_s, B_ckv, ckvn, B_ckvn, kvg_s, 3, rstd2, B_rstd2)

                        sect_fm(w2d, o_qc, CH * 128, h_plain(qcT))
                        sect_fm(w2d, o_kc, CH * 128, h_plain(kcT))
                        proj_tm(w2d, o_vc, CH * 256, vc, t0)
                        sect_fm(w2d, o_rc, CH * 256, h_silu2)

                        for h in range(DH):
                            mm(S, PS[0][:, 0:T], [(wuq_s[0:rws, ci, h * 192:h * 192 + 128], cqn[0:rws, ci, :]) for ci, (o, rws) in enumerate(QCH)],
                               [B_od, B_cqn], [BPS[0]])
                            si = get_stg()
                            cp(S, "scalar", stg[si][:], PS[0][:, 0:T], [BPS[0]], [B_stg[si]])
                            dma(S, "sync", qnT[h * 128:(h + 1) * 128, tsl], stg[si][:], [B_stg[si]], ())
                            mm(S, PS[1][0:64, 0:T], [(wuq_s[0:rws, ci, h * 192 + 128:h * 192 + 192], cqn[0:rws, ci, :]) for ci, (o, rws) in enumerate(QCH)],
                               [B_od, B_cqn], [BPS[1]])
                            rope_fm(1, 64, rope_t[0:64, 0, :], rope_t[0:64, 1, :], PM, qrT[h * 64:(h + 1) * 64, tsl])
                        for h in range(DH):
                            pk = 2 + h % 2
                            mm(S, PS[pk][:, 0:T], [(wukv_s[0:rws, ci, h * 256:h * 256 + 128], ckvn[0:rws, ci, :]) for ci, (o, rws) in enumerate(KVCH)],
                               [B_od, B_ckvn], [BPS[pk]])
                            si = get_stg()
                            cp(S, "scalar", stg[si][:], PS[pk][:, 0:T], [BPS[pk]], [B_stg[si]])
                            dma(S, "sync", knT[h * 128:(h + 1) * 128, tsl], stg[si][:], [B_stg[si]], ())
                        for b in range(TB):
                            for h0 in range(0, DH, 4):
                                nh = min(4, DH - h0)
                                pb = 4 + b % 2
                                for hh in range(nh):
                                    h = h0 + hh
                                    mm(S, PS[pb][:, hh * 128:(hh + 1) * 128],
                                       [(ckvn[0:rws, ci, b * 128:(b + 1) * 128], wukv_s[0:rws, ci, h * 256 + 128:h * 256 + 256]) for ci, (o, rws) in enumerate(KVCH)],
                                       [B_od, B_ckvn], [BPS[pb]])
                                si = get_stg()
                                cp(S, "scalar", stg[si][:, 0:nh * 128], PS[pb][:, 0:nh * 128], [BPS[pb]], [B_stg[si]])
                                dma(S, "sync", vm[t0 + b * 128:t0 + (b + 1) * 128, h0 * 128:(h0 + nh) * 128], stg[si][:, 0:nh * 128], [B_stg[si]], ())
                        for d_ in range(2):
                            dstT = lfT if d_ == 0 else lbT
                            for hc in range(CH):
                                pgt = (d_ * CH + hc) % 2
                                mm(S, PS[pgt][:, 0:T], [(w2[:, d_, hc * 128:(hc + 1) * 128], lowf[:, d_, :])], [B_od, B_low], [BPS[pgt]])
                                k = cnt["stgf"] = cnt["stgf"] + 1
                                k %= 2
                                act(S, stgf[k][:], PS[pgt][:, 0:T], AF.Exp, [BPS[pgt], B_od], [B_stgf[k]], scale=-1.0, bias=nbias[:, d_, hc:hc + 1])
                                act(S, stgf[k][:], stgf[k][:], AF.Ln, [B_stgf[k]], [B_stgf[k]], scale=1.0, bias=1.0)
                                dma(S, "sync", dstT[hc * 128:(hc + 1) * 128, tsl], stgf[k][:], [B_stgf[k]], ())
                S.barrier()
                S.flush()

        def mixer_even():
            scale = 128.0 ** -0.5
            with contextlib.ExitStack() as st:
                NB = S_ // 128
                qT2 = [sb(st, "a_q%d" % j, [128, S_], BF16) for j in range(2)]
                kT2 = [sb(st, "a_k%d" % j, [128, S_], BF16) for j in range(2)]
                vbr2 = [[sb(st, "a_v%d_%d" % (i, j), [128, NB, 128], BF16) for i in range(3)] for j in range(2)]
                B_q2, B_k2, B_v2 = bufs(2), bufs(2), [bufs(3) for _ in range(2)]
                accn = sb(st, "a_n", [128, S_], F32)
                accd = sb(st, "a_d", [128, S_], F32)
                eb = [sb(st, "a_e%d" % i, [128, 256], BF16) for i in range(3)]
                em = [sb(st, "a_em%d" % i, [128, 256], BF16) for i in range(3)]
                obf = sb(st, "a_o", [128, S_], BF16)
                junk = sb(st, "a_junk", [128, 4], F32)
                B_junk = Buf()
                B_n, B_d, B_o = Buf(), Buf(), Buf()
                B_e, B_em = bufs(3), bufs(3)
                MB = cb[:, C_MB:C_MB + 256]
                SB_ = (0, 1, 6)
                it = 0
                def load_head(h):
                    j = h % 2
                    hs = slice(h * 128, (h + 1) * 128)
                    dma(S, "sync", qT2[j][:], qaT[hs, :], (), [B_q2[j]])
                    dma(S, "sync", kT2[j][:], kaT[hs, :], (), [B_k2[j]])
                    for bi, dil in enumerate((1, 4, 16)):
                        L = S_ // dil
                        nb = L // 128
                        for r in range(dil):
                            dma(S, "sync", vbr2[j][bi][:, r * nb:(r + 1) * nb, :],
                                va[r::dil, hs].rearrange("(b p) e -> p b e", p=128), (), [B_v2[j][bi]])

                load_head(0)
                for h in range(AH):
                    hs = slice(h * 128, (h + 1) * 128)
                    qT, kT, vbr = qT2[h % 2], kT2[h % 2], vbr2[h % 2]
                    B_q, B_k, B_v = B_q2[h % 2], B_k2[h % 2], B_v2[h % 2]
                    if h + 1 < AH:
                        load_head(h + 1)
                    B_nr = {(bi, r): Buf() for bi, dil in enumerate((1, 4, 16)) for r in range(dil)}
                    B_dr = {(bi, r): Buf() for bi, dil in enumerate((1, 4, 16)) for r in range(dil)}
                    memset(S, "gpsimd", accn[:], 0.0, [B_n] + [B_nr[(0, 0)]])
                    memset(S, "gpsimd", accd[:], 0.0, [B_d] + [B_dr[(0, 0)]])
                    blocks = []
                    for bi, dil in enumerate((1, 4, 16)):
                        L = S_ // dil
                        nb = L // 128
                        for b in range(nb):
                            for r in range(dil):
                                blocks.append((bi, dil, L, nb, r, b))

                    def geom(blk):
                        bi, dil, L, nb, r, b = blk
                        j0 = 128 * b
                        qlo, qhi = max(0, j0 - 64), min(L, j0 + 192)
                        return j0, qlo, qhi, qhi - qlo, qlo - (j0 - 64)

                    def issue_scores(x):
                        bi, dil, L, nb, r, b = blocks[x]
                        j0, qlo, qhi, n, flo = geom(blocks[x])
                        kc = kT[:, r + dil * j0:r + dil * (j0 + 127) + 1:dil]
                        qc = qT[:, r + dil * qlo:r + dil * (qhi - 1) + 1:dil]
                        sbk = SB_[(it + x) % 3]
                        mm(S, PS[sbk][:, 0:n], [(kc, qc)], [B_k, B_q], [BPS[sbk]])

                    issue_scores(0)
                    issue_scores(1)
                    for x, blk in enumerate(blocks):
                        bi, dil, L, nb, r, b = blk
                        j0, qlo, qhi, n, flo = geom(blk)
                        k3 = (it + x) % 3
                        sbk = SB_[k3]
                        k = (it + x) % 2
                        if x + 2 < len(blocks):
                            issue_scores(x + 2)
                        act(S, eb[k3][:, 0:n], PS[sbk][:, 0:n], AF.Exp, [BPS[sbk]], [B_e[k3]], scale=scale)
                        tt(S, "gpsimd", em[k3][:, 0:n], eb[k3][:, 0:n], MB[:, flo:flo + n], ALU.mult, [B_e[k3], B_c], [B_em[k3]])
                        mm(S, PS[2 + k][:, 0:n], [(vbr[bi][:, r * nb + b, :], em[k3][:, 0:n])], [B_v[bi], B_em[k3]], [BPS[2 + k]])
                        mm(S, PS[4 + k][:, 0:n], [(ones_b, em[k3][:, 0:n])], [B_c, B_em[k3]], [BPS[4 + k]])
                        cs = slice(r + dil * qlo, r + dil * (qhi - 1) + 1, dil)
                        if x > 0 and blocks[x - 1][0] != bi:
                            pbi = blocks[x - 1][0]
                            pdil = (1, 4, 16)[pbi]
                            S.op("vector", lambda e: e.memset(junk[:, 0:1], 0.0),
                                 [B_nr[(pbi, rr)] for rr in range(pdil)] + [B_dr[(pbi, rr)] for rr in range(pdil)],
                                 [B_nr[(bi, rr)] for rr in range(dil)] + [B_dr[(bi, rr)] for rr in range(dil)] + [B_junk])
                        tt(S, "vector", accn[:, cs], accn[:, cs], PS[2 + k][:, 0:n], ALU.add, [BPS[2 + k]], [B_nr[(bi, r)]])
                        tt(S, "vector", accd[:, cs], accd[:, cs], PS[4 + k][:, 0:n], ALU.add, [BPS[4 + k]], [B_dr[(bi, r)]])
                    it += len(blocks)
                    S.op("vector", lambda e: e.memset(junk[:, 0:1], 0.0),
                         [B_nr[(2, rr)] for rr in range(16)] + [B_dr[(2, rr)] for rr in range(16)], [B_n, B_d, B_junk])
                    S.op("vector", lambda e: e.reciprocal(out=accd[:], in_=accd[:]), [B_d], [B_d])
                    tt(S, "gpsimd", obf[:], accn[:], accd[:], ALU.mult, [B_n, B_d], [B_o])
                    dma(S, "sync", mixT[hs, :], obf[:], [B_o], ())
                S.barrier()
                S.flush()

        def mixer_ret():
            lns = math.log(128.0 ** -0.5)
            NGR = S_ // 512
            with contextlib.ExitStack() as st:
                PSb = [PS[i][:].bitcast(BF16) for i in range(8)]

                class Slot:
                    pass
                slots = []
                for si in range(2):
                    L = Slot()
                    n = "b%d_" % si
                    L.qT = sb(st, n + "q", [128, S_], BF16)
                    L.kT = sb(st, n + "k", [128, S_], BF16)
                    L.gT = sb(st, n + "g", [128, S_], BF16)
                    L.v = sb(st, n + "v", [128, NCH, 128], BF16)
                    L.SfA = sb(st, n + "sf", [128, NCH, 128], BF16)
                    L.SbA = sb(st, n + "sb", [128, NCH, 128], BF16)
                    L.Sf2 = [sb(st, n + "sfr%d" % i, [128, 128], F32) for i in range(2)]
                    L.kdall = [sb(st, n + "kdall%d" % i, [128, NCH, 128], BF16) for i in range(2)]
                    L.DT = sb(st, n + "dt", [128, 128], F32)
                    L.DT2 = sb(st, n + "dt2", [128, 128], F32)
                    L.qdf = sb(st, n + "qdf", [128, 512], F32)
                    L.qdb = sb(st, n + "qdb", [128, 512], F32)
                    L.kd = sb(st, n + "kd", [128, 4], F32)
                    L.qf = [sb(st, n + "qf%d" % i, [128, 512], BF16) for i in range(2)]
                    L.qb_ = [sb(st, n + "qb%d" % i, [128, 512], BF16) for i in range(2)]
                    L.A = [sb(st, n + "A%d" % i, [128, 128], BF16) for i in range(2)]
                    L.o = sb(st, n + "o", [128, 512], F32)
                    L.o2 = sb(st, n + "o2", [128, 512], F32)
                    L.msq = sb(st, n + "msq", [128, 512], F32)
                    L.var = sb(st, n + "var", [128, 512], F32)
                    L.res = sb(st, n + "res", [128, 512], F32)
                    L.ob = [sb(st, n + "ob%d" % i, [128, 512], BF16) for i in range(2)]
                    (L.B_q, L.B_k, L.B_g, L.B_v, L.B_sfa, L.B_sba, L.B_hc, L.B_o, L.B_o2, L.B_msq, L.B_var, L.B_res) = (Buf() for _ in range(12))
                    L.B_sf2, L.B_kdall, L.B_A, L.B_ob, L.B_qf, L.B_qb = bufs(2), bufs(2), bufs(2), bufs(2), bufs(2), bufs(2)
                    L.pt, L.po = si, (2 + 2 * si, 3 + 2 * si)
                    L.pk = L.po[0]
                    slots.append(L)

                def head_gen(h, L):
                    hs = slice(h * 128, (h + 1) * 128)
                    lgf, lgb = lg[:, h:h + 1], lg[:, BH + h:BH + h + 1]
                    dma(S, "sync", L.qT[:], qbT[hs, :], (), [L.B_q])
                    dma(S, "sync", L.kT[:], kbT[hs, :], (), [L.B_k])
                    dma(S, "sync", L.gT[:], gbT[hs, :], (), [L.B_g])
                    dma(S, "sync", L.v[:], vb[:, hs].rearrange("(b p) e -> p b e", p=128), (), [L.B_v])
                    yield
                    act(S, L.DT[:], cf[:, C_POS:C_POS + 128], AF.Exp, [B_c], [L.B_hc], scale=lgf, bias=lns)
                    act(S, L.DT2[:], cf[:, C_NEG:C_NEG + 128], AF.Exp, [B_c], [L.B_hc], scale=lgb)
                    tt(S, "vector", L.DT[:], L.DT[:], L.DT2[:], ALU.mult, [L.B_hc], [L.B_hc])
                    act(S, L.qdf[:], cf[:, C_IDX:C_IDX + 512], AF.Exp, [B_c], [L.B_hc], scale=lgf, bias=lns)
                    act(S, L.qdb[:], cf[:, C_CMA:C_CMA + 512], AF.Exp, [B_c], [L.B_hc], scale=lgb, bias=lns)
                    act(S, L.kd[:, 0:1], cf[:, C_COLF:C_COLF + 1], AF.Exp, [B_c], [L.B_hc], scale=lgf)
                    act(S, L.kd[:, 1:2], cf[:, C_COLB:C_COLB + 1], AF.Exp, [B_c], [L.B_hc], scale=lgb)
                    act(S, L.kd[:, 2:3], cf[:, C_CC:C_CC + 1], AF.Exp, [B_c], [L.B_hc], scale=lgf)
                    act(S, L.kd[:, 3:4], cf[:, C_CC:C_CC + 1], AF.Exp, [B_c], [L.B_hc], scale=lgb)
                    yield
                    for direction in (0, 1):
                        SA, B_sa = (L.SfA, L.B_sfa) if direction == 0 else (L.SbA, L.B_sba)
                        first = 0 if direction == 0 else NCH - 1
                        for g8 in range(NCH // 8):
                            for j in range(8):
                                i = g8 * 8 + j
                                tr(S, PSb[L.pt][:, j * 128:(j + 1) * 128], L.kT[:, i * 128:(i + 1) * 128], ident_b, [L.B_k, B_c], [BPS[L.pt]])
                            ts(S, "vector", L.kdall[direction][:, g8 * 8:(g8 + 1) * 8, :], PSb[L.pt][:, 0:1024].rearrange("p (c d) -> p c d", d=128),
                               L.kd[:, direction:direction + 1], ALU.mult, [BPS[L.pt], L.B_hc], [L.B_kdall[direction]])
                            yield
                        memset(S, "vector", L.Sf2[0][:], 0.0, [L.B_sf2[0]])
                        memset(S, "gpsimd", SA[:, first, :], 0.0, [B_sa])
                        order = range(0, NCH - 1) if direction == 0 else range(NCH - 1, 0, -1)
                        for s_, i in enumerate(order):
                            mm(S, PS[L.pk][:, 0:128], [(L.kdall[direction][:, i, :], L.v[:, i, :])], [L.B_kdall[direction], L.B_v], [BPS[L.pk]])
                            stt(S, L.Sf2[(s_ + 1) % 2][:], L.Sf2[s_ % 2][:], L.kd[:, 2 + direction:3 + direction], PS[L.pk][:, 0:128], ALU.mult, ALU.add,
                                [BPS[L.pk], L.B_hc, L.B_sf2[s_ % 2]], [L.B_sf2[(s_ + 1) % 2]])
                            nxt = i + 1 if direction == 0 else i - 1
                            cp(S, "gpsimd", SA[:, nxt, :], L.Sf2[(s_ + 1) % 2][:], [L.B_sf2[(s_ + 1) % 2]], [B_sa])
                            yield

                    def front(g):
                        gs = slice(g * 512, (g + 1) * 512)
                        q2 = g % 2
                        tt(S, "gpsimd", L.qf[q2][:], L.qT[:, gs], L.qdf[:], ALU.mult, [L.B_q, L.B_hc], [L.B_qf[q2]])
                        tt(S, "gpsimd", L.qb_[q2][:], L.qT[:, gs], L.qdb[:], ALU.mult, [L.B_q, L.B_hc], [L.B_qb[q2]])
                        po = L.po[g % 2]
                        for ci in range(4):
                            i = g * 4 + ci
                            cs = slice(i * 128, (i + 1) * 128)
                            k = i % 2
                            mm(S, PS[L.pt][:, 0:128], [(L.kT[:, cs], L.qT[:, cs])], [L.B_k, L.B_q], [BPS[L.pt]])
                            tt(S, "vector", L.A[k][:], PS[L.pt][:, 0:128], L.DT[:], ALU.mult, [BPS[L.pt], L.B_hc], [L.B_A[k]])
                            mm(S, PS[po][:, ci * 128:(ci + 1) * 128],
                               [(L.v[:, i, :], L.A[k][:]), (L.SfA[:, i, :], L.qf[q2][:, ci * 128:(ci + 1) * 128]), (L.SbA[:, i, :], L.qb_[q2][:, ci * 128:(ci + 1) * 128])],
                               [L.B_v, L.B_A[k], L.B_sfa, L.B_sba, L.B_qf[q2], L.B_qb[q2]], [BPS[po]])
                            yield

                    def post(g):
                        gs = slice(g * 512, (g + 1) * 512)
                        po = L.po[g % 2]
                        cp(S, "scalar", L.o[:], PS[po][:], [BPS[po]], [L.B_o])
                        act(S, L.o2[:], PS[po][:], AF.Square, [BPS[po]], [L.B_o2])
                        yield
                        mm(S, PS[6][:], [(mean_f, L.o[:])], [L.B_o, B_c], [BPS[6]])
                        mm(S, PS[7][:], [(mean_f, L.o2[:])], [L.B_o2, B_c], [BPS[7]])
                        act(S, L.msq[:], PS[6][:], AF.Square, [BPS[6]], [L.B_msq])
                        tt(S, "vector", L.var[:], PS[7][:], L.msq[:], ALU.subtract, [BPS[7], L.B_msq], [L.B_var])
                        tt(S, "vector", L.res[:], L.o[:], PS[6][:], ALU.subtract, [L.B_o, BPS[6]], [L.B_res])
                        yield
                        act(S, L.var[:], L.var[:], AF.Ln, [L.B_var], [L.B_var], scale=1.0, bias=EPS)
                        act(S, L.var[:], L.var[:], AF.Exp, [L.B_var], [L.B_var], scale=-0.5)
                        tt(S, "gpsimd", L.res[:], L.res[:], L.var[:], ALU.mult, [L.B_res, L.B_var], [L.B_res])
                        kk = g % 2
                        tt(S, "gpsimd", L.ob[kk][:], L.res[:], L.gT[:, gs], ALU.mult, [L.B_res, L.B_g], [L.B_ob[kk]])
                        dma(S, "sync", mixT[AH * 128 + h * 128:AH * 128 + (h + 1) * 128, gs], L.ob[kk][:], [L.B_ob[kk]], ())
                        yield

                    yield from front(0)
                    for g in range(NGR):
                        if g + 1 < NGR:
                            yield from front(g + 1)
                        yield from post(g)

                for h0 in range(0, BH, 2):
                    gens = [head_gen(h0 + j, slots[j]) for j in range(min(2, BH - h0))]
                    while gens:
                        for g_ in list(gens):
                            try:
                                next(g_)
                            except StopIteration:
                                gens.remove(g_)
                S.barrier()
                S.flush()

        def mixer_gla():
            sc = 128.0 ** -0.5
            with contextlib.ExitStack() as st:
                NG = S_ // 512
                v = sb(st, "c_v", [128, NCH, 256], BF16)
                qF = sb(st, "c_qF", [128, S_], BF16)
                kF = sb(st, "c_kF", [128, S_], BF16)
                qB = sb(st, "c_qB", [128, S_], BF16)
                qB2 = sb(st, "c_qB2", [128, S_], BF16)
                kB = sb(st, "c_kB", [128, S_], BF16)
                eFl = sb(st, "c_eFl", [128, NCH], F32)
                eEt = sb(st, "c_eEt", [128, NCH], F32)
                SfA = sb(st, "c_sf", [128, NCH, 256], BF16)
                SbA = sb(st, "c_sb", [128, NCH, 256], BF16)
                Sr2 = [sb(st, "c_sr%d" % i, [128, 256], F32) for i in range(2)]
                kve = [sb(st, "c_kve%d" % i, [128, 256], F32) for i in range(2)]
                kdall = [sb(st, "c_kdall%d" % i, [128, NCH, 128], BF16) for i in range(2)]
                B_sr2, B_kve, B_kdall = bufs(2), bufs(2), bufs(2)
                qt = [sb(st, "c_qt%d" % i, [128, 512], BF16) for i in range(2)]
                kt = [sb(st, "c_kt%d" % i, [128, 512], BF16) for i in range(2)]
                lt = [sb(st, "c_lt%d" % i, [128, 2, 512], F32) for i in range(2)]
                Lc = sb(st, "c_Lc", [128, 2, 512], F32)
                X1 = sb(st, "c_X1", [128, 512], F32)
                X2 = sb(st, "c_X2", [128, 512], F32)
                X3 = sb(st, "c_X3", [128, 512], F32)
                kdec = [sb(st, "c_kdec%d" % i, [128, 128], BF16) for i in range(2)]
                A2 = [sb(st, "c_A%d" % i, [128, 256], BF16) for i in range(2)]
                rt = sb(st, "c_rt", [128, 2, 512], BF16)
                o = sb(st, "c_o", [128, 2, 512], F32)
                o2 = sb(st, "c_o2", [128, 512], F32)
                rs = sb(st, "c_rs", [128, 512], F32)
                ob = [sb(st, "c_ob%d" % i, [128, 512], BF16) for i in range(2)]
                gn = sb(st, "c_gn", [128, 2], F32)
                B_v, B_qF, B_kF, B_qB, B_qB2, B_kB, B_eFl, B_eEt = (Buf() for _ in range(8))
                B_sfa, B_sba, B_sr, B_st, B_L, B_X1, B_X2, B_X3, B_rt, B_o, B_o2, B_rs, B_gn = (Buf() for _ in range(13))
                B_qt, B_kt, B_lt, B_kdec, B_A2, B_ob = bufs(2), bufs(2), bufs(2), bufs(2), bufs(2), bufs(2)
                PSb = [PS[i][:].bitcast(BF16) for i in range(8)]
                MG = cb[:, C_MG:C_MG + 256]
                dma(S, "sync", gn[:], gla_g[0].rearrange("(c p) -> p c", p=128), (), [B_gn], slow=True)
                it = 0
                for hc in range(CH):
                    hs = slice(hc * 128, (hc + 1) * 128)
                    dma(S, "sync", v[:], vc[:, hc * 256:(hc + 1) * 256].rearrange("(b p) e -> p b e", p=128), (), [B_v])
                    for g in range(NG):
                        gs = slice(g * 512, (g + 1) * 512)
                        k = g % 2
                        dma(S, "sync", qt[k][:], qcT[hs, gs], (), [B_qt[k]])
                        dma(S, "sync", kt[k][:], kcT[hs, gs], (), [B_kt[k]])
                        dma(S, "sync", lt[k][:, 0, :], lfT[hs, gs], (), [B_lt[k]])
                        dma(S, "sync", lt[k][:, 1, :], lbT[hs, gs], (), [B_lt[k]])
                        for d_ in range(2):
                            for ci in range(4):
                                cs = slice(ci * 128, (ci + 1) * 128)
                                S.op("vector", (lambda e, d_=d_, cs=cs, k=k: e.tensor_tensor_scan(
                                    out=Lc[:, d_, cs], data0=cf[:, C_ONE:C_ONE + 128], data1=lt[k][:, d_, cs], initial=0.0,
                                    op0=ALU.mult, op1=ALU.add)), [B_lt[k], B_c], [B_L])
                        act(S, X1[:], Lc[:, 0, :], AF.Exp, [B_L], [B_X1], scale=-1.0 / 16)
                        cp(S, "gpsimd", eFl[:, g * 4:(g + 1) * 4], X1[:, 127::128], [B_X1], [B_eFl])
                        stt(S, qF[:, gs], qt[k][:], sc, X1[:], ALU.mult, ALU.mult, [B_qt[k], B_X1], [B_qF])
                        act(S, X2[:], Lc[:, 0, :], AF.Exp, [B_L], [B_X2], scale=1.0 / 16)
                        tt(S, "gpsimd", kF[:, gs], kt[k][:], X2[:], ALU.mult, [B_kt[k], B_X2], [B_kF])
                        act(S, eEt[:, g * 4:(g + 1) * 4], Lc[:, 1, 127::128], AF.Exp, [B_L], [B_eEt], scale=-1.0 / 16)
                        tt(S, "gpsimd", X3[:], Lc[:, 1, :], lt[k][:, 1, :], ALU.subtract, [B_L, B_lt[k]], [B_X3])
                        act(S, X1[:], X3[:], AF.Exp, [B_X3], [B_X1], scale=-1.0 / 16)
                        tt(S, "gpsimd", kB[:, gs], kt[k][:], X1[:], ALU.mult, [B_kt[k], B_X1], [B_kB])
                        act(S, X2[:], X3[:], AF.Exp, [B_X3], [B_X2], scale=1.0 / 16)
                        stt(S, qB[:, gs], qt[k][:], sc, X2[:], ALU.mult, ALU.mult, [B_qt[k], B_X2], [B_qB])
                        for ci in range(4):
                            cs = slice(ci * 128, (ci + 1) * 128)
                            ts(S, "vector", X3[:, cs], X3[:, cs], -1.0, ALU.mult, [B_X3, B_L], [B_X3],
                               s2=Lc[:, 1, ci * 128 + 127:ci * 128 + 128], op1=ALU.add)
                        act(S, X1[:], X3[:], AF.Exp, [B_X3], [B_X1], scale=-1.0 / 16)
                        stt(S, qB2[:, gs], qt[k][:], sc, X1[:], ALU.mult, ALU.mult, [B_qt[k], B_X1], [B_qB2])
                    for direction in (0, 1):
                        SA, B_sa = (SfA, B_sfa) if direction == 0 else (SbA, B_sba)
                        KX, B_kx = (kF, B_kF) if direction == 0 else (kB, B_kB)
                        first = 0 if direction == 0 else NCH - 1
                        for g8 in range(NCH // 8):
                            k = it % 2
                            it += 1
                            for j in range(8):
                                i = g8 * 8 + j
                                tr(S, PSb[k][:, j * 128:(j + 1) * 128], KX[:, i * 128:(i + 1) * 128], ident_b, [B_kx, B_c], [BPS[k]])
                            cp(S, "scalar", kdall[direction][:, g8 * 8:(g8 + 1) * 8, :], PSb[k][:, 0:1024].rearrange("p (c d) -> p c d", d=128),
                               [BPS[k]], [B_kdall[direction]])
                        memset(S, "vector", Sr2[0][:], 0.0, [B_sr2[0]])
                        memset(S, "gpsimd", SA[:, first, :], 0.0, [B_sa])
                        order = range(0, NCH - 1) if direction == 0 else range(NCH - 1, 0, -1)
                        for s_, i in enumerate(order):
                            k = it % 2
                            it += 1
                            a_, b_ = s_ % 2, (s_ + 1) % 2
                            mm(S, PS[2 + k][:, 0:256], [(kdall[direction][:, i, :], v[:, i, :])], [B_kdall[direction], B_v], [BPS[2 + k]])
                            if direction == 0:
                                act(S, kve[k][:], PS[2 + k][:, 0:256], AF.Copy, [BPS[2 + k], B_eFl], [B_kve[k]], scale=eFl[:, i:i + 1])
                                stt(S, Sr2[b_][:], Sr2[a_][:], eFl[:, i:i + 1], kve[k][:], ALU.mult, ALU.add, [B_sr2[a_], B_eFl, B_kve[k]], [B_sr2[b_]])
                                nxt = i + 1
                            else:
                                stt(S, Sr2[b_][:], Sr2[a_][:], eEt[:, i:i + 1], PS[2 + k][:, 0:256], ALU.mult, ALU.add,
                                    [B_sr2[a_], BPS[2 + k], B_eEt], [B_sr2[b_]])
                                nxt = i - 1
                            cp(S, "gpsimd", SA[:, nxt, :], Sr2[b_][:], [B_sr2[b_]], [B_sa])
                    def front(g):
                        nonlocal it
                        for ci in range(4):
                            i = g * 4 + ci
                            cs = slice(i * 128, (i + 1) * 128)
                            k = it % 2
                            it += 1
                            mm(S, PS[k][:, 0:128], [(kF[:, cs], qF[:, cs])], [B_kF, B_qF], [BPS[k]])
                            mm(S, PS[k][:, 128:256], [(kB[:, cs], qB[:, cs])], [B_kB, B_qB], [BPS[k]])
                            tt(S, "vector", A2[k][:], PS[k][:, 0:256], MG, ALU.mult, [BPS[k], B_c], [B_A2[k]])
                            for ec in range(2):
                                es = slice(ec * 128, (ec + 1) * 128)
                                pb = 2 + 2 * (g % 2) + ec
                                mm(S, PS[pb][:, ci * 128:(ci + 1) * 128],
                                   [(v[:, i, es], A2[k][:, 0:128]), (v[:, i, es], A2[k][:, 128:256]),
                                    (SfA[:, i, es], qF[:, cs]), (SbA[:, i, es], qB2[:, cs])],
                                   [B_v, B_A2[k], B_sfa, B_sba, B_qF, B_qB2], [BPS[pb]])

                    def post(g):
                        gs = slice(g * 512, (g + 1) * 512)
                        dma(S, "sync", rt[:], rcT[hc * 256:(hc + 1) * 256, gs].rearrange("(c p) t -> p c t", p=128), (), [B_rt])
                        for ec in range(2):
                            pb = 2 + 2 * (g % 2) + ec
                            cp(S, "scalar", o[:, ec, :], PS[pb][:], [BPS[pb]], [B_o])
                            act(S, o2[:], PS[pb][:], AF.Square, [BPS[pb]], [B_o2])
                            mm(S, PS[6][:], [(ones_f, o2[:])], [B_o2, B_c], [BPS[6]], start=(ec == 0), stop=(ec == 1))
                        act(S, rs[:], PS[6][:], AF.Ln, [BPS[6]], [B_rs], scale=1.0 / 256, bias=EPS)
                        act(S, rs[:], rs[:], AF.Exp, [B_rs], [B_rs], scale=-0.5)
                        for ec in range(2):
                            stt(S, o[:, ec, :], o[:, ec, :], gn[:, ec:ec + 1], rs[:], ALU.mult, ALU.mult, [B_rs, B_gn], [B_o])
                            kk = (g * 2 + ec) % 2
                            tt(S, "gpsimd", ob[kk][:], o[:, ec, :], rt[:, ec, :], ALU.mult, [B_o, B_rt], [B_ob[kk]])
                            dma(S, "sync", mixT[hc * 256 + ec * 128:hc * 256 + (ec + 1) * 128, gs], ob[kk][:], [B_ob[kk]], ())

                    front(0)
                    for g in range(NG):
                        if g + 1 < NG:
                            front(g + 1)
                        post(g)
                S.barrier()
                S.flush()

        def mixer_mla():
            scale = 192.0 ** -0.5
            with contextlib.ExitStack() as st:
                NG = S_ // 512
                kr = sb(st, "d_kr", [64, S_], BF16)
                qn2 = [sb(st, "d_qn%d" % j, [128, S_], BF16) for j in range(2)]
                kn2 = [sb(st, "d_kn%d" % j, [128, S_], BF16) for j in range(2)]
                qr2 = [sb(st, "d_qr%d" % j, [64, S_], BF16) for j in range(2)]
                v2 = [sb(st, "d_v%d" % j, [128, NCH, 128], BF16) for j in range(2)]
                B_qn2, B_kn2, B_qr2, B_v2 = bufs(2), bufs(2), bufs(2), bufs(2)
                NE = 6
                E = [sb(st, "d_E%d" % i, [128, 512], BF16) for i in range(NE)]
                rd = sb(st, "d_rd", [128, 512], F32)
                ob = [sb(st, "d_ob%d" % i, [128, 512], BF16) for i in range(2)]
                B_kr, B_rd = Buf(), Buf()
                B_E, B_ob = bufs(NE), bufs(2)
                dma(S, "sync", kr[:], krT[:, :], (), [B_kr])
                it = 0
                def load_head(h):
                    j = h % 2
                    hs = slice(h * 128, (h + 1) * 128)
                    dma(S, "sync", qn2[j][:], qnT[hs, :], (), [B_qn2[j]])
                    dma(S, "sync", kn2[j][:], knT[hs, :], (), [B_kn2[j]])
                    dma(S, "sync", qr2[j][:], qrT[h * 64:(h + 1) * 64, :], (), [B_qr2[j]])
                    dma(S, "sync", v2[j][:], vm[:, hs].rearrange("(b p) e -> p b e", p=128), (), [B_v2[j]])

                load_head(0)
                for h in range(DH):
                    hs = slice(h * 128, (h + 1) * 128)
                    qn, kn, qr, v = qn2[h % 2], kn2[h % 2], qr2[h % 2], v2[h % 2]
                    B_qn, B_kn, B_qr, B_v = B_qn2[h % 2], B_kn2[h % 2], B_qr2[h % 2], B_v2[h % 2]
                    if h + 1 < DH:
                        load_head(h + 1)
                    blocks = [(g, kb_) for g in range(NG) for kb_ in range(NCH)]

                    def issue_scores(bi_):
                        g, kb_ = blocks[bi_]
                        gs = slice(g * 512, (g + 1) * 512)
                        ks = slice(kb_ * 128, (kb_ + 1) * 128)
                        sbk = (it0 + bi_) % 4
                        mm(S, PS[sbk][:], [(kn[:, ks], qn[:, gs]), (kr[:, ks], qr[:, gs])], [B_kn, B_qn, B_kr, B_qr], [BPS[sbk]])

                    it0 = it
                    issue_scores(0)
                    issue_scores(1)
                    for bi_, (g, kb_) in enumerate(blocks):
                        gs = slice(g * 512, (g + 1) * 512)
                        po, pd = 4 + g % 2, 6 + g % 2
                        sbk = (it0 + bi_) % 4
                        k = (it0 + bi_) % NE
                        if bi_ + 2 < len(blocks):
                            issue_scores(bi_ + 2)
                        act(S, E[k][:], PS[sbk][:], AF.Exp, [BPS[sbk]], [B_E[k]], scale=scale)
                        mm(S, PS[po][:], [(v[:, kb_, :], E[k][:])], [B_v, B_E[k]], [BPS[po]], start=(kb_ == 0), stop=(kb_ == NCH - 1))
                        mm(S, PS[pd][:], [(ones_b, E[k][:])], [B_c, B_E[k]], [BPS[pd]], start=(kb_ == 0), stop=(kb_ == NCH - 1))
                        if kb_ == NCH - 1:
                            S.op("vector", (lambda e, pd=pd: e.reciprocal(out=rd[:], in_=PS[pd][:])), [BPS[pd]], [B_rd])
                            kk = g % 2
                            tt(S, "vector", ob[kk][:], PS[po][:], rd[:], ALU.mult, [BPS[po], B_rd], [B_ob[kk]])
                            dma(S, "sync", mixT[CH * 256 + h * 128:CH * 256 + (h + 1) * 128, gs], ob[kk][:], [B_ob[kk]], ())
                    it += len(blocks)
                S.barrier()
                S.flush()

        S.barrier()
        S.flush()
        stages = [lambda: token_phase(0), mixer_even, mixer_ret, lambda: token_phase(1), mixer_gla, mixer_mla, lambda: token_phase(2)]
        for f in stages[:cfg.get("stages", 7)]:
            f()
        S.finish()
    return nc


def _consts(S_):
    c = np.zeros((128, NCOLS), np.float32)
    p = np.arange(128)[:, None].astype(np.float32)
    f = np.arange(128)[None, :].astype(np.float32)
    c[:, C_ID:C_ID + 128] = np.eye(128, dtype=np.float32)
    c[:, C_ONE:C_ONE + 128] = 1.0
    c[:, C_MEAN:C_MEAN + 128] = 1.0 / 128
    c[:, C_POS:C_POS + 128] = np.maximum(f - p, 0)
    c[:, C_NEG:C_NEG + 128] = np.maximum(p - f, 0)
    a4 = np.tile(np.arange(128, dtype=np.float32), 4)[None, :]
    c[:, C_IDX:C_IDX + 512] = a4 + 1.0
    c[:, C_CMA:C_CMA + 512] = 128.0 - a4
    c[:, C_COLF] = 127.0 - p[:, 0]
    c[:, C_COLB] = p[:, 0]
    c[:, C_CC] = 128.0

    def perm(n, half, rot):
        m = np.zeros((128, 128), np.float32)
        for i in range(half):
            m[i, i + half] = 1.0
            m[i + half, i] = 1.0
        return m
    c[:, C_PA:C_PA + 128] = perm(128, 16, 32)
    c[:, C_PB:C_PB + 128] = perm(128, 64, 128)
    c[:, C_PM:C_PM + 128] = perm(128, 32, 64)
    f2 = np.arange(256)[None, :].astype(np.float32)
    c[:, C_MB:C_MB + 256] = ((f2 - p >= 0) & (f2 - p <= 128)).astype(np.float32)
    c[:, C_MG:C_MG + 128] = (p <= f).astype(np.float32)
    c[:, C_MG + 128:C_MG + 256] = (p > f).astype(np.float32)

    def rope_tab(theta, rot, rows):
        half = rot // 2
        inv = np.power(np.float32(theta), -np.arange(half, dtype=np.float32) * np.float32(2.0 / rot)).astype(np.float32)
        pos = np.arange(S_, dtype=np.float32)
        ang = (pos[:, None] * inv[None, :]).astype(np.float32)
        cs, sn = np.cos(ang).astype(np.float32), np.sin(ang).astype(np.float32)
        C = np.ones((rows, S_), np.float32)
        Sn = np.zeros((rows, S_), np.float32)
        C[0:half] = cs.T
        C[half:rot] = cs.T
        Sn[0:half] = -sn.T
        Sn[half:rot] = sn.T
        return np.ascontiguousarray(np.stack([C, Sn]))
    return c, rope_tab(500000.0, 32, 128), rope_tab(10000.0, 128, 128), rope_tab(10000.0, 64, 64)


_NC_CACHE = {}


def run_cfg(cfg, seqs, weights, n_cores):
    key = tuple(sorted(cfg.items()))
    if key not in _NC_CACHE:
        _NC_CACHE[key] = build(cfg)
    nc = _NC_CACHE[key]
    cst, rA, rB, rM = _consts(cfg["S"])
    base = {k: np.ascontiguousarray(np.asarray(v, dtype=np.float32)) for k, v in weights.items()}
    base.update(cst=cst, ropeA=rA, ropeB=rB, ropeM=rM)
    in_maps = []
    for i in range(n_cores):
        m = dict(base)
        m["x"] = np.ascontiguousarray(seqs[i % len(seqs)])
        in_maps.append(m)
    res = run_bass_kernel_spmd(nc, in_maps, core_ids=list(range(n_cores)))
    return [res.results[i]["y"] for i in range(len(seqs))]


def kernel(x_prompt, x_sample, **weights):
    xp = np.asarray(x_prompt, dtype=np.float32)
    xs = np.asarray(x_sample, dtype=np.float32)
    seqs = [xp[i] for i in range(xp.shape[0])] + [xs[i] for i in range(xs.shape[0])]
    outs = run_cfg(FULL_CFG, seqs, weights, 8)
    yp = np.stack(outs[:xp.shape[0]]).astype(np.float32)
    ys = np.stack(outs[xp.shape[0]:]).astype(np.float32)
    return (yp, ys)
```

```python
import contextlib
import math
import numpy as np
import concourse.bass as bass
import concourse.mybir as mybir
from concourse.bass_utils import run_bass_kernel_spmd

F32 = mybir.dt.float32
BF16 = mybir.dt.bfloat16
AF = mybir.ActivationFunctionType
ALU = mybir.AluOpType

SAME_ENGINE_SYNC = True
N_DMA_SEMS = 8
EPS = 1e-6

FULL_CFG = dict(S=4096, D=2048, FF=5632, AH=8, BH=8, CH=4, DH=8, QR=448, KVR=160, T=512)

C_ID, C_ONE, C_MEAN, C_POS, C_NEG, C_IDX, C_CMA = 0, 128, 256, 384, 512, 640, 1152
C_COLF, C_COLB, C_CC = 1664, 1665, 1666
C_PA, C_PB, C_PM, C_MB, C_MG = 1668, 1796, 1924, 2052, 2308
NCOLS = 2564


class Buf:
    __slots__ = ("w", "r", "excl")

    def __init__(self, excl=False):
        self.w = None
        self.r = []
        self.excl = excl


def bufs(n):
    return [Buf() for _ in range(n)]


class Sched:
    ENGS = ("tensor", "vector", "scalar", "gpsimd", "sync")
    QUEUES = ("sync", "gpsimd")

    def __init__(self, nc, stack):
        self.nc = nc
        self.lists = {e: [] for e in self.ENGS}
        self.sem = {e: stack.enter_context(nc.semaphore("s_" + e)) for e in self.ENGS}
        self.count = {e: 0 for e in self.ENGS}
        self.waited = {e: {} for e in self.ENGS}
        self.dsem, self.dval, self.dnext = {}, {}, {}
        for q in self.QUEUES:
            self.dsem[q] = [stack.enter_context(nc.semaphore("d_%s%d" % (q, i))) for i in range(N_DMA_SEMS)]
            self.dval[q] = [0] * N_DMA_SEMS
            self.dnext[q] = 0
        self.n_ops = 0

    def _wait(self, eng, tok):
        key, val, teng, sem = tok
        if teng == eng and (eng == "tensor" or not SAME_ENGINE_SYNC):
            return
        if self.waited[eng].get(key, 0) >= val:
            return
        self.waited[eng][key] = val
        self.lists[eng].append(("wait", sem, val))

    def _deps(self, eng, reads, writes):
        for b in reads:
            if b.w is not None:
                self._wait(eng, b.w)
            if b.excl:
                for t in b.r:
                    if t[2] != eng:
                        self._wait(eng, t)
        for b in writes:
            if b.w is not None:
                self._wait(eng, b.w)
            for t in b.r:
                self._wait(eng, t)

    @staticmethod
    def _commit(tok, reads, writes):
        for b in reads:
            b.r.append(tok)
        for b in writes:
            b.w = tok
            b.r = []

    def op(self, eng, fn, reads=(), writes=()):
        self._deps(eng, reads, writes)
        self.count[eng] += 1
        tok = ("e_" + eng, self.count[eng], eng, self.sem[eng])
        self.lists[eng].append(("inst", fn, self.sem[eng], 1))
        self._commit(tok, reads, writes)
        self.n_ops += 1
        return tok

    def dma(self, q, fn, reads=(), writes=()):
        i = self.dnext[q]
        self.dnext[q] = (i + 1) % N_DMA_SEMS
        sem = self.dsem[q][i]
        key = "d_%s%d" % (q, i)
        if self.dval[q][i] > 0:
            self._wait(q, (key, self.dval[q][i], "dma", sem))
        self._deps(q, reads, writes)
        self.dval[q][i] += 16
        tok = (key, self.dval[q][i], "dma", sem)
        self.lists[q].append(("inst", fn, sem, 16))
        self._commit(tok, reads, writes)
        self.n_ops += 1
        return tok

    def barrier(self):
        for e in self.ENGS:
            for x in self.ENGS:
                if x != e and self.count[x] > 0:
                    self._wait(e, ("e_" + x, self.count[x], x, self.sem[x]))
            for q in self.QUEUES:
                for i in range(N_DMA_SEMS):
                    if self.dval[q][i] > 0:
                        self._wait(e, ("d_%s%d" % (q, i), self.dval[q][i], "dma", self.dsem[q][i]))

    def flush(self):
        lists = self.lists
        self.lists = {e: [] for e in self.ENGS}

        def run(e, name):
            for ent in lists[name]:
                if ent[0] == "wait":
                    e.wait_ge(ent[1], ent[2])
                else:
                    ent[1](e).then_inc(ent[2], ent[3])

        with self.nc.Block() as block:
            @block.tensor
            def _(e):
                run(e, "tensor")

            @block.vector
            def _(e):
                run(e, "vector")

            @block.scalar
            def _(e):
                run(e, "scalar")

            @block.gpsimd
            def _(e):
                run(e, "gpsimd")

            @block.sync
            def _(e):
                run(e, "sync")

    def finish(self):
        self.barrier()
        self.flush()


def act(S, out, in_, func, r, w, scale=None, bias=None):
    kw = {}
    if scale is not None:
        kw["scale"] = scale
    if bias is not None:
        kw["bias"] = bias
    return S.op("scalar", lambda e: e.activation(out=out, in_=in_, func=func, **kw), r, w)


def tt(S, eng, out, a, b, op, r, w):
    return S.op(eng, lambda e: e.tensor_tensor(out=out, in0=a, in1=b, op=op), r, w)


def ts(S, eng, out, a, s1, op0, r, w, s2=None, op1=None):
    if op1 is None:
        return S.op(eng, lambda e: e.tensor_scalar(out=out, in0=a, scalar1=s1, scalar2=None, op0=op0), r, w)
    return S.op(eng, lambda e: e.tensor_scalar(out=out, in0=a, scalar1=s1, scalar2=s2, op0=op0, op1=op1), r, w)


def stt(S, out, a, scalar, b, op0, op1, r, w):
    return S.op("vector", lambda e: e.scalar_tensor_tensor(out=out, in0=a, scalar=scalar, in1=b, op0=op0, op1=op1), r, w)


def cp(S, eng, out, in_, r, w):
    if eng == "scalar":
        return S.op("scalar", lambda e: e.activation(out=out, in_=in_, func=AF.Copy), r, w)
    return S.op(eng, lambda e: e.tensor_copy(out=out, in_=in_), r, w)


def mm(S, out, pairs, r, w, start=True, stop=True):
    pairs = list(pairs)

    def fn(e):
        n = len(pairs)
        inst = None
        for i, (l, rr) in enumerate(pairs):
            inst = e.matmul(out, l, rr, start=(start and i == 0), stop=(stop and i == n - 1))
        return inst

    return S.op("tensor", fn, r, w)


def tr(S, out, in_, ident, r, w):
    return S.op("tensor", lambda e: e.transpose(out, in_, ident), r, w)


def dma(S, q, out, in_, r, w, slow=False):
    if slow:
        return S.dma(q, lambda e: e.dma_start(out=out, in_=in_, allow_slow_non_contiguous=True), r, w)
    return S.dma(q, lambda e: e.dma_start(out=out, in_=in_), r, w)


def memset(S, eng, ap, val, w):
    return S.op(eng, lambda e: e.memset(ap, val), (), w)


def rows_of(n):
    out = []
    o = 0
    while o < n:
        out.append((o, min(128, n - o)))
        o += 128
    return out


def build(cfg):
    S_, D, FF, AH, BH, CH, DH, QR, KVR, T = (cfg[k] for k in ("S", "D", "FF", "AH", "BH", "CH", "DH", "QR", "KVR", "T"))
    DC, FC, NT, NCH = D // 128, FF // 128, S_ // T, S_ // 128
    MIXE, MIXO = (AH + BH) * 128, CH * 256 + DH * 128
    MEC, MOC = MIXE // 128, MIXO // 128
    XC = max(DC, MEC, MOC)
    EVEN_IN = 3 * AH * 128 + 4 * BH * 128
    ODD_IN = 2 * CH * 128 + 2 * CH * 256 + 32 + QR + KVR + 64
    QCH, KVCH = rows_of(QR), rows_of(KVR)
    TB = T // 128
    FH = FC // 2
    assert FC % 4 == 0

    nc = bass.Bass("TRN2", target_bir_lowering=False)

    def din(name, shape):
        return nc.dram_tensor(name, list(shape), F32, kind="ExternalInput").ap()

    def dscr(name, shape, dt):
        return nc.dram_tensor(name, list(shape), dt, kind="Internal").ap()

    x = din("x", [S_, D])
    norm_g = din("norm_g", [2, 3, D])
    final_g = din("final_norm_g", [D])
    wg = din("ffn_w_gate", [2, 2, D, FF])
    wu = din("ffn_w_up", [2, 2, D, FF])
    wd = din("ffn_w_down", [2, 2, FF, D])
    ab_in = din("ab_w_in", [1, D, EVEN_IN])
    ab_out = din("ab_w_out", [1, MIXE, D])
    ret_decay = din("ret_decay", [1, 2, BH])
    cd_in = din("cd_w_in", [1, D, ODD_IN])
    cd_out = din("cd_w_out", [1, MIXO, D])
    gate_w2 = din("gla_gate_w2", [1, 2, 16, CH * 128])
    gate_b = din("gla_gate_b", [1, 2, CH * 128])
    gla_g = din("gla_norm_g", [1, 256])
    q_g = din("mla_q_norm_g", [1, QR])
    w_uq = din("mla_w_uq", [1, QR, DH * 192])
    kv_g = din("mla_kv_norm_g", [1, KVR])
    w_ukv = din("mla_w_ukv", [1, KVR, DH * 256])
    cst = din("cst", [128, NCOLS])
    ropeA = din("ropeA", [2, 128, S_])
    ropeB = din("ropeB", [2, 128, S_])
    ropeM = din("ropeM", [2, 64, S_])
    y = nc.dram_tensor("y", [S_, D], F32, kind="ExternalOutput").ap()

    hT = dscr("hT", [D, S_], F32)
    mixT = dscr("mixT", [XC * 128, S_], BF16)
    qaT = dscr("qaT", [AH * 128, S_], BF16)
    kaT = dscr("kaT", [AH * 128, S_], BF16)
    va = dscr("va", [S_, AH * 128], BF16)
    qbT = dscr("qbT", [BH * 128, S_], BF16)
    kbT = dscr("kbT", [BH * 128, S_], BF16)
    vb = dscr("vb", [S_, BH * 128], BF16)
    gbT = dscr("gbT", [BH * 128, S_], BF16)
    qcT = dscr("qcT", [CH * 128, S_], BF16)
    kcT = dscr("kcT", [CH * 128, S_], BF16)
    vc = dscr("vc", [S_, CH * 256], BF16)
    rcT = dscr("rcT", [CH * 256, S_], BF16)
    lfT = dscr("lfT", [CH * 128, S_], F32)
    lbT = dscr("lbT", [CH * 128, S_], F32)
    qnT = dscr("qnT", [DH * 128, S_], BF16)
    qrT = dscr("qrT", [DH * 64, S_], BF16)
    knT = dscr("knT", [DH * 128, S_], BF16)
    krT = dscr("krT", [64, S_], BF16)
    vm = dscr("vm", [S_, DH * 128], BF16)
    NBA = 4 * (FF // 256) + 48
    WSB = 48
    wscr_l = [dscr("wscr%d" % i, [WSB, 128, XC * 512], BF16) for i in range((NBA + WSB - 1) // WSB)]

    class _W:
        def __getitem__(self, b):
            return wscr_l[b // WSB][b % WSB]
    wscr = _W()
    wdscr = dscr("wdscr", [8 * DC, 128, FH * 128], BF16)
    wkeys, wdkeys = {}, {}

    with contextlib.ExitStack() as gst:
        S = Sched(nc, gst)

        uniq = [0]

        def sb(stack, name, shape, dt):
            uniq[0] += 1
            return stack.enter_context(nc.sbuf_tensor("%s_%d" % (name, uniq[0]), list(shape), dt))

        cf = sb(gst, "cf", [128, NCOLS], F32)
        cb = sb(gst, "cb", [128, NCOLS], BF16)
        ng = sb(gst, "ng", [128, 6, DC], F32)
        fg = sb(gst, "fg", [128, DC], F32)
        lg = sb(gst, "lg", [128, 2 * BH], F32)
        B_c = Buf()
        PS = [gst.enter_context(nc.psum_tensor("ps%d" % i, [128, 512], F32)) for i in range(8)]
        BPS = [Buf(excl=True) for _ in range(8)]

        dma(S, "sync", cf[:], cst[:, :], (), [B_c])
        cp(S, "vector", cb[:], cf[:], [B_c], [B_c])
        gtmp = sb(gst, "gtmp", [128, 128], F32)
        ftmp = sb(gst, "ftmp", [128, 128], F32)
        dma(S, "sync", gtmp[0:6 * DC, :], norm_g.rearrange("a b (c p) -> (a b c) p", p=128), (), [B_c])
        dma(S, "sync", ftmp[0:DC, :], final_g.rearrange("(c p) -> c p", p=128), (), [B_c])
        tr(S, PS[7][:, 0:6 * DC], gtmp[0:6 * DC, :], cf[0:6 * DC, C_ID:C_ID + 6 * DC], [B_c], [BPS[7]])
        cp(S, "vector", ng[:].rearrange("p k c -> p (k c)"), PS[7][:, 0:6 * DC], [BPS[7]], [B_c])
        tr(S, PS[7][:, 0:DC], ftmp[0:DC, :], cf[0:DC, C_ID:C_ID + DC], [B_c], [BPS[7]])
        cp(S, "vector", fg[:], PS[7][:, 0:DC], [BPS[7]], [B_c])
        dma(S, "sync", lg[:], ret_decay.rearrange("a b c -> (a b c)").partition_broadcast(128), (), [B_c])
        act(S, lg[:], lg[:], AF.Exp, [B_c], [B_c])
        act(S, lg[:], lg[:], AF.Copy, [B_c], [B_c], scale=-1.0)

        ident_f = cf[:, C_ID:C_ID + 128]
        ones_f = cf[:, C_ONE:C_ONE + 128]
        mean_f = cf[:, C_MEAN:C_MEAN + 128]
        ident_b = cb[:, C_ID:C_ID + 128]
        ones_b = cb[:, C_ONE:C_ONE + 128]

        def token_phase(pidx):
            with contextlib.ExitStack() as st:
                hT_t = sb(st, "hT_t", [128, DC, T], F32)
                xn = sb(st, "xn", [128, XC, T], BF16)
                aT = sb(st, "aT", [128, FH, T], BF16)
                wA = [sb(st, "wA%d" % i, [128, XC, 512], BF16) for i in range(2)]
                wD = [sb(st, "wD%d" % i, [128, FH, 128], BF16) for i in range(2)]
                sq = [sb(st, "sq%d" % i, [128, T], BF16) for i in range(2)]
                rstd = sb(st, "rstd", [128, T], F32)
                sg = [sb(st, "sg%d" % i, [128, T], F32) for i in range(2)]
                xio = [sb(st, "xio%d" % i, [128, D], F32) for i in range(2)] if pidx != 1 else None
                stg = [sb(st, "stg%d" % i, [128, T], BF16) for i in range(4)]
                stgf = [sb(st, "stgf%d" % i, [128, T], F32) for i in range(2)] if pidx == 1 else None
                t1b = [sb(st, "t1b%d" % i, [128, T], F32) for i in range(2)]
                t2b = [sb(st, "t2b%d" % i, [128, T], F32) for i in range(2)]
                xbf = [sb(st, "xbf%d" % i, [128, T], BF16) for i in range(2)]
                rope_t = sb(st, "rope_t", [128, 4 if pidx == 0 else 2, T], F32)
                B_h, B_xn, B_a = bufs(DC), bufs(XC), bufs(FH)
                B_wA, B_wD, B_sq, B_sg, B_xio = bufs(2), bufs(2), bufs(2), bufs(2), bufs(2)
                B_stg, B_stgf, B_t1, B_t2, B_xbf = bufs(4), bufs(2), bufs(2), bufs(2), bufs(2)
                B_rstd, B_rope = Buf(), Buf()
                cnt = {"wA": 0, "wD": 0, "stg": 0, "stgf": 0, "rp": 0, "xio": 0}
                if pidx == 1:
                    w2 = sb(st, "w2", [16, 2, CH * 128], BF16)
                    nbias = sb(st, "nbias", [128, 2, CH], F32)
                    wuq_s = sb(st, "wuq_s", [128, len(QCH), DH * 192], BF16)
                    wukv_s = sb(st, "wukv_s", [128, len(KVCH), DH * 256], BF16)
                    qg_s = sb(st, "qg_s", [128, len(QCH)], F32)
                    kvg_s = sb(st, "kvg_s", [128, len(KVCH)], F32)
                    lowf = sb(st, "lowf", [16, 2, T], BF16)
                    cq_s = sb(st, "cq_s", [128, len(QCH), T], F32)
                    cqn = sb(st, "cqn", [128, len(QCH), T], BF16)
                    ckv_s = sb(st, "ckv_s", [128, len(KVCH), T], F32)
                    ckvn = sb(st, "ckvn", [128, len(KVCH), T], BF16)
                    B_od = Buf()
                    B_low, B_cq, B_cqn, B_ckv, B_ckvn = Buf(), Buf(), Buf(), Buf(), Buf()
                    rstd2 = sb(st, "rstd2", [128, T], F32)
                    B_rstd2 = Buf()
                    dma(S, "gpsimd", w2[:], gate_w2[0].rearrange("a r n -> r a n"), (), [B_od])
                    dma(S, "sync", nbias[:], gate_b[0].rearrange("a (c p) -> p a c", p=128), (), [B_od], slow=True)
                    ts(S, "vector", nbias[:], nbias[:], -1.0, ALU.mult, [B_od], [B_od])
                    for ci, (o, rws) in enumerate(QCH):
                        dma(S, "gpsimd", wuq_s[0:rws, ci, :], w_uq[0, o:o + rws, :], (), [B_od])
                        dma(S, "sync", qg_s[0:rws, ci:ci + 1], q_g[0, o:o + rws].rearrange("(p a) -> p a", a=1), (), [B_od])
                    for ci, (o, rws) in enumerate(KVCH):
                        dma(S, "gpsimd", wukv_s[0:rws, ci, :], w_ukv[0, o:o + rws, :], (), [B_od])
                        dma(S, "sync", kvg_s[0:rws, ci:ci + 1], kv_g[0, o:o + rws].rearrange("(p a) -> p a", a=1), (), [B_od])

                def load_wA(src2d, kc, ncols, col0, dst_col0=0, key=None):
                    i = cnt["wA_cur"]
                    if key is None:
                        key = (src2d.tensor.name, src2d.offset, col0, ncols, dst_col0)
                    if key not in wkeys:
                        assert len(wkeys) < NBA
                        wkeys[key] = (len(wkeys), Buf(), cnt["tile"])
                    blk, bb, t_created = wkeys[key]
                    sview = wscr[blk].rearrange("p (c f) -> p c f", f=512)[:, 0:kc, dst_col0:dst_col0 + ncols]
                    if t_created == cnt["tile"]:
                        dma(S, "gpsimd", wA[i][:, 0:kc, dst_col0:dst_col0 + ncols],
                            src2d[:, col0:col0 + ncols].rearrange("(c p) f -> p c f", p=128), (), [B_wA[i]])
                        dma(S, "sync", sview, wA[i][:, 0:kc, dst_col0:dst_col0 + ncols], [B_wA[i]], [bb])
                    elif key[0] == "gu":
                        if dst_col0 == 0:
                            dma(S, "gpsimd", wA[i][:, 0:kc, :], wscr[blk].rearrange("p (c f) -> p c f", f=512)[:, 0:kc, :], [bb], [B_wA[i]])
                    else:
                        dma(S, "gpsimd", wA[i][:, 0:kc, dst_col0:dst_col0 + ncols], sview, [bb], [B_wA[i]])

                def next_wA():
                    cnt["wA"] += 1
                    cnt["wA_cur"] = cnt["wA"] % 2
                    return cnt["wA_cur"]

                def sumsq_chunk(c):
                    i = c % 2
                    act(S, sq[i][:], hT_t[:, c, :], AF.Square, [B_h[c]], [B_sq[i]])
                    mm(S, PS[6][:], [(ones_b, sq[i][:])], [B_sq[i], B_c], [BPS[6]], start=(c == 0), stop=(c == DC - 1))
                    if c == DC - 1:
                        cnt["ss"] = True

                def rms_to_xn(gcol):
                    if not cnt.get("ss"):
                        for c in range(DC):
                            sumsq_chunk(c)
                    cnt["ss"] = False
                    act(S, rstd[:], PS[6][:], AF.Ln, [BPS[6]], [B_rstd], scale=1.0 / D, bias=EPS)
                    act(S, rstd[:], rstd[:], AF.Exp, [B_rstd], [B_rstd], scale=-0.5)

                def norm_apply(gap_fn, out_fn, out_bufs):
                    for c in range(DC):
                        stt(S, out_fn(c), hT_t[:, c, :], gap_fn(c), rstd[:], ALU.mult, ALU.mult,
                            [B_h[c], B_rstd, B_c], [out_bufs[c]])
                    if out_bufs is B_xn:
                        cnt["fresh"] = True

                def mm_xn(out, pair_fn, kc, rd, wbuf):
                    if cnt.get("fresh"):
                        cnt["fresh"] = False
                        for c in range(kc):
                            mm(S, out, [pair_fn(c)], rd + [B_xn[c]], [wbuf], start=(c == 0), stop=(c == kc - 1))
                    else:
                        mm(S, out, [pair_fn(c) for c in range(kc)], rd + B_xn[0:kc], [wbuf])

                def ffn(l, w):
                    rms_to_xn(None)
                    gk = l * 3 + (0 if w == 0 else 2)
                    norm_apply(lambda c: ng[:, gk, c:c + 1], lambda c: xn[:, c, :], B_xn)
                    for half in range(2):
                        f0 = half * FH
                        for j in range(FH // 2):
                            i = next_wA()
                            load_wA(wg[l, w], DC, 256, (f0 + 2 * j) * 128, 0, key=("gu", l, w, half, j))
                            load_wA(wu[l, w], DC, 256, (f0 + 2 * j) * 128, 256, key=("gu", l, w, half, j))
                            for fi in range(2):
                                f = 2 * j + fi
                                pg, pu = f % 2, 2 + f % 2
                                mm_xn(PS[pg][:, 0:T], (lambda c, i=i, fi=fi: (wA[i][:, c, fi * 128:(fi + 1) * 128], xn[:, c, :])), DC,
                                      [B_wA[i]], BPS[pg])
                                mm(S, PS[pu][:, 0:T], [(wA[i][:, c, 256 + fi * 128:256 + (fi + 1) * 128], xn[:, c, :]) for c in range(DC)],
                                   [B_wA[i]] + B_xn[0:DC], [BPS[pu]])
                                act(S, sg[f % 2][:], PS[pg][:, 0:T], AF.Silu, [BPS[pg]], [B_sg[f % 2]])
                                tt(S, "vector", aT[:, f, :], sg[f % 2][:], PS[pu][:, 0:T], ALU.mult, [B_sg[f % 2], BPS[pu]], [B_a[f]])
                        for dcn in range(DC):
                            cnt["wD"] += 1
                            i = cnt["wD"] % 2
                            key = (l, w, half, dcn)
                            if key not in wdkeys:
                                blk = len(wdkeys)
                                wdkeys[key] = (blk, Buf())
                                dma(S, "gpsimd", wD[i][:], wd[l, w][f0 * 128:(f0 + FH) * 128, dcn * 128:(dcn + 1) * 128].rearrange("(f p) d -> p f d", p=128),
                                    (), [B_wD[i]])
                                dma(S, "sync", wdscr[blk].rearrange("p (f d) -> p f d", d=128), wD[i][:], [B_wD[i]], [wdkeys[key][1]])
                            else:
                                blk, bb = wdkeys[key]
                                dma(S, "gpsimd", wD[i][:], wdscr[blk].rearrange("p (f d) -> p f d", d=128), [bb], [B_wD[i]])
                            pb = 4 + dcn % 2
                            mm(S, PS[pb][:, 0:T], [(wD[i][:, f, :], aT[:, f, :]) for f in range(FH)], [B_wD[i]] + B_a, [BPS[pb]])
                            stt(S, hT_t[:, dcn, :], PS[pb][:, 0:T], 0.5, hT_t[:, dcn, :], ALU.mult, ALU.add, [BPS[pb]], [B_h[dcn]])
                            if half == 1:
                                if dcn >= 2:
                                    sumsq_chunk(dcn - 2)
                                if dcn == DC - 1:
                                    for c_ in range(max(DC - 2, 0), DC):
                                        sumsq_chunk(c_)

                def out_proj(w_out2d, mc):
                    for cb0 in range(0, D, 512):
                        ncol = min(512, D - cb0)
                        i = next_wA()
                        load_wA(w_out2d, mc, ncol, cb0)
                        for dd in range(ncol // 128):
                            dcn = cb0 // 128 + dd
                            pb = 4 + dcn % 2
                            mm(S, PS[pb][:, 0:T], [(wA[i][:, m, dd * 128:(dd + 1) * 128], xn[:, m, :]) for m in range(mc)],
                               [B_wA[i]] + B_xn[0:mc], [BPS[pb]])
                            tt(S, "vector", hT_t[:, dcn, :], hT_t[:, dcn, :], PS[pb][:, 0:T], ALU.add, [BPS[pb]], [B_h[dcn]])
                            if dcn >= 2:
                                sumsq_chunk(dcn - 2)
                            if dcn == DC - 1:
                                for c_ in range(max(DC - 2, 0), DC):
                                    sumsq_chunk(c_)

                def get_stg():
                    cnt["stg"] += 1
                    return cnt["stg"] % 4

                def rope_fm(ps_i, rows, ctab, stab, perm, out_dram):
                    cnt["rp"] += 1
                    k = cnt["rp"] % 2
                    RP = cfg.get("rp", 9)
                    if RP < 1:
                        return
                    cp(S, "scalar", xbf[k][0:rows, :], PS[ps_i][0:rows, 0:T], [BPS[ps_i]], [B_xbf[k]])
                    if RP < 2:
                        return
                    mm(S, PS[7][0:rows, 0:T], [(perm, xbf[k][0:rows, :])], [B_xbf[k], B_c], [BPS[7]])
                    if RP < 3:
                        return
                    tt(S, "vector", t1b[k][0:rows, :], PS[ps_i][0:rows, 0:T], ctab, ALU.mult, [BPS[ps_i], B_rope], [B_t1[k]])
                    tt(S, "vector", t2b[k][0:rows, :], PS[7][0:rows, 0:T], stab, ALU.mult, [BPS[7], B_rope], [B_t2[k]])
                    if RP < 4:
                        return
                    si = get_stg()
                    tt(S, "vector", stg[si][0:rows, :], t1b[k][0:rows, :], t2b[k][0:rows, :], ALU.add, [B_t1[k], B_t2[k]], [B_stg[si]])
                    if RP < 5:
                        return
                    dma(S, "sync", out_dram, stg[si][0:rows, :], [B_stg[si]], ())

                def proj_fm(i, col, ncols_out, kc, ps_i):
                    mm_xn(PS[ps_i][0:ncols_out, 0:T], (lambda c: (wA[i][:, c, col:col + ncols_out], xn[:, c, :])), kc,
                          [B_wA[i]], BPS[ps_i])

                def proj_tm(w2d, col0, ncols, out_dram2d, t0):
                    for cbk in range(0, ncols, 512):
                        nb_ = min(512, ncols - cbk)
                        i = next_wA()
                        load_wA(w2d, DC, nb_, col0 + cbk)
                        for b in range(TB):
                            pb = 4 + b % 2
                            mm_xn(PS[pb][:, 0:nb_], (lambda c, i=i, b=b, nb_=nb_: (xn[:, c, b * 128:(b + 1) * 128], wA[i][:, c, 0:nb_])), DC,
                                  [B_wA[i]], BPS[pb])
                            si = get_stg()
                            cp(S, "scalar", stg[si][:, 0:nb_], PS[pb][:, 0:nb_], [BPS[pb]], [B_stg[si]])
                            dma(S, "sync", out_dram2d[t0 + b * 128:t0 + (b + 1) * 128, cbk:cbk + nb_], stg[si][:, 0:nb_], [B_stg[si]], ())

                def sect_fm(w2d, col0, nfeat, handler):
                    for cbk in range(0, nfeat, 512):
                        nb_ = min(512, nfeat - cbk)
                        i = next_wA()
                        load_wA(w2d, DC, nb_, col0 + cbk)
                        for (o, rws) in rows_of(nb_):
                            ps_i = (cnt["rp"] + o // 128) % 2
                            ps_i = 0 if (o // 128) % 2 == 0 else 1
                            proj_fm(i, o, rws, DC, ps_i)
                            handler(ps_i, cbk + o, rws)

                DBG = cfg.get("dbg", 99)
                for t in range(min(NT, cfg.get("ntiles", NT))):
                    t0 = t * T
                    tsl = slice(t0, t0 + T)
                    cnt["tile"] = (pidx, t)
                    if pidx == 0:
                        for b in range(TB):
                            cnt["xio"] += 1
                            k = cnt["xio"] % 2
                            dma(S, "sync", xio[k][:], x[t0 + b * 128:t0 + (b + 1) * 128, :], (), [B_xio[k]])
                            for c0 in range(0, DC, 4):
                                nn = min(4, DC - c0)
                                for cc in range(nn):
                                    c = c0 + cc
                                    tr(S, PS[7][:, cc * 128:(cc + 1) * 128], xio[k][:, c * 128:(c + 1) * 128], ident_f,
                                       [B_xio[k], B_c], [BPS[7]])
                                cp(S, "vector", hT_t[:, c0:c0 + nn, b * 128:(b + 1) * 128],
                                   PS[7][:, 0:nn * 128].rearrange("p (c t) -> p c t", t=128), [BPS[7]], B_h[c0:c0 + nn])
                    else:
                        dma(S, "sync", hT_t[:], hT[:, tsl].rearrange("(c p) t -> p c t", p=128), (), B_h)
                        mc = MEC if pidx == 1 else MOC
                        dma(S, "sync", xn[:, 0:mc, :], mixT[0:mc * 128, tsl].rearrange("(c p) t -> p c t", p=128), (), B_xn[0:mc])
                        out_proj(ab_out[0] if pidx == 1 else cd_out[0], mc)
                    if DBG < 2:
                        continue
                    if pidx == 0:
                        ffn(0, 0)
                    elif pidx == 1:
                        ffn(0, 1)
                        ffn(1, 0)
                    else:
                        ffn(1, 1)
                    if pidx == 2:
                        rms_to_xn(None)
                        norm_apply(lambda c: fg[:, c:c + 1], lambda c: hT_t[:, c, :], B_h)
                        for b in range(TB):
                            cnt["xio"] += 1
                            k = cnt["xio"] % 2
                            for c0 in range(0, DC, 4):
                                nn = min(4, DC - c0)
                                for cc in range(nn):
                                    c = c0 + cc
                                    tr(S, PS[7][:, cc * 128:(cc + 1) * 128], hT_t[:, c, b * 128:(b + 1) * 128], ident_f,
                                       [B_h[c], B_c], [BPS[7]])
                                cp(S, "vector", xio[k][:, c0 * 128:(c0 + nn) * 128], PS[7][:, 0:nn * 128], [BPS[7]], [B_xio[k]])
                            dma(S, "sync", y[t0 + b * 128:t0 + (b + 1) * 128, :], xio[k][:], [B_xio[k]], ())
                        continue
                    if DBG < 3:
                        continue
                    dma(S, "sync", hT[:, tsl].rearrange("(c p) t -> p c t", p=128), hT_t[:], B_h, ())
                    rms_to_xn(None)
                    gk = pidx * 3 + 1
                    norm_apply(lambda c: ng[:, gk, c:c + 1], lambda c: xn[:, c, :], B_xn)
                    if DBG < 4:
                        continue
                    if pidx == 0:
                        w2d = ab_in[0]
                        dma(S, "sync", rope_t[:, 0:2, :], ropeA[:, :, tsl].rearrange("a p t -> p a t"), (), [B_rope])
                        dma(S, "sync", rope_t[:, 2:4, :], ropeB[:, :, tsl].rearrange("a p t -> p a t"), (), [B_rope])
                        PA = cb[:, C_PA:C_PA + 128]
                        PB = cb[:, C_PB:C_PB + 128]
                        o_qa, o_ka, o_va = 0, AH * 128, 2 * AH * 128
                        o_qb = 3 * AH * 128
                        o_kb, o_vb, o_gb = o_qb + BH * 128, o_qb + 2 * BH * 128, o_qb + 3 * BH * 128
                        def h_silu(p, fo, r, dst=gbT):
                            si = get_stg()
                            act(S, stg[si][0:r, :], PS[p][0:r, 0:T], AF.Silu, [BPS[p]], [B_stg[si]])
                            dma(S, "sync", dst[fo:fo + r, tsl], stg[si][0:r, :], [B_stg[si]], ())
                        sects = [
                            lambda: sect_fm(w2d, o_qa, AH * 128, lambda p, fo, r: rope_fm(p, r, rope_t[0:r, 0, :], rope_t[0:r, 1, :], PA, qaT[fo:fo + r, tsl])),
                            lambda: sect_fm(w2d, o_ka, AH * 128, lambda p, fo, r: rope_fm(p, r, rope_t[0:r, 0, :], rope_t[0:r, 1, :], PA, kaT[fo:fo + r, tsl])),
                            lambda: proj_tm(w2d, o_va, AH * 128, va, t0),
                            lambda: sect_fm(w2d, o_qb, BH * 128, lambda p, fo, r: rope_fm(p, r, rope_t[0:r, 2, :], rope_t[0:r, 3, :], PB, qbT[fo:fo + r, tsl])),
                            lambda: sect_fm(w2d, o_kb, BH * 128, lambda p, fo, r: rope_fm(p, r, rope_t[0:r, 2, :], rope_t[0:r, 3, :], PB, kbT[fo:fo + r, tsl])),
                            lambda: proj_tm(w2d, o_vb, BH * 128, vb, t0),
                            lambda: sect_fm(w2d, o_gb, BH * 128, h_silu),
                        ]
                        for f_ in sects[:cfg.get("nsect", 7)]:
                            f_()
                    else:
                        w2d = cd_in[0]
                        dma(S, "sync", rope_t[0:64, 0:2, :], ropeM[:, :, tsl].rearrange("a p t -> p a t"), (), [B_rope])
                        PM = cb[0:64, C_PM:C_PM + 64]
                        o_qc, o_kc, o_vc = 0, CH * 128, 2 * CH * 128
                        o_rc = o_vc + CH * 256
                        o_af = o_rc + CH * 256
                        o_ab, o_cq = o_af + 16, o_af + 32
                        o_ckv = o_cq + QR
                        o_kr = o_ckv + KVR

                        def h_plain(dst):
                            def hh(p, fo, r):
                                si = get_stg()
                                cp(S, "scalar", stg[si][0:r, :], PS[p][0:r, 0:T], [BPS[p]], [B_stg[si]])
                                dma(S, "sync", dst[fo:fo + r, tsl], stg[si][0:r, :], [B_stg[si]], ())
                            return hh
                        def h_silu2(p, fo, r):
                            si = get_stg()
                            act(S, stg[si][0:r, :], PS[p][0:r, 0:T], AF.Silu, [BPS[p]], [B_stg[si]])
                            dma(S, "sync", rcT[fo:fo + r, tsl], stg[si][0:r, :], [B_stg[si]], ())

                        blk = []
                        for (cbk, nb_) in ((0, 32 + QR), (32 + QR, KVR + 64)):
                            assert nb_ <= 512
                            i = next_wA()
                            load_wA(w2d, DC, nb_, o_af + cbk)
                            blk.append((cbk, nb_, i))

                        def small_proj(goff, rows, ps_i):
                            for (cbk, nb_, i) in blk:
                                if cbk <= goff and goff + rows <= cbk + nb_:
                                    proj_fm(i, goff - cbk, rows, DC, ps_i)
                                    return
                            raise AssertionError("straddle %d %d" % (goff, rows))

                        for d_ in range(2):
                            small_proj(16 * d_, 16, d_)
                            cp(S, "scalar", lowf[:, d_, :], PS[d_][0:16, 0:T], [BPS[d_]], [B_low])

                        def latent_raw(goff, chs, raw, B_raw, ps_sum):
                            for ci, (o, rws) in enumerate(chs):
                                pr = ci % 2
                                small_proj(goff + o, rws, pr)
                                cp(S, "scalar", raw[0:rws, ci, :], PS[pr][0:rws, 0:T], [BPS[pr]], [B_raw])
                                act(S, sq[ci % 2][0:rws, :], PS[pr][0:rws, 0:T], AF.Square, [BPS[pr]], [B_sq[ci % 2]])
                                mm(S, PS[ps_sum][:, 0:T], [(ones_b[0:rws, :], sq[ci % 2][0:rws, :])], [B_sq[ci % 2], B_c], [BPS[ps_sum]],
                                   start=(ci == 0), stop=(ci == len(chs) - 1))

                        def latent_norm(chs, nfeat, raw, B_raw, nrm, B_nrm, gtile, ps_sum, rs_t, B_rs):
                            act(S, rs_t[:], PS[ps_sum][:, 0:T], AF.Ln, [BPS[ps_sum]], [B_rs], scale=1.0 / nfeat, bias=EPS)
                            act(S, rs_t[:], rs_t[:], AF.Exp, [B_rs], [B_rs], scale=-0.5)
                            for ci, (o, rws) in enumerate(chs):
                                stt(S, nrm[0:rws, ci, :], raw[0:rws, ci, :], gtile[0:rws, ci:ci + 1], rs_t[0:rws, :], ALU.mult, ALU.mult,
                                    [B_raw, B_rs, B_od], [B_nrm])

                        latent_raw(32, QCH, cq_s, B_cq, 2)
                        latent_raw(32 + QR, KVCH, ckv_s, B_ckv, 3)
                        small_proj(32 + QR + KVR, 64, 1)
                        rope_fm(1, 64, rope_t[0:64, 0, :], rope_t[0:64, 1, :], PM, krT[0:64, tsl])
                        latent_norm(QCH, QR, cq_s, B_cq, cqn, B_cqn, qg_s, 2, rstd, B_rstd)
                        latent_norm(KVCH, KVR, ckv_s, B_ckv, ckvn, B_ckvn, kvg_s, 3, rstd2, B_rstd2)

                        sect_fm(w2d, o_qc, CH * 128, h_plain(qcT))
                        sect_fm(w2d, o_kc, CH * 128, h_plain(kcT))
                        proj_tm(w2d, o_vc, CH * 256, vc, t0)
                        sect_fm(w2d, o_rc, CH * 256, h_silu2)

                        for h in range(DH):
                            mm(S, PS[0][:, 0:T], [(wuq_s[0:rws, ci, h * 192:h * 192 + 128], cqn[0:rws, ci, :]) for ci, (o, rws) in enumerate(QCH)],
                               [B_od, B_cqn], [BPS[0]])
                            si = get_stg()
                            cp(S, "scalar", stg[si][:], PS[0][:, 0:T], [BPS[0]], [B_stg[si]])
                            dma(S, "sync", qnT[h * 128:(h + 1) * 128, tsl], stg[si][:], [B_stg[si]], ())
                            mm(S, PS[1][0:64, 0:T], [(wuq_s[0:rws, ci, h * 192 + 128:h * 192 + 192], cqn[0:rws, ci, :]) for ci, (o, rws) in enumerate(QCH)],
                               [B_od, B_cqn], [BPS[1]])
                            rope_fm(1, 64, rope_t[0:64, 0, :], rope_t[0:64, 1, :], PM, qrT[h * 64:(h + 1) * 64, tsl])
                        for h in range(DH):
                            pk = 2 + h % 2
                            mm(S, PS[pk][:, 0:T], [(wukv_s[0:rws, ci, h * 256:h * 256 + 128], ckvn[0:rws, ci, :]) for ci, (o, rws) in enumerate(KVCH)],
                               [B_od, B_ckvn], [BPS[pk]])
                            si = get_stg()
                            cp(S, "scalar", stg[si][:], PS[pk][:, 0:T], [BPS[pk]], [B_stg[si]])
                            dma(S, "sync", knT[h * 128:(h + 1) * 128, tsl], stg[si][:], [B_stg[si]], ())
                        for b in range(TB):
                            for h0 in range(0, DH, 4):
                                nh = min(4, DH - h0)
                                pb = 4 + b % 2
                                for hh in range(nh):
                                    h = h0 + hh
                                    mm(S, PS[pb][:, hh * 128:(hh + 1) * 128],
                                       [(ckvn[0:rws, ci, b * 128:(b + 1) * 128], wukv_s[0:rws, ci, h * 256 + 128:h * 256 + 256]) for ci, (o, rws) in enumerate(KVCH)],
                                       [B_od, B_ckvn], [BPS[pb]])
                                si = get_stg()
                                cp(S, "scalar", stg[si][:, 0:nh * 128], PS[pb][:, 0:nh * 128], [BPS[pb]], [B_stg[si]])
                                dma(S, "sync", vm[t0 + b * 128:t0 + (b + 1) * 128, h0 * 128:(h0 + nh) * 128], stg[si][:, 0:nh * 128], [B_stg[si]], ())
                        for d_ in range(2):
                            dstT = lfT if d_ == 0 else lbT
                            for hc in range(CH):
                                pgt = (d_ * CH + hc) % 2
                                mm(S, PS[pgt][:, 0:T], [(w2[:, d_, hc * 128:(hc + 1) * 128], lowf[:, d_, :])], [B_od, B_low], [BPS[pgt]])
                                k = cnt["stgf"] = cnt["stgf"] + 1
                                k %= 2
                                act(S, stgf[k][:], PS[pgt][:, 0:T], AF.Exp, [BPS[pgt], B_od], [B_stgf[k]], scale=-1.0, bias=nbias[:, d_, hc:hc + 1])
                                act(S, stgf[k][:], stgf[k][:], AF.Ln, [B_stgf[k]], [B_stgf[k]], scale=1.0, bias=1.0)
                                dma(S, "sync", dstT[hc * 128:(hc + 1) * 128, tsl], stgf[k][:], [B_stgf[k]], ())
                S.barrier()
                S.flush()

        def mixer_even():
            scale = 128.0 ** -0.5
            with contextlib.ExitStack() as st:
                NB = S_ // 128
                qT2 = [sb(st, "a_q%d" % j, [128, S_], BF16) for j in range(2)]
                kT2 = [sb(st, "a_k%d" % j, [128, S_], BF16) for j in range(2)]
                vbr2 = [[sb(st, "a_v%d_%d" % (i, j), [128, NB, 128], BF16) for i in range(3)] for j in range(2)]
                B_q2, B_k2, B_v2 = bufs(2), bufs(2), [bufs(3) for _ in range(2)]
                accn = sb(st, "a_n", [128, S_], F32)
                accd = sb(st, "a_d", [128, S_], F32)
                eb = [sb(st, "a_e%d" % i, [128, 256], BF16) for i in range(3)]
                em = [sb(st, "a_em%d" % i, [128, 256], BF16) for i in range(3)]
                obf = sb(st, "a_o", [128, S_], BF16)
                junk = sb(st, "a_junk", [128, 4], F32)
                B_junk = Buf()
                B_n, B_d, B_o = Buf(), Buf(), Buf()
                B_e, B_em = bufs(3), bufs(3)
                MB = cb[:, C_MB:C_MB + 256]
                SB_ = (0, 1, 6)
                it = 0
                def load_head(h):
                    j = h % 2
                    hs = slice(h * 128, (h + 1) * 128)
                    dma(S, "sync", qT2[j][:], qaT[hs, :], (), [B_q2[j]])
                    dma(S, "sync", kT2[j][:], kaT[hs, :], (), [B_k2[j]])
                    for bi, dil in enumerate((1, 4, 16)):
                        L = S_ // dil
                        nb = L // 128
                        for r in range(dil):
                            dma(S, "sync", vbr2[j][bi][:, r * nb:(r + 1) * nb, :],
                                va[r::dil, hs].rearrange("(b p) e -> p b e", p=128), (), [B_v2[j][bi]])

                load_head(0)
                for h in range(AH):
                    hs = slice(h * 128, (h + 1) * 128)
                    qT, kT, vbr = qT2[h % 2], kT2[h % 2], vbr2[h % 2]
                    B_q, B_k, B_v = B_q2[h % 2], B_k2[h % 2], B_v2[h % 2]
                    if h + 1 < AH:
                        load_head(h + 1)
                    B_nr = {(bi, r): Buf() for bi, dil in enumerate((1, 4, 16)) for r in range(dil)}
                    B_dr = {(bi, r): Buf() for bi, dil in enumerate((1, 4, 16)) for r in range(dil)}
                    memset(S, "gpsimd", accn[:], 0.0, [B_n] + [B_nr[(0, 0)]])
                    memset(S, "gpsimd", accd[:], 0.0, [B_d] + [B_dr[(0, 0)]])
                    blocks = []
                    for bi, dil in enumerate((1, 4, 16)):
                        L = S_ // dil
                        nb = L // 128
                        for b in range(nb):
                            for r in range(dil):
                                blocks.append((bi, dil, L, nb, r, b))

                    def geom(blk):
                        bi, dil, L, nb, r, b = blk
                        j0 = 128 * b
                        qlo, qhi = max(0, j0 - 64), min(L, j0 + 192)
                        return j0, qlo, qhi, qhi - qlo, qlo - (j0 - 64)

                    def issue_scores(x):
                        bi, dil, L, nb, r, b = blocks[x]
                        j0, qlo, qhi, n, flo = geom(blocks[x])
                        kc = kT[:, r + dil * j0:r + dil * (j0 + 127) + 1:dil]
                        qc = qT[:, r + dil * qlo:r + dil * (qhi - 1) + 1:dil]
                        sbk = SB_[(it + x) % 3]
                        mm(S, PS[sbk][:, 0:n], [(kc, qc)], [B_k, B_q], [BPS[sbk]])

                    issue_scores(0)
                    issue_scores(1)
                    for x, blk in enumerate(blocks):
                        bi, dil, L, nb, r, b = blk
                        j0, qlo, qhi, n, flo = geom(blk)
                        k3 = (it + x) % 3
                        sbk = SB_[k3]
                        k = (it + x) % 2
                        if x + 2 < len(blocks):
                            issue_scores(x + 2)
                        act(S, eb[k3][:, 0:n], PS[sbk][:, 0:n], AF.Exp, [BPS[sbk]], [B_e[k3]], scale=scale)
                        tt(S, "gpsimd", em[k3][:, 0:n], eb[k3][:, 0:n], MB[:, flo:flo + n], ALU.mult, [B_e[k3], B_c], [B_em[k3]])
                        mm(S, PS[2 + k][:, 0:n], [(vbr[bi][:, r * nb + b, :], em[k3][:, 0:n])], [B_v[bi], B_em[k3]], [BPS[2 + k]])
                        mm(S, PS[4 + k][:, 0:n], [(ones_b, em[k3][:, 0:n])], [B_c, B_em[k3]], [BPS[4 + k]])
                        cs = slice(r + dil * qlo, r + dil * (qhi - 1) + 1, dil)
                        if x > 0 and blocks[x - 1][0] != bi:
                            pbi = blocks[x - 1][0]
                            pdil = (1, 4, 16)[pbi]
                            S.op("vector", lambda e: e.memset(junk[:, 0:1], 0.0),
                                 [B_nr[(pbi, rr)] for rr in range(pdil)] + [B_dr[(pbi, rr)] for rr in range(pdil)],
                                 [B_nr[(bi, rr)] for rr in range(dil)] + [B_dr[(bi, rr)] for rr in range(dil)] + [B_junk])
                        tt(S, "vector", accn[:, cs], accn[:, cs], PS[2 + k][:, 0:n], ALU.add, [BPS[2 + k]], [B_nr[(bi, r)]])
                        tt(S, "vector", accd[:, cs], accd[:, cs], PS[4 + k][:, 0:n], ALU.add, [BPS[4 + k]], [B_dr[(bi, r)]])
                    it += len(blocks)
                    S.op("vector", lambda e: e.memset(junk[:, 0:1], 0.0),
                         [B_nr[(2, rr)] for rr in range(16)] + [B_dr[(2, rr)] for rr in range(16)], [B_n, B_d, B_junk])
                    S.op("vector", lambda e: e.reciprocal(out=accd[:], in_=accd[:]), [B_d], [B_d])
                    tt(S, "gpsimd", obf[:], accn[:], accd[:], ALU.mult, [B_n, B_d], [B_o])
                    dma(S, "sync", mixT[hs, :], obf[:], [B_o], ())
                S.barrier()
                S.flush()

        def mixer_ret():
            lns = math.log(128.0 ** -0.5)
            NGR = S_ // 512
            with contextlib.ExitStack() as st:
                PSb = [PS[i][:].bitcast(BF16) for i in range(8)]

                class Slot:
                    pass
                slots = []
                for si in range(2):
                    L = Slot()
                    n = "b%d_" % si
                    L.qT = sb(st, n + "q", [128, S_], BF16)
                    L.kT = sb(st, n + "k", [128, S_], BF16)
                    L.gT = sb(st, n + "g", [128, S_], BF16)
                    L.v = sb(st, n + "v", [128, NCH, 128], BF16)
                    L.SfA = sb(st, n + "sf", [128, NCH, 128], BF16)
                    L.SbA = sb(st, n + "sb", [128, NCH, 128], BF16)
                    L.Sf2 = [sb(st, n + "sfr%d" % i, [128, 128], F32) for i in range(2)]
                    L.kdall = [sb(st, n + "kdall%d" % i, [128, NCH, 128], BF16) for i in range(2)]
                    L.DT = sb(st, n + "dt", [128, 128], F32)
                    L.DT2 = sb(st, n + "dt2", [128, 128], F32)
                    L.qdf = sb(st, n + "qdf", [128, 512], F32)
                    L.qdb = sb(st, n + "qdb", [128, 512], F32)
                    L.kd = sb(st, n + "kd", [128, 4], F32)
                    L.qf = [sb(st, n + "qf%d" % i, [128, 512], BF16) for i in range(2)]
                    L.qb_ = [sb(st, n + "qb%d" % i, [128, 512], BF16) for i in range(2)]
                    L.A = [sb(st, n + "A%d" % i, [128, 128], BF16) for i in range(2)]
                    L.o = sb(st, n + "o", [128, 512], F32)
                    L.o2 = sb(st, n + "o2", [128, 512], F32)
                    L.msq = sb(st, n + "msq", [128, 512], F32)
                    L.var = sb(st, n + "var", [128, 512], F32)
                    L.res = sb(st, n + "res", [128, 512], F32)
                    L.ob = [sb(st, n + "ob%d" % i, [128, 512], BF16) for i in range(2)]
                    (L.B_q, L.B_k, L.B_g, L.B_v, L.B_sfa, L.B_sba, L.B_hc, L.B_o, L.B_o2, L.B_msq, L.B_var, L.B_res) = (Buf() for _ in range(12))
                    L.B_sf2, L.B_kdall, L.B_A, L.B_ob, L.B_qf, L.B_qb = bufs(2), bufs(2), bufs(2), bufs(2), bufs(2), bufs(2)
                    L.pt, L.po = si, (2 + 2 * si, 3 + 2 * si)
                    L.pk = L.po[0]
                    slots.append(L)

                def head_gen(h, L):
                    hs = slice(h * 128, (h + 1) * 128)
                    lgf, lgb = lg[:, h:h + 1], lg[:, BH + h:BH + h + 1]
                    dma(S, "sync", L.qT[:], qbT[hs, :], (), [L.B_q])
                    dma(S, "sync", L.kT[:], kbT[hs, :], (), [L.B_k])
                    dma(S, "sync", L.gT[:], gbT[hs, :], (), [L.B_g])
                    dma(S, "sync", L.v[:], vb[:, hs].rearrange("(b p) e -> p b e", p=128), (), [L.B_v])
                    yield
                    act(S, L.DT[:], cf[:, C_POS:C_POS + 128], AF.Exp, [B_c], [L.B_hc], scale=lgf, bias=lns)
                    act(S, L.DT2[:], cf[:, C_NEG:C_NEG + 128], AF.Exp, [B_c], [L.B_hc], scale=lgb)
                    tt(S, "vector", L.DT[:], L.DT[:], L.DT2[:], ALU.mult, [L.B_hc], [L.B_hc])
                    act(S, L.qdf[:], cf[:, C_IDX:C_IDX + 512], AF.Exp, [B_c], [L.B_hc], scale=lgf, bias=lns)
                    act(S, L.qdb[:], cf[:, C_CMA:C_CMA + 512], AF.Exp, [B_c], [L.B_hc], scale=lgb, bias=lns)
                    act(S, L.kd[:, 0:1], cf[:, C_COLF:C_COLF + 1], AF.Exp, [B_c], [L.B_hc], scale=lgf)
                    act(S, L.kd[:, 1:2], cf[:, C_COLB:C_COLB + 1], AF.Exp, [B_c], [L.B_hc], scale=lgb)
                    act(S, L.kd[:, 2:3], cf[:, C_CC:C_CC + 1], AF.Exp, [B_c], [L.B_hc], scale=lgf)
                    act(S, L.kd[:, 3:4], cf[:, C_CC:C_CC + 1], AF.Exp, [B_c], [L.B_hc], scale=lgb)
                    yield
                    for direction in (0, 1):
                        SA, B_sa = (L.SfA, L.B_sfa) if direction == 0 else (L.SbA, L.B_sba)
                        first = 0 if direction == 0 else NCH - 1
                        for g8 in range(NCH // 8):
                            for j in range(8):
                                i = g8 * 8 + j
                                tr(S, PSb[L.pt][:, j * 128:(j + 1) * 128], L.kT[:, i * 128:(i + 1) * 128], ident_b, [L.B_k, B_c], [BPS[L.pt]])
                            ts(S, "vector", L.kdall[direction][:, g8 * 8:(g8 + 1) * 8, :], PSb[L.pt][:, 0:1024].rearrange("p (c d) -> p c d", d=128),
                               L.kd[:, direction:direction + 1], ALU.mult, [BPS[L.pt], L.B_hc], [L.B_kdall[direction]])
                            yield
                        memset(S, "vector", L.Sf2[0][:], 0.0, [L.B_sf2[0]])
                        memset(S, "gpsimd", SA[:, first, :], 0.0, [B_sa])
                        order = range(0, NCH - 1) if direction == 0 else range(NCH - 1, 0, -1)
                        for s_, i in enumerate(order):
                            mm(S, PS[L.pk][:, 0:128], [(L.kdall[direction][:, i, :], L.v[:, i, :])], [L.B_kdall[direction], L.B_v], [BPS[L.pk]])
                            stt(S, L.Sf2[(s_ + 1) % 2][:], L.Sf2[s_ % 2][:], L.kd[:, 2 + direction:3 + direction], PS[L.pk][:, 0:128], ALU.mult, ALU.add,
                                [BPS[L.pk], L.B_hc, L.B_sf2[s_ % 2]], [L.B_sf2[(s_ + 1) % 2]])
                            nxt = i + 1 if direction == 0 else i - 1
                            cp(S, "gpsimd", SA[:, nxt, :], L.Sf2[(s_ + 1) % 2][:], [L.B_sf2[(s_ + 1) % 2]], [B_sa])
                            yield

                    def front(g):
                        gs = slice(g * 512, (g + 1) * 512)
                        q2 = g % 2
                        tt(S, "gpsimd", L.qf[q2][:], L.qT[:, gs], L.qdf[:], ALU.mult, [L.B_q, L.B_hc], [L.B_qf[q2]])
                        tt(S, "gpsimd", L.qb_[q2][:], L.qT[:, gs], L.qdb[:], ALU.mult, [L.B_q, L.B_hc], [L.B_qb[q2]])
                        po = L.po[g % 2]
                        for ci in range(4):
                            i = g * 4 + ci
                            cs = slice(i * 128, (i + 1) * 128)
                            k = i % 2
                            mm(S, PS[L.pt][:, 0:128], [(L.kT[:, cs], L.qT[:, cs])], [L.B_k, L.B_q], [BPS[L.pt]])
                            tt(S, "vector", L.A[k][:], PS[L.pt][:, 0:128], L.DT[:], ALU.mult, [BPS[L.pt], L.B_hc], [L.B_A[k]])
                            mm(S, PS[po][:, ci * 128:(ci + 1) * 128],
                               [(L.SfA[:, i, :], L.qf[q2][:, ci * 128:(ci + 1) * 128]), (L.SbA[:, i, :], L.qb_[q2][:, ci * 128:(ci + 1) * 128])],
                               [L.B_sfa, L.B_sba, L.B_qf[q2], L.B_qb[q2]], [BPS[po]], start=True, stop=False)
                            mm(S, PS[po][:, ci * 128:(ci + 1) * 128], [(L.v[:, i, :], L.A[k][:])], [L.B_v, L.B_A[k]], [BPS[po]],
                               start=False, stop=True)
                            yield

                    def post(g):
                        gs = slice(g * 512, (g + 1) * 512)
                        po = L.po[g % 2]
                        cp(S, "scalar", L.o[:], PS[po][:], [BPS[po]], [L.B_o])
                        act(S, L.o2[:], PS[po][:], AF.Square, [BPS[po]], [L.B_o2])
                        yield
                        mm(S, PS[6][:], [(mean_f, L.o[:])], [L.B_o, B_c], [BPS[6]])
                        mm(S, PS[7][:], [(mean_f, L.o2[:])], [L.B_o2, B_c], [BPS[7]])
                        act(S, L.msq[:], PS[6][:], AF.Square, [BPS[6]], [L.B_msq])
                        tt(S, "vector", L.var[:], PS[7][:], L.msq[:], ALU.subtract, [BPS[7], L.B_msq], [L.B_var])
                        tt(S, "vector", L.res[:], L.o[:], PS[6][:], ALU.subtract, [L.B_o, BPS[6]], [L.B_res])
                        yield
                        act(S, L.var[:], L.var[:], AF.Ln, [L.B_var], [L.B_var], scale=1.0, bias=EPS)
                        act(S, L.var[:], L.var[:], AF.Exp, [L.B_var], [L.B_var], scale=-0.5)
                        tt(S, "gpsimd", L.res[:], L.res[:], L.var[:], ALU.mult, [L.B_res, L.B_var], [L.B_res])
                        kk = g % 2
                        tt(S, "gpsimd", L.ob[kk][:], L.res[:], L.gT[:, gs], ALU.mult, [L.B_res, L.B_g], [L.B_ob[kk]])
                        dma(S, "sync", mixT[AH * 128 + h * 128:AH * 128 + (h + 1) * 128, gs], L.ob[kk][:], [L.B_ob[kk]], ())
                        yield

                    yield from front(0)
                    for g in range(NGR):
                        if g + 1 < NGR:
                            yield from front(g + 1)
                        yield from post(g)

                for h0 in range(0, BH, 2):
                    gens = [head_gen(h0 + j, slots[j]) for j in range(min(2, BH - h0))]
                    while gens:
                        for g_ in list(gens):
                            try:
                                next(g_)
                            except StopIteration:
                                gens.remove(g_)
                S.barrier()
                S.flush()

        def mixer_gla():
            sc = 128.0 ** -0.5
            with contextlib.ExitStack() as st:
                NG = S_ // 512
                v = sb(st, "c_v", [128, NCH, 256], BF16)
                qF = sb(st, "c_qF", [128, S_], BF16)
                kF = sb(st, "c_kF", [128, S_], BF16)
                qB = sb(st, "c_qB", [128, S_], BF16)
                qB2 = sb(st, "c_qB2", [128, S_], BF16)
                kB = sb(st, "c_kB", [128, S_], BF16)
                eFl = sb(st, "c_eFl", [128, NCH], F32)
                eEt = sb(st, "c_eEt", [128, NCH], F32)
                SfA = sb(st, "c_sf", [128, NCH, 256], BF16)
                SbA = sb(st, "c_sb", [128, NCH, 256], BF16)
                Sr2 = [sb(st, "c_sr%d" % i, [128, 256], F32) for i in range(2)]
                kve = [sb(st, "c_kve%d" % i, [128, 256], F32) for i in range(2)]
                kdall = [sb(st, "c_kdall%d" % i, [128, NCH, 128], BF16) for i in range(2)]
                B_sr2, B_kve, B_kdall = bufs(2), bufs(2), bufs(2)
                qt = [sb(st, "c_qt%d" % i, [128, 512], BF16) for i in range(2)]
                kt = [sb(st, "c_kt%d" % i, [128, 512], BF16) for i in range(2)]
                lt = [sb(st, "c_lt%d" % i, [128, 2, 512], F32) for i in range(2)]
                Lc = sb(st, "c_Lc", [128, 2, 512], F32)
                X1 = sb(st, "c_X1", [128, 512], F32)
                X2 = sb(st, "c_X2", [128, 512], F32)
                X3 = sb(st, "c_X3", [128, 512], F32)
                kdec = [sb(st, "c_kdec%d" % i, [128, 128], BF16) for i in range(2)]
                A2 = [sb(st, "c_A%d" % i, [128, 256], BF16) for i in range(2)]
                rt = sb(st, "c_rt", [128, 2, 512], BF16)
                o = sb(st, "c_o", [128, 2, 512], F32)
                o2 = sb(st, "c_o2", [128, 512], F32)
                rs = sb(st, "c_rs", [128, 512], F32)
                ob = [sb(st, "c_ob%d" % i, [128, 512], BF16) for i in range(2)]
                gn = sb(st, "c_gn", [128, 2], F32)
                B_v, B_qF, B_kF, B_qB, B_qB2, B_kB, B_eFl, B_eEt = (Buf() for _ in range(8))
                B_sfa, B_sba, B_sr, B_st, B_L, B_X1, B_X2, B_X3, B_rt, B_o, B_o2, B_rs, B_gn = (Buf() for _ in range(13))
                B_qt, B_kt, B_lt, B_kdec, B_A2, B_ob = bufs(2), bufs(2), bufs(2), bufs(2), bufs(2), bufs(2)
                PSb = [PS[i][:].bitcast(BF16) for i in range(8)]
                MG = cb[:, C_MG:C_MG + 256]
                dma(S, "sync", gn[:], gla_g[0].rearrange("(c p) -> p c", p=128), (), [B_gn], slow=True)
                it = 0
                for hc in range(CH):
                    hs = slice(hc * 128, (hc + 1) * 128)
                    dma(S, "sync", v[:], vc[:, hc * 256:(hc + 1) * 256].rearrange("(b p) e -> p b e", p=128), (), [B_v])
                    for g in range(NG):
                        gs = slice(g * 512, (g + 1) * 512)
                        k = g % 2
                        dma(S, "sync", qt[k][:], qcT[hs, gs], (), [B_qt[k]])
                        dma(S, "sync", kt[k][:], kcT[hs, gs], (), [B_kt[k]])
                        dma(S, "sync", lt[k][:, 0, :], lfT[hs, gs], (), [B_lt[k]])
                        dma(S, "sync", lt[k][:, 1, :], lbT[hs, gs], (), [B_lt[k]])
                        for d_ in range(2):
                            for ci in range(4):
                                cs = slice(ci * 128, (ci + 1) * 128)
                                S.op("vector", (lambda e, d_=d_, cs=cs, k=k: e.tensor_tensor_scan(
                                    out=Lc[:, d_, cs], data0=cf[:, C_ONE:C_ONE + 128], data1=lt[k][:, d_, cs], initial=0.0,
                                    op0=ALU.mult, op1=ALU.add)), [B_lt[k], B_c], [B_L])
                        act(S, X1[:], Lc[:, 0, :], AF.Exp, [B_L], [B_X1], scale=-1.0 / 16)
                        cp(S, "gpsimd", eFl[:, g * 4:(g + 1) * 4], X1[:, 127::128], [B_X1], [B_eFl])
                        stt(S, qF[:, gs], qt[k][:], sc, X1[:], ALU.mult, ALU.mult, [B_qt[k], B_X1], [B_qF])
                        act(S, X2[:], Lc[:, 0, :], AF.Exp, [B_L], [B_X2], scale=1.0 / 16)
                        tt(S, "gpsimd", kF[:, gs], kt[k][:], X2[:], ALU.mult, [B_kt[k], B_X2], [B_kF])
                        act(S, eEt[:, g * 4:(g + 1) * 4], Lc[:, 1, 127::128], AF.Exp, [B_L], [B_eEt], scale=-1.0 / 16)
                        tt(S, "gpsimd", X3[:], Lc[:, 1, :], lt[k][:, 1, :], ALU.subtract, [B_L, B_lt[k]], [B_X3])
                        act(S, X1[:], X3[:], AF.Exp, [B_X3], [B_X1], scale=-1.0 / 16)
                        tt(S, "gpsimd", kB[:, gs], kt[k][:], X1[:], ALU.mult, [B_kt[k], B_X1], [B_kB])
                        act(S, X2[:], X3[:], AF.Exp, [B_X3], [B_X2], scale=1.0 / 16)
                        stt(S, qB[:, gs], qt[k][:], sc, X2[:], ALU.mult, ALU.mult, [B_qt[k], B_X2], [B_qB])
                        for ci in range(4):
                            cs = slice(ci * 128, (ci + 1) * 128)
                            ts(S, "vector", X3[:, cs], X3[:, cs], -1.0, ALU.mult, [B_X3, B_L], [B_X3],
                               s2=Lc[:, 1, ci * 128 + 127:ci * 128 + 128], op1=ALU.add)
                        act(S, X1[:], X3[:], AF.Exp, [B_X3], [B_X1], scale=-1.0 / 16)
                        stt(S, qB2[:, gs], qt[k][:], sc, X1[:], ALU.mult, ALU.mult, [B_qt[k], B_X1], [B_qB2])
                    for direction in (0, 1):
                        SA, B_sa = (SfA, B_sfa) if direction == 0 else (SbA, B_sba)
                        KX, B_kx = (kF, B_kF) if direction == 0 else (kB, B_kB)
                        first = 0 if direction == 0 else NCH - 1
                        for g8 in range(NCH // 8):
                            k = it % 2
                            it += 1
                            for j in range(8):
                                i = g8 * 8 + j
                                tr(S, PSb[k][:, j * 128:(j + 1) * 128], KX[:, i * 128:(i + 1) * 128], ident_b, [B_kx, B_c], [BPS[k]])
                            cp(S, "scalar", kdall[direction][:, g8 * 8:(g8 + 1) * 8, :], PSb[k][:, 0:1024].rearrange("p (c d) -> p c d", d=128),
                               [BPS[k]], [B_kdall[direction]])
                        memset(S, "vector", Sr2[0][:], 0.0, [B_sr2[0]])
                        memset(S, "gpsimd", SA[:, first, :], 0.0, [B_sa])
                        order = range(0, NCH - 1) if direction == 0 else range(NCH - 1, 0, -1)
                        for s_, i in enumerate(order):
                            k = it % 2
                            it += 1
                            a_, b_ = s_ % 2, (s_ + 1) % 2
                            mm(S, PS[2 + k][:, 0:256], [(kdall[direction][:, i, :], v[:, i, :])], [B_kdall[direction], B_v], [BPS[2 + k]])
                            if direction == 0:
                                act(S, kve[k][:], PS[2 + k][:, 0:256], AF.Copy, [BPS[2 + k], B_eFl], [B_kve[k]], scale=eFl[:, i:i + 1])
                                stt(S, Sr2[b_][:], Sr2[a_][:], eFl[:, i:i + 1], kve[k][:], ALU.mult, ALU.add, [B_sr2[a_], B_eFl, B_kve[k]], [B_sr2[b_]])
                                nxt = i + 1
                            else:
                                stt(S, Sr2[b_][:], Sr2[a_][:], eEt[:, i:i + 1], PS[2 + k][:, 0:256], ALU.mult, ALU.add,
                                    [B_sr2[a_], BPS[2 + k], B_eEt], [B_sr2[b_]])
                                nxt = i - 1
                            cp(S, "gpsimd", SA[:, nxt, :], Sr2[b_][:], [B_sr2[b_]], [B_sa])
                    def front(g):
                        nonlocal it
                        for ci in range(4):
                            i = g * 4 + ci
                            cs = slice(i * 128, (i + 1) * 128)
                            k = it % 2
                            it += 1
                            mm(S, PS[k][:, 0:128], [(kF[:, cs], qF[:, cs])], [B_kF, B_qF], [BPS[k]])
                            mm(S, PS[k][:, 128:256], [(kB[:, cs], qB[:, cs])], [B_kB, B_qB], [BPS[k]])
                            tt(S, "vector", A2[k][:], PS[k][:, 0:256], MG, ALU.mult, [BPS[k], B_c], [B_A2[k]])
                            for ec in range(2):
                                es = slice(ec * 128, (ec + 1) * 128)
                                pb = 2 + 2 * (g % 2) + ec
                                mm(S, PS[pb][:, ci * 128:(ci + 1) * 128],
                                   [(SfA[:, i, es], qF[:, cs]), (SbA[:, i, es], qB2[:, cs])],
                                   [B_sfa, B_sba, B_qF, B_qB2], [BPS[pb]], start=True, stop=False)
                            for ec in range(2):
                                es = slice(ec * 128, (ec + 1) * 128)
                                pb = 2 + 2 * (g % 2) + ec
                                mm(S, PS[pb][:, ci * 128:(ci + 1) * 128],
                                   [(v[:, i, es], A2[k][:, 0:128]), (v[:, i, es], A2[k][:, 128:256])],
                                   [B_v, B_A2[k]], [BPS[pb]], start=False, stop=True)

                    def post(g):
                        gs = slice(g * 512, (g + 1) * 512)
                        dma(S, "sync", rt[:], rcT[hc * 256:(hc + 1) * 256, gs].rearrange("(c p) t -> p c t", p=128), (), [B_rt])
                        for ec in range(2):
                            pb = 2 + 2 * (g % 2) + ec
                            cp(S, "scalar", o[:, ec, :], PS[pb][:], [BPS[pb]], [B_o])
                            act(S, o2[:], PS[pb][:], AF.Square, [BPS[pb]], [B_o2])
                            mm(S, PS[6][:], [(ones_f, o2[:])], [B_o2, B_c], [BPS[6]], start=(ec == 0), stop=(ec == 1))
                        act(S, rs[:], PS[6][:], AF.Ln, [BPS[6]], [B_rs], scale=1.0 / 256, bias=EPS)
                        act(S, rs[:], rs[:], AF.Exp, [B_rs], [B_rs], scale=-0.5)
                        for ec in range(2):
                            stt(S, o[:, ec, :], o[:, ec, :], gn[:, ec:ec + 1], rs[:], ALU.mult, ALU.mult, [B_rs, B_gn], [B_o])
                            kk = (g * 2 + ec) % 2
                            tt(S, "gpsimd", ob[kk][:], o[:, ec, :], rt[:, ec, :], ALU.mult, [B_o, B_rt], [B_ob[kk]])
                            dma(S, "sync", mixT[hc * 256 + ec * 128:hc * 256 + (ec + 1) * 128, gs], ob[kk][:], [B_ob[kk]], ())

                    front(0)
                    for g in range(NG):
                        if g + 1 < NG:
                            front(g + 1)
                        post(g)
                S.barrier()
                S.flush()

        def mixer_mla():
            scale = 192.0 ** -0.5
            with contextlib.ExitStack() as st:
                NG = S_ // 512
                kr = sb(st, "d_kr", [64, S_], BF16)
                qn2 = [sb(st, "d_qn%d" % j, [128, S_], BF16) for j in range(2)]
                kn2 = [sb(st, "d_kn%d" % j, [128, S_], BF16) for j in range(2)]
                qr2 = [sb(st, "d_qr%d" % j, [64, S_], BF16) for j in range(2)]
                v2 = [sb(st, "d_v%d" % j, [128, NCH, 128], BF16) for j in range(2)]
                B_qn2, B_kn2, B_qr2, B_v2 = bufs(2), bufs(2), bufs(2), bufs(2)
                NE = 6
                E = [sb(st, "d_E%d" % i, [128, 512], BF16) for i in range(NE)]
                rd = sb(st, "d_rd", [128, 512], F32)
                ob = [sb(st, "d_ob%d" % i, [128, 512], BF16) for i in range(2)]
                B_kr, B_rd = Buf(), Buf()
                B_E, B_ob = bufs(NE), bufs(2)
                dma(S, "sync", kr[:], krT[:, :], (), [B_kr])
                it = 0
                def load_head(h):
                    j = h % 2
                    hs = slice(h * 128, (h + 1) * 128)
                    dma(S, "sync", qn2[j][:], qnT[hs, :], (), [B_qn2[j]])
                    dma(S, "sync", kn2[j][:], knT[hs, :], (), [B_kn2[j]])
                    dma(S, "sync", qr2[j][:], qrT[h * 64:(h + 1) * 64, :], (), [B_qr2[j]])
                    dma(S, "sync", v2[j][:], vm[:, hs].rearrange("(b p) e -> p b e", p=128), (), [B_v2[j]])

                load_head(0)
                for h in range(DH):
                    hs = slice(h * 128, (h + 1) * 128)
                    qn, kn, qr, v = qn2[h % 2], kn2[h % 2], qr2[h % 2], v2[h % 2]
                    B_qn, B_kn, B_qr, B_v = B_qn2[h % 2], B_kn2[h % 2], B_qr2[h % 2], B_v2[h % 2]
                    if h + 1 < DH:
                        load_head(h + 1)
                    blocks = [(g, kb_) for g in range(NG) for kb_ in range(NCH)]

                    def issue_scores(bi_):
                        g, kb_ = blocks[bi_]
                        gs = slice(g * 512, (g + 1) * 512)
                        ks = slice(kb_ * 128, (kb_ + 1) * 128)
                        sbk = (it0 + bi_) % 4
                        mm(S, PS[sbk][:], [(kn[:, ks], qn[:, gs]), (kr[:, ks], qr[:, gs])], [B_kn, B_qn, B_kr, B_qr], [BPS[sbk]])

                    it0 = it
                    issue_scores(0)
                    issue_scores(1)
                    for bi_, (g, kb_) in enumerate(blocks):
                        gs = slice(g * 512, (g + 1) * 512)
                        po, pd = 4 + g % 2, 6 + g % 2
                        sbk = (it0 + bi_) % 4
                        k = (it0 + bi_) % NE
                        if bi_ + 2 < len(blocks):
                            issue_scores(bi_ + 2)
                        act(S, E[k][:], PS[sbk][:], AF.Exp, [BPS[sbk]], [B_E[k]], scale=scale)
                        mm(S, PS[po][:], [(v[:, kb_, :], E[k][:])], [B_v, B_E[k]], [BPS[po]], start=(kb_ == 0), stop=(kb_ == NCH - 1))
                        mm(S, PS[pd][:], [(ones_b, E[k][:])], [B_c, B_E[k]], [BPS[pd]], start=(kb_ == 0), stop=(kb_ == NCH - 1))
                        if kb_ == NCH - 1:
                            S.op("vector", (lambda e, pd=pd: e.reciprocal(out=rd[:], in_=PS[pd][:])), [BPS[pd]], [B_rd])
                            kk = g % 2
                            tt(S, "vector", ob[kk][:], PS[po][:], rd[:], ALU.mult, [BPS[po], B_rd], [B_ob[kk]])
                            dma(S, "sync", mixT[CH * 256 + h * 128:CH * 256 + (h + 1) * 128, gs], ob[kk][:], [B_ob[kk]], ())
                    it += len(blocks)
                S.barrier()
                S.flush()

        S.barrier()
        S.flush()
        stages = [lambda: token_phase(0), mixer_even, mixer_ret, lambda: token_phase(1), mixer_gla, mixer_mla, lambda: token_phase(2)]
        for f in stages[:cfg.get("stages", 7)]:
            f()
        S.finish()
    return nc


def _consts(S_):
    c = np.zeros((128, NCOLS), np.float32)
    p = np.arange(128)[:, None].astype(np.float32)
    f = np.arange(128)[None, :].astype(np.float32)
    c[:, C_ID:C_ID + 128] = np.eye(128, dtype=np.float32)
    c[:, C_ONE:C_ONE + 128] = 1.0
    c[:, C_MEAN:C_MEAN + 128] = 1.0 / 128
    c[:, C_POS:C_POS + 128] = np.maximum(f - p, 0)
    c[:, C_NEG:C_NEG + 128] = np.maximum(p - f, 0)
    a4 = np.tile(np.arange(128, dtype=np.float32), 4)[None, :]
    c[:, C_IDX:C_IDX + 512] = a4 + 1.0
    c[:, C_CMA:C_CMA + 512] = 128.0 - a4
    c[:, C_COLF] = 127.0 - p[:, 0]
    c[:, C_COLB] = p[:, 0]
    c[:, C_CC] = 128.0

    def perm(n, half, rot):
        m = np.zeros((128, 128), np.float32)
        for i in range(half):
            m[i, i + half] = 1.0
            m[i + half, i] = 1.0
        return m
    c[:, C_PA:C_PA + 128] = perm(128, 16, 32)
    c[:, C_PB:C_PB + 128] = perm(128, 64, 128)
    c[:, C_PM:C_PM + 128] = perm(128, 32, 64)
    f2 = np.arange(256)[None, :].astype(np.float32)
    c[:, C_MB:C_MB + 256] = ((f2 - p >= 0) & (f2 - p <= 128)).astype(np.float32)
    c[:, C_MG:C_MG + 128] = (p <= f).astype(np.float32)
    c[:, C_MG + 128:C_MG + 256] = (p > f).astype(np.float32)

    def rope_tab(theta, rot, rows):
        half = rot // 2
        inv = np.power(np.float32(theta), -np.arange(half, dtype=np.float32) * np.float32(2.0 / rot)).astype(np.float32)
        pos = np.arange(S_, dtype=np.float32)
        ang = (pos[:, None] * inv[None, :]).astype(np.float32)
        cs, sn = np.cos(ang).astype(np.float32), np.sin(ang).astype(np.float32)
        C = np.ones((rows, S_), np.float32)
        Sn = np.zeros((rows, S_), np.float32)
        C[0:half] = cs.T
        C[half:rot] = cs.T
        Sn[0:half] = -sn.T
        Sn[half:rot] = sn.T
        return np.ascontiguousarray(np.stack([C, Sn]))
    return c, rope_tab(500000.0, 32, 128), rope_tab(10000.0, 128, 128), rope_tab(10000.0, 64, 64)


_NC_CACHE = {}


def run_cfg(cfg, seqs, weights, n_cores):
    key = tuple(sorted(cfg.items()))
    if key not in _NC_CACHE:
        _NC_CACHE[key] = build(cfg)
    nc = _NC_CACHE[key]
    cst, rA, rB, rM = _consts(cfg["S"])
    base = {k: np.ascontiguousarray(np.asarray(v, dtype=np.float32)) for k, v in weights.items()}
    base.update(cst=cst, ropeA=rA, ropeB=rB, ropeM=rM)
    in_maps = []
    for i in range(n_cores):
        m = dict(base)
        m["x"] = np.ascontiguousarray(seqs[i % len(seqs)])
        in_maps.append(m)
    res = run_bass_kernel_spmd(nc, in_maps, core_ids=list(range(n_cores)))
    return [res.results[i]["y"] for i in range(len(seqs))]


def kernel(x_prompt, x_sample, **weights):
    xp = np.asarray(x_prompt, dtype=np.float32)
    xs = np.asarray(x_sample, dtype=np.float32)
    seqs = [xp[i] for i in range(xp.shape[0])] + [xs[i] for i in range(xs.shape[0])]
    outs = run_cfg(FULL_CFG, seqs, weights, 8)
    yp = np.stack(outs[:xp.shape[0]]).astype(np.float32)
    ys = np.stack(outs[xp.shape[0]:]).astype(np.float32)
    return (yp, ys)
```

```python
import contextlib
import math
import numpy as np
import concourse.bass as bass
import concourse.mybir as mybir
from concourse.bass_utils import run_bass_kernel_spmd

F32 = mybir.dt.float32
BF16 = mybir.dt.bfloat16
AF = mybir.ActivationFunctionType
ALU = mybir.AluOpType

SAME_ENGINE_SYNC = True
N_DMA_SEMS = 8
EPS = 1e-6

FULL_CFG = dict(S=4096, D=2048, FF=5632, AH=8, BH=8, CH=4, DH=8, QR=448, KVR=160, T=512)

C_ID, C_ONE, C_MEAN, C_POS, C_NEG, C_IDX, C_CMA = 0, 128, 256, 384, 512, 640, 1152
C_COLF, C_COLB, C_CC = 1664, 1665, 1666
C_PA, C_PB, C_PM, C_MB, C_MG = 1668, 1796, 1924, 2052, 2308
NCOLS = 2564


class Buf:
    __slots__ = ("w", "r", "excl")

    def __init__(self, excl=False):
        self.w = None
        self.r = []
        self.excl = excl


def bufs(n):
    return [Buf() for _ in range(n)]


class Sched:
    ENGS = ("tensor", "vector", "scalar", "gpsimd", "sync")
    QUEUES = ("sync", "gpsimd")

    def __init__(self, nc, stack):
        self.nc = nc
        self.lists = {e: [] for e in self.ENGS}
        self.sem = {e: stack.enter_context(nc.semaphore("s_" + e)) for e in self.ENGS}
        self.count = {e: 0 for e in self.ENGS}
        self.waited = {e: {} for e in self.ENGS}
        self.dsem, self.dval, self.dnext = {}, {}, {}
        for q in self.QUEUES:
            self.dsem[q] = [stack.enter_context(nc.semaphore("d_%s%d" % (q, i))) for i in range(N_DMA_SEMS)]
            self.dval[q] = [0] * N_DMA_SEMS
            self.dnext[q] = 0
        self.n_ops = 0

    def _wait(self, eng, tok):
        key, val, teng, sem = tok
        if teng == eng and (eng == "tensor" or not SAME_ENGINE_SYNC):
            return
        if self.waited[eng].get(key, 0) >= val:
            return
        self.waited[eng][key] = val
        self.lists[eng].append(("wait", sem, val))

    def _deps(self, eng, reads, writes):
        for b in reads:
            if b.w is not None:
                self._wait(eng, b.w)
            if b.excl:
                for t in b.r:
                    if t[2] != eng:
                        self._wait(eng, t)
        for b in writes:
            if b.w is not None:
                self._wait(eng, b.w)
            for t in b.r:
                self._wait(eng, t)

    @staticmethod
    def _commit(tok, reads, writes):
        for b in reads:
            b.r.append(tok)
        for b in writes:
            b.w = tok
            b.r = []

    def op(self, eng, fn, reads=(), writes=()):
        self._deps(eng, reads, writes)
        self.count[eng] += 1
        tok = ("e_" + eng, self.count[eng], eng, self.sem[eng])
        self.lists[eng].append(("inst", fn, self.sem[eng], 1))
        self._commit(tok, reads, writes)
        self.n_ops += 1
        return tok

    def dma(self, q, fn, reads=(), writes=()):
        i = self.dnext[q]
        self.dnext[q] = (i + 1) % N_DMA_SEMS
        sem = self.dsem[q][i]
        key = "d_%s%d" % (q, i)
        if self.dval[q][i] > 0:
            self._wait(q, (key, self.dval[q][i], "dma", sem))
        self._deps(q, reads, writes)
        self.dval[q][i] += 16
        tok = (key, self.dval[q][i], "dma", sem)
        self.lists[q].append(("inst", fn, sem, 16))
        self._commit(tok, reads, writes)
        self.n_ops += 1
        return tok

    def barrier(self):
        for e in self.ENGS:
            for x in self.ENGS:
                if x != e and self.count[x] > 0:
                    self._wait(e, ("e_" + x, self.count[x], x, self.sem[x]))
            for q in self.QUEUES:
                for i in range(N_DMA_SEMS):
                    if self.dval[q][i] > 0:
                        self._wait(e, ("d_%s%d" % (q, i), self.dval[q][i], "dma", self.dsem[q][i]))

    def flush(self):
        lists = self.lists
        self.lists = {e: [] for e in self.ENGS}

        def run(e, name):
            for ent in lists[name]:
                if ent[0] == "wait":
                    e.wait_ge(ent[1], ent[2])
                else:
                    ent[1](e).then_inc(ent[2], ent[3])

        with self.nc.Block() as block:
            @block.tensor
            def _(e):
                run(e, "tensor")

            @block.vector
            def _(e):
                run(e, "vector")

            @block.scalar
            def _(e):
                run(e, "scalar")

            @block.gpsimd
            def _(e):
                run(e, "gpsimd")

            @block.sync
            def _(e):
                run(e, "sync")

    def finish(self):
        self.barrier()
        self.flush()


def act(S, out, in_, func, r, w, scale=None, bias=None):
    kw = {}
    if scale is not None:
        kw["scale"] = scale
    if bias is not None:
        kw["bias"] = bias
    return S.op("scalar", lambda e: e.activation(out=out, in_=in_, func=func, **kw), r, w)


def tt(S, eng, out, a, b, op, r, w):
    return S.op(eng, lambda e: e.tensor_tensor(out=out, in0=a, in1=b, op=op), r, w)


def ts(S, eng, out, a, s1, op0, r, w, s2=None, op1=None):
    if op1 is None:
        return S.op(eng, lambda e: e.tensor_scalar(out=out, in0=a, scalar1=s1, scalar2=None, op0=op0), r, w)
    return S.op(eng, lambda e: e.tensor_scalar(out=out, in0=a, scalar1=s1, scalar2=s2, op0=op0, op1=op1), r, w)


def stt(S, out, a, scalar, b, op0, op1, r, w):
    return S.op("vector", lambda e: e.scalar_tensor_tensor(out=out, in0=a, scalar=scalar, in1=b, op0=op0, op1=op1), r, w)


def cp(S, eng, out, in_, r, w):
    if eng == "scalar":
        return S.op("scalar", lambda e: e.activation(out=out, in_=in_, func=AF.Copy), r, w)
    return S.op(eng, lambda e: e.tensor_copy(out=out, in_=in_), r, w)


def mm(S, out, pairs, r, w, start=True, stop=True):
    pairs = list(pairs)

    def fn(e):
        n = len(pairs)
        inst = None
        for i, (l, rr) in enumerate(pairs):
            inst = e.matmul(out, l, rr, start=(start and i == 0), stop=(stop and i == n - 1))
        return inst

    return S.op("tensor", fn, r, w)


def tr(S, out, in_, ident, r, w):
    return S.op("tensor", lambda e: e.transpose(out, in_, ident), r, w)


def dma(S, q, out, in_, r, w, slow=False):
    if slow:
        return S.dma(q, lambda e: e.dma_start(out=out, in_=in_, allow_slow_non_contiguous=True), r, w)
    return S.dma(q, lambda e: e.dma_start(out=out, in_=in_), r, w)


def memset(S, eng, ap, val, w):
    return S.op(eng, lambda e: e.memset(ap, val), (), w)


def rows_of(n):
    out = []
    o = 0
    while o < n:
        out.append((o, min(128, n - o)))
        o += 128
    return out


def build(cfg):
    S_, D, FF, AH, BH, CH, DH, QR, KVR, T = (cfg[k] for k in ("S", "D", "FF", "AH", "BH", "CH", "DH", "QR", "KVR", "T"))
    DC, FC, NT, NCH = D // 128, FF // 128, S_ // T, S_ // 128
    MIXE, MIXO = (AH + BH) * 128, CH * 256 + DH * 128
    MEC, MOC = MIXE // 128, MIXO // 128
    XC = max(DC, MEC, MOC)
    EVEN_IN = 3 * AH * 128 + 4 * BH * 128
    ODD_IN = 2 * CH * 128 + 2 * CH * 256 + 32 + QR + KVR + 64
    QCH, KVCH = rows_of(QR), rows_of(KVR)
    TB = T // 128
    FH = FC // 2
    assert FC % 4 == 0

    nc = bass.Bass("TRN2", target_bir_lowering=False)

    def din(name, shape):
        return nc.dram_tensor(name, list(shape), F32, kind="ExternalInput").ap()

    def dscr(name, shape, dt):
        return nc.dram_tensor(name, list(shape), dt, kind="Internal").ap()

    x = din("x", [S_, D])
    norm_g = din("norm_g", [2, 3, D])
    final_g = din("final_norm_g", [D])
    wg = din("ffn_w_gate", [2, 2, D, FF])
    wu = din("ffn_w_up", [2, 2, D, FF])
    wd = din("ffn_w_down", [2, 2, FF, D])
    ab_in = din("ab_w_in", [1, D, EVEN_IN])
    ab_out = din("ab_w_out", [1, MIXE, D])
    ret_decay = din("ret_decay", [1, 2, BH])
    cd_in = din("cd_w_in", [1, D, ODD_IN])
    cd_out = din("cd_w_out", [1, MIXO, D])
    gate_w2 = din("gla_gate_w2", [1, 2, 16, CH * 128])
    gate_b = din("gla_gate_b", [1, 2, CH * 128])
    gla_g = din("gla_norm_g", [1, 256])
    q_g = din("mla_q_norm_g", [1, QR])
    w_uq = din("mla_w_uq", [1, QR, DH * 192])
    kv_g = din("mla_kv_norm_g", [1, KVR])
    w_ukv = din("mla_w_ukv", [1, KVR, DH * 256])
    cst = din("cst", [128, NCOLS])
    ropeA = din("ropeA", [2, 128, S_])
    ropeB = din("ropeB", [2, 128, S_])
    ropeM = din("ropeM", [2, 64, S_])
    y = nc.dram_tensor("y", [S_, D], F32, kind="ExternalOutput").ap()

    hT = dscr("hT", [D, S_], F32)
    mixT = dscr("mixT", [XC * 128, S_], BF16)
    qaT = dscr("qaT", [AH * 128, S_], BF16)
    kaT = dscr("kaT", [AH * 128, S_], BF16)
    va = dscr("va", [S_, AH * 128], BF16)
    qbT = dscr("qbT", [BH * 128, S_], BF16)
    kbT = dscr("kbT", [BH * 128, S_], BF16)
    vb = dscr("vb", [S_, BH * 128], BF16)
    gbT = dscr("gbT", [BH * 128, S_], BF16)
    qcT = dscr("qcT", [CH * 128, S_], BF16)
    kcT = dscr("kcT", [CH * 128, S_], BF16)
    vc = dscr("vc", [S_, CH * 256], BF16)
    rcT = dscr("rcT", [CH * 256, S_], BF16)
    lfT = dscr("lfT", [CH * 128, S_], F32)
    lbT = dscr("lbT", [CH * 128, S_], F32)
    qnT = dscr("qnT", [DH * 128, S_], BF16)
    qrT = dscr("qrT", [DH * 64, S_], BF16)
    knT = dscr("knT", [DH * 128, S_], BF16)
    krT = dscr("krT", [64, S_], BF16)
    vm = dscr("vm", [S_, DH * 128], BF16)
    NBA = 4 * (FF // 256) + 48
    WSB = 48
    wscr_l = [dscr("wscr%d" % i, [WSB, 128, XC * 512], BF16) for i in range((NBA + WSB - 1) // WSB)]

    class _W:
        def __getitem__(self, b):
            return wscr_l[b // WSB][b % WSB]
    wscr = _W()
    wdscr = dscr("wdscr", [8 * DC, 128, FH * 128], BF16)
    wkeys, wdkeys = {}, {}

    with contextlib.ExitStack() as gst:
        S = Sched(nc, gst)

        uniq = [0]

        def sb(stack, name, shape, dt):
            uniq[0] += 1
            return stack.enter_context(nc.sbuf_tensor("%s_%d" % (name, uniq[0]), list(shape), dt))

        cf = sb(gst, "cf", [128, NCOLS], F32)
        cb = sb(gst, "cb", [128, NCOLS], BF16)
        ng = sb(gst, "ng", [128, 6, DC], F32)
        fg = sb(gst, "fg", [128, DC], F32)
        lg = sb(gst, "lg", [128, 2 * BH], F32)
        B_c = Buf()
        PS = [gst.enter_context(nc.psum_tensor("ps%d" % i, [128, 512], F32)) for i in range(8)]
        BPS = [Buf(excl=True) for _ in range(8)]

        dma(S, "sync", cf[:], cst[:, :], (), [B_c])
        cp(S, "vector", cb[:], cf[:], [B_c], [B_c])
        gtmp = sb(gst, "gtmp", [128, 128], F32)
        ftmp = sb(gst, "ftmp", [128, 128], F32)
        dma(S, "sync", gtmp[0:6 * DC, :], norm_g.rearrange("a b (c p) -> (a b c) p", p=128), (), [B_c])
        dma(S, "sync", ftmp[0:DC, :], final_g.rearrange("(c p) -> c p", p=128), (), [B_c])
        tr(S, PS[7][:, 0:6 * DC], gtmp[0:6 * DC, :], cf[0:6 * DC, C_ID:C_ID + 6 * DC], [B_c], [BPS[7]])
        cp(S, "vector", ng[:].rearrange("p k c -> p (k c)"), PS[7][:, 0:6 * DC], [BPS[7]], [B_c])
        tr(S, PS[7][:, 0:DC], ftmp[0:DC, :], cf[0:DC, C_ID:C_ID + DC], [B_c], [BPS[7]])
        cp(S, "vector", fg[:], PS[7][:, 0:DC], [BPS[7]], [B_c])
        dma(S, "sync", lg[:], ret_decay.rearrange("a b c -> (a b c)").partition_broadcast(128), (), [B_c])
        act(S, lg[:], lg[:], AF.Exp, [B_c], [B_c])
        act(S, lg[:], lg[:], AF.Copy, [B_c], [B_c], scale=-1.0)

        ident_f = cf[:, C_ID:C_ID + 128]
        ones_f = cf[:, C_ONE:C_ONE + 128]
        mean_f = cf[:, C_MEAN:C_MEAN + 128]
        ident_b = cb[:, C_ID:C_ID + 128]
        ones_b = cb[:, C_ONE:C_ONE + 128]

        def token_phase(pidx):
            with contextlib.ExitStack() as st:
                hT_t = sb(st, "hT_t", [128, DC, T], F32)
                xn = sb(st, "xn", [128, XC, T], BF16)
                aT = sb(st, "aT", [128, FH, T], BF16)
                wA = [sb(st, "wA%d" % i, [128, XC, 512], BF16) for i in range(2)]
                wD = [sb(st, "wD%d" % i, [128, FH, 128], BF16) for i in range(2)]
                sq = [sb(st, "sq%d" % i, [128, T], BF16) for i in range(2)]
                rstd = sb(st, "rstd", [128, T], F32)
                sg = [sb(st, "sg%d" % i, [128, T], F32) for i in range(2)]
                xio = [sb(st, "xio%d" % i, [128, D], F32) for i in range(2)] if pidx != 1 else None
                stg = [sb(st, "stg%d" % i, [128, T], BF16) for i in range(4)]
                stgf = [sb(st, "stgf%d" % i, [128, T], F32) for i in range(2)] if pidx == 1 else None
                t1b = [sb(st, "t1b%d" % i, [128, T], F32) for i in range(2)]
                t2b = [sb(st, "t2b%d" % i, [128, T], F32) for i in range(2)]
                xbf = [sb(st, "xbf%d" % i, [128, T], BF16) for i in range(2)]
                rope_t = sb(st, "rope_t", [128, 4 if pidx == 0 else 2, T], F32)
                B_h, B_xn, B_a = bufs(DC), bufs(XC), bufs(FH)
                B_wA, B_wD, B_sq, B_sg, B_xio = bufs(2), bufs(2), bufs(2), bufs(2), bufs(2)
                B_stg, B_stgf, B_t1, B_t2, B_xbf = bufs(4), bufs(2), bufs(2), bufs(2), bufs(2)
                B_rstd, B_rope = Buf(), Buf()
                cnt = {"wA": 0, "wD": 0, "stg": 0, "stgf": 0, "rp": 0, "xio": 0}
                if pidx == 1:
                    w2 = sb(st, "w2", [16, 2, CH * 128], BF16)
                    nbias = sb(st, "nbias", [128, 2, CH], F32)
                    wuq_s = sb(st, "wuq_s", [128, len(QCH), DH * 192], BF16)
                    wukv_s = sb(st, "wukv_s", [128, len(KVCH), DH * 256], BF16)
                    qg_s = sb(st, "qg_s", [128, len(QCH)], F32)
                    kvg_s = sb(st, "kvg_s", [128, len(KVCH)], F32)
                    lowf = sb(st, "lowf", [16, 2, T], BF16)
                    cq_s = sb(st, "cq_s", [128, len(QCH), T], F32)
                    cqn = sb(st, "cqn", [128, len(QCH), T], BF16)
                    ckv_s = sb(st, "ckv_s", [128, len(KVCH), T], F32)
                    ckvn = sb(st, "ckvn", [128, len(KVCH), T], BF16)
                    B_od = Buf()
                    B_low, B_cq, B_cqn, B_ckv, B_ckvn = Buf(), Buf(), Buf(), Buf(), Buf()
                    rstd2 = sb(st, "rstd2", [128, T], F32)
                    B_rstd2 = Buf()
                    dma(S, "gpsimd", w2[:], gate_w2[0].rearrange("a r n -> r a n"), (), [B_od])
                    dma(S, "sync", nbias[:], gate_b[0].rearrange("a (c p) -> p a c", p=128), (), [B_od], slow=True)
                    ts(S, "vector", nbias[:], nbias[:], -1.0, ALU.mult, [B_od], [B_od])
                    for ci, (o, rws) in enumerate(QCH):
                        dma(S, "gpsimd", wuq_s[0:rws, ci, :], w_uq[0, o:o + rws, :], (), [B_od])
                        dma(S, "sync", qg_s[0:rws, ci:ci + 1], q_g[0, o:o + rws].rearrange("(p a) -> p a", a=1), (), [B_od])
                    for ci, (o, rws) in enumerate(KVCH):
                        dma(S, "gpsimd", wukv_s[0:rws, ci, :], w_ukv[0, o:o + rws, :], (), [B_od])
                        dma(S, "sync", kvg_s[0:rws, ci:ci + 1], kv_g[0, o:o + rws].rearrange("(p a) -> p a", a=1), (), [B_od])

                def load_wA(src2d, kc, ncols, col0, dst_col0=0, key=None):
                    i = cnt["wA_cur"]
                    if key is None:
                        key = (src2d.tensor.name, src2d.offset, col0, ncols, dst_col0)
                    if key not in wkeys:
                        assert len(wkeys) < NBA
                        wkeys[key] = (len(wkeys), Buf(), cnt["tile"])
                    blk, bb, t_created = wkeys[key]
                    sview = wscr[blk].rearrange("p (c f) -> p c f", f=512)[:, 0:kc, dst_col0:dst_col0 + ncols]
                    if t_created == cnt["tile"]:
                        dma(S, "gpsimd", wA[i][:, 0:kc, dst_col0:dst_col0 + ncols],
                            src2d[:, col0:col0 + ncols].rearrange("(c p) f -> p c f", p=128), (), [B_wA[i]])
                        dma(S, "sync", sview, wA[i][:, 0:kc, dst_col0:dst_col0 + ncols], [B_wA[i]], [bb])
                    elif key[0] == "gu":
                        if dst_col0 == 0:
                            dma(S, "gpsimd", wA[i][:, 0:kc, :], wscr[blk].rearrange("p (c f) -> p c f", f=512)[:, 0:kc, :], [bb], [B_wA[i]])
                    else:
                        dma(S, "gpsimd", wA[i][:, 0:kc, dst_col0:dst_col0 + ncols], sview, [bb], [B_wA[i]])

                def next_wA():
                    cnt["wA"] += 1
                    cnt["wA_cur"] = cnt["wA"] % 2
                    return cnt["wA_cur"]

                def sumsq_chunk(c):
                    i = c % 2
                    act(S, sq[i][:], hT_t[:, c, :], AF.Square, [B_h[c]], [B_sq[i]])
                    mm(S, PS[6][:], [(ones_b, sq[i][:])], [B_sq[i], B_c], [BPS[6]], start=(c == 0), stop=(c == DC - 1))
                    if c == DC - 1:
                        cnt["ss"] = True

                def rms_to_xn(gcol):
                    if not cnt.get("ss"):
                        for c in range(DC):
                            sumsq_chunk(c)
                    cnt["ss"] = False
                    act(S, rstd[:], PS[6][:], AF.Ln, [BPS[6]], [B_rstd], scale=1.0 / D, bias=EPS)
                    act(S, rstd[:], rstd[:], AF.Exp, [B_rstd], [B_rstd], scale=-0.5)

                def norm_apply(gap_fn, out_fn, out_bufs):
                    for c in range(DC):
                        stt(S, out_fn(c), hT_t[:, c, :], gap_fn(c), rstd[:], ALU.mult, ALU.mult,
                            [B_h[c], B_rstd, B_c], [out_bufs[c]])
                    if out_bufs is B_xn:
                        cnt["fresh"] = True

                def mm_xn(out, pair_fn, kc, rd, wbuf):
                    if cnt.get("fresh"):
                        cnt["fresh"] = False
                        for c in range(kc):
                            mm(S, out, [pair_fn(c)], rd + [B_xn[c]], [wbuf], start=(c == 0), stop=(c == kc - 1))
                    else:
                        mm(S, out, [pair_fn(c) for c in range(kc)], rd + B_xn[0:kc], [wbuf])

                def ffn(l, w):
                    rms_to_xn(None)
                    gk = l * 3 + (0 if w == 0 else 2)
                    norm_apply(lambda c: ng[:, gk, c:c + 1], lambda c: xn[:, c, :], B_xn)
                    for half in range(2):
                        f0 = half * FH
                        for j in range(FH // 2):
                            i = next_wA()
                            load_wA(wg[l, w], DC, 256, (f0 + 2 * j) * 128, 0, key=("gu", l, w, half, j))
                            load_wA(wu[l, w], DC, 256, (f0 + 2 * j) * 128, 256, key=("gu", l, w, half, j))
                            for fi in range(2):
                                f = 2 * j + fi
                                pg, pu = f % 2, 2 + f % 2
                                mm_xn(PS[pg][:, 0:T], (lambda c, i=i, fi=fi: (wA[i][:, c, fi * 128:(fi + 1) * 128], xn[:, c, :])), DC,
                                      [B_wA[i]], BPS[pg])
                                mm(S, PS[pu][:, 0:T], [(wA[i][:, c, 256 + fi * 128:256 + (fi + 1) * 128], xn[:, c, :]) for c in range(DC)],
                                   [B_wA[i]] + B_xn[0:DC], [BPS[pu]])
                                act(S, sg[f % 2][:], PS[pg][:, 0:T], AF.Silu, [BPS[pg]], [B_sg[f % 2]])
                                tt(S, "vector", aT[:, f, :], sg[f % 2][:], PS[pu][:, 0:T], ALU.mult, [B_sg[f % 2], BPS[pu]], [B_a[f]])
                        for dcn in range(DC):
                            cnt["wD"] += 1
                            i = cnt["wD"] % 2
                            key = (l, w, half, dcn)
                            if key not in wdkeys:
                                blk = len(wdkeys)
                                wdkeys[key] = (blk, Buf())
                                dma(S, "gpsimd", wD[i][:], wd[l, w][f0 * 128:(f0 + FH) * 128, dcn * 128:(dcn + 1) * 128].rearrange("(f p) d -> p f d", p=128),
                                    (), [B_wD[i]])
                                dma(S, "sync", wdscr[blk].rearrange("p (f d) -> p f d", d=128), wD[i][:], [B_wD[i]], [wdkeys[key][1]])
                            else:
                                blk, bb = wdkeys[key]
                                dma(S, "gpsimd", wD[i][:], wdscr[blk].rearrange("p (f d) -> p f d", d=128), [bb], [B_wD[i]])
                            pb = 4 + dcn % 2
                            mm(S, PS[pb][:, 0:T], [(wD[i][:, f, :], aT[:, f, :]) for f in range(FH)], [B_wD[i]] + B_a, [BPS[pb]])
                            stt(S, hT_t[:, dcn, :], PS[pb][:, 0:T], 0.5, hT_t[:, dcn, :], ALU.mult, ALU.add, [BPS[pb]], [B_h[dcn]])
                            if half == 1:
                                if dcn >= 2:
                                    sumsq_chunk(dcn - 2)
                                if dcn == DC - 1:
                                    for c_ in range(max(DC - 2, 0), DC):
                                        sumsq_chunk(c_)

                def out_proj(w_out2d, mc):
                    for cb0 in range(0, D, 512):
                        ncol = min(512, D - cb0)
                        i = next_wA()
                        load_wA(w_out2d, mc, ncol, cb0)
                        for dd in range(ncol // 128):
                            dcn = cb0 // 128 + dd
                            pb = 4 + dcn % 2
                            mm(S, PS[pb][:, 0:T], [(wA[i][:, m, dd * 128:(dd + 1) * 128], xn[:, m, :]) for m in range(mc)],
                               [B_wA[i]] + B_xn[0:mc], [BPS[pb]])
                            tt(S, "vector", hT_t[:, dcn, :], hT_t[:, dcn, :], PS[pb][:, 0:T], ALU.add, [BPS[pb]], [B_h[dcn]])
                            if dcn >= 2:
                                sumsq_chunk(dcn - 2)
                            if dcn == DC - 1:
                                for c_ in range(max(DC - 2, 0), DC):
                                    sumsq_chunk(c_)

                def get_stg():
                    cnt["stg"] += 1
                    return cnt["stg"] % 4

                def rope_fm(ps_i, rows, ctab, stab, perm, out_dram):
                    cnt["rp"] += 1
                    k = cnt["rp"] % 2
                    RP = cfg.get("rp", 9)
                    if RP < 1:
                        return
                    cp(S, "scalar", xbf[k][0:rows, :], PS[ps_i][0:rows, 0:T], [BPS[ps_i]], [B_xbf[k]])
                    if RP < 2:
                        return
                    mm(S, PS[7][0:rows, 0:T], [(perm, xbf[k][0:rows, :])], [B_xbf[k], B_c], [BPS[7]])
                    if RP < 3:
                        return
                    tt(S, "vector", t1b[k][0:rows, :], PS[ps_i][0:rows, 0:T], ctab, ALU.mult, [BPS[ps_i], B_rope], [B_t1[k]])
                    tt(S, "vector", t2b[k][0:rows, :], PS[7][0:rows, 0:T], stab, ALU.mult, [BPS[7], B_rope], [B_t2[k]])
                    if RP < 4:
                        return
                    si = get_stg()
                    tt(S, "vector", stg[si][0:rows, :], t1b[k][0:rows, :], t2b[k][0:rows, :], ALU.add, [B_t1[k], B_t2[k]], [B_stg[si]])
                    if RP < 5:
                        return
                    dma(S, "sync", out_dram, stg[si][0:rows, :], [B_stg[si]], ())

                def proj_fm(i, col, ncols_out, kc, ps_i):
                    mm_xn(PS[ps_i][0:ncols_out, 0:T], (lambda c: (wA[i][:, c, col:col + ncols_out], xn[:, c, :])), kc,
                          [B_wA[i]], BPS[ps_i])

                def proj_tm(w2d, col0, ncols, out_dram2d, t0):
                    for cbk in range(0, ncols, 512):
                        nb_ = min(512, ncols - cbk)
                        i = next_wA()
                        load_wA(w2d, DC, nb_, col0 + cbk)
                        for b in range(TB):
                            pb = 4 + b % 2
                            mm_xn(PS[pb][:, 0:nb_], (lambda c, i=i, b=b, nb_=nb_: (xn[:, c, b * 128:(b + 1) * 128], wA[i][:, c, 0:nb_])), DC,
                                  [B_wA[i]], BPS[pb])
                            si = get_stg()
                            cp(S, "scalar", stg[si][:, 0:nb_], PS[pb][:, 0:nb_], [BPS[pb]], [B_stg[si]])
                            dma(S, "sync", out_dram2d[t0 + b * 128:t0 + (b + 1) * 128, cbk:cbk + nb_], stg[si][:, 0:nb_], [B_stg[si]], ())

                def sect_fm(w2d, col0, nfeat, handler):
                    for cbk in range(0, nfeat, 512):
                        nb_ = min(512, nfeat - cbk)
                        i = next_wA()
                        load_wA(w2d, DC, nb_, col0 + cbk)
                        for (o, rws) in rows_of(nb_):
                            ps_i = (cnt["rp"] + o // 128) % 2
                            ps_i = 0 if (o // 128) % 2 == 0 else 1
                            proj_fm(i, o, rws, DC, ps_i)
                            handler(ps_i, cbk + o, rws)

                DBG = cfg.get("dbg", 99)
                for t in range(min(NT, cfg.get("ntiles", NT))):
                    t0 = t * T
                    tsl = slice(t0, t0 + T)
                    cnt["tile"] = (pidx, t)
                    if pidx == 0:
                        for b in range(TB):
                            cnt["xio"] += 1
                            k = cnt["xio"] % 2
                            dma(S, "sync", xio[k][:], x[t0 + b * 128:t0 + (b + 1) * 128, :], (), [B_xio[k]])
                            for c0 in range(0, DC, 4):
                                nn = min(4, DC - c0)
                                for cc in range(nn):
                                    c = c0 + cc
                                    tr(S, PS[7][:, cc * 128:(cc + 1) * 128], xio[k][:, c * 128:(c + 1) * 128], ident_f,
                                       [B_xio[k], B_c], [BPS[7]])
                                cp(S, "vector", hT_t[:, c0:c0 + nn, b * 128:(b + 1) * 128],
                                   PS[7][:, 0:nn * 128].rearrange("p (c t) -> p c t", t=128), [BPS[7]], B_h[c0:c0 + nn])
                    else:
                        dma(S, "sync", hT_t[:], hT[:, tsl].rearrange("(c p) t -> p c t", p=128), (), B_h)
                        mc = MEC if pidx == 1 else MOC
                        dma(S, "sync", xn[:, 0:mc, :], mixT[0:mc * 128, tsl].rearrange("(c p) t -> p c t", p=128), (), B_xn[0:mc])
                        out_proj(ab_out[0] if pidx == 1 else cd_out[0], mc)
                    if DBG < 2:
                        continue
                    if pidx == 0:
                        ffn(0, 0)
                    elif pidx == 1:
                        ffn(0, 1)
                        ffn(1, 0)
                    else:
                        ffn(1, 1)
                    if pidx == 2:
                        rms_to_xn(None)
                        norm_apply(lambda c: fg[:, c:c + 1], lambda c: hT_t[:, c, :], B_h)
                        for b in range(TB):
                            cnt["xio"] += 1
                            k = cnt["xio"] % 2
                            for c0 in range(0, DC, 4):
                                nn = min(4, DC - c0)
                                for cc in range(nn):
                                    c = c0 + cc
                                    tr(S, PS[7][:, cc * 128:(cc + 1) * 128], hT_t[:, c, b * 128:(b + 1) * 128], ident_f,
                                       [B_h[c], B_c], [BPS[7]])
                                cp(S, "vector", xio[k][:, c0 * 128:(c0 + nn) * 128], PS[7][:, 0:nn * 128], [BPS[7]], [B_xio[k]])
                            dma(S, "sync", y[t0 + b * 128:t0 + (b + 1) * 128, :], xio[k][:], [B_xio[k]], ())
                        continue
                    if DBG < 3:
                        continue
                    dma(S, "sync", hT[:, tsl].rearrange("(c p) t -> p c t", p=128), hT_t[:], B_h, ())
                    rms_to_xn(None)
                    gk = pidx * 3 + 1
                    norm_apply(lambda c: ng[:, gk, c:c + 1], lambda c: xn[:, c, :], B_xn)
                    if DBG < 4:
                        continue
                    if pidx == 0:
                        w2d = ab_in[0]
                        dma(S, "sync", rope_t[:, 0:2, :], ropeA[:, :, tsl].rearrange("a p t -> p a t"), (), [B_rope])
                        dma(S, "sync", rope_t[:, 2:4, :], ropeB[:, :, tsl].rearrange("a p t -> p a t"), (), [B_rope])
                        PA = cb[:, C_PA:C_PA + 128]
                        PB = cb[:, C_PB:C_PB + 128]
                        o_qa, o_ka, o_va = 0, AH * 128, 2 * AH * 128
                        o_qb = 3 * AH * 128
                        o_kb, o_vb, o_gb = o_qb + BH * 128, o_qb + 2 * BH * 128, o_qb + 3 * BH * 128
                        def h_silu(p, fo, r, dst=gbT):
                            si = get_stg()
                            act(S, stg[si][0:r, :], PS[p][0:r, 0:T], AF.Silu, [BPS[p]], [B_stg[si]])
                            dma(S, "sync", dst[fo:fo + r, tsl], stg[si][0:r, :], [B_stg[si]], ())
                        sects = [
                            lambda: sect_fm(w2d, o_qa, AH * 128, lambda p, fo, r: rope_fm(p, r, rope_t[0:r, 0, :], rope_t[0:r, 1, :], PA, qaT[fo:fo + r, tsl])),
                            lambda: sect_fm(w2d, o_ka, AH * 128, lambda p, fo, r: rope_fm(p, r, rope_t[0:r, 0, :], rope_t[0:r, 1, :], PA, kaT[fo:fo + r, tsl])),
                            lambda: proj_tm(w2d, o_va, AH * 128, va, t0),
                            lambda: sect_fm(w2d, o_qb, BH * 128, lambda p, fo, r: rope_fm(p, r, rope_t[0:r, 2, :], rope_t[0:r, 3, :], PB, qbT[fo:fo + r, tsl])),
                            lambda: sect_fm(w2d, o_kb, BH * 128, lambda p, fo, r: rope_fm(p, r, rope_t[0:r, 2, :], rope_t[0:r, 3, :], PB, kbT[fo:fo + r, tsl])),
                            lambda: proj_tm(w2d, o_vb, BH * 128, vb, t0),
                            lambda: sect_fm(w2d, o_gb, BH * 128, h_silu),
                        ]
                        for f_ in sects[:cfg.get("nsect", 7)]:
                            f_()
                    else:
                        w2d = cd_in[0]
                        dma(S, "sync", rope_t[0:64, 0:2, :], ropeM[:, :, tsl].rearrange("a p t -> p a t"), (), [B_rope])
                        PM = cb[0:64, C_PM:C_PM + 64]
                        o_qc, o_kc, o_vc = 0, CH * 128, 2 * CH * 128
                        o_rc = o_vc + CH * 256
                        o_af = o_rc + CH * 256
                        o_ab, o_cq = o_af + 16, o_af + 32
                        o_ckv = o_cq + QR
                        o_kr = o_ckv + KVR

                        def h_plain(dst):
                            def hh(p, fo, r):
                                si = get_stg()
                                cp(S, "scalar", stg[si][0:r, :], PS[p][0:r, 0:T], [BPS[p]], [B_stg[si]])
                                dma(S, "sync", dst[fo:fo + r, tsl], stg[si][0:r, :], [B_stg[si]], ())
                            return hh
                        def h_silu2(p, fo, r):
                            si = get_stg()
                            act(S, stg[si][0:r, :], PS[p][0:r, 0:T], AF.Silu, [BPS[p]], [B_stg[si]])
                            dma(S, "sync", rcT[fo:fo + r, tsl], stg[si][0:r, :], [B_stg[si]], ())

                        blk = []
                        for (cbk, nb_) in ((0, 32 + QR), (32 + QR, KVR + 64)):
                            assert nb_ <= 512
                            i = next_wA()
                            load_wA(w2d, DC, nb_, o_af + cbk)
                            blk.append((cbk, nb_, i))

                        def small_proj(goff, rows, ps_i):
                            for (cbk, nb_, i) in blk:
                                if cbk <= goff and goff + rows <= cbk + nb_:
                                    proj_fm(i, goff - cbk, rows, DC, ps_i)
                                    return
                            raise AssertionError("straddle %d %d" % (goff, rows))

                        for d_ in range(2):
                            small_proj(16 * d_, 16, d_)
                            cp(S, "scalar", lowf[:, d_, :], PS[d_][0:16, 0:T], [BPS[d_]], [B_low])

                        def latent_raw(goff, chs, raw, B_raw, ps_sum):
                            for ci, (o, rws) in enumerate(chs):
                                pr = ci % 2
                                small_proj(goff + o, rws, pr)
                                cp(S, "scalar", raw[0:rws, ci, :], PS[pr][0:rws, 0:T], [BPS[pr]], [B_raw])
                                act(S, sq[ci % 2][0:rws, :], PS[pr][0:rws, 0:T], AF.Square, [BPS[pr]], [B_sq[ci % 2]])
                                mm(S, PS[ps_sum][:, 0:T], [(ones_b[0:rws, :], sq[ci % 2][0:rws, :])], [B_sq[ci % 2], B_c], [BPS[ps_sum]],
                                   start=(ci == 0), stop=(ci == len(chs) - 1))

                        def latent_norm(chs, nfeat, raw, B_raw, nrm, B_nrm, gtile, ps_sum, rs_t, B_rs):
                            act(S, rs_t[:], PS[ps_sum][:, 0:T], AF.Ln, [BPS[ps_sum]], [B_rs], scale=1.0 / nfeat, bias=EPS)
                            act(S, rs_t[:], rs_t[:], AF.Exp, [B_rs], [B_rs], scale=-0.5)
                            for ci, (o, rws) in enumerate(chs):
                                stt(S, nrm[0:rws, ci, :], raw[0:rws, ci, :], gtile[0:rws, ci:ci + 1], rs_t[0:rws, :], ALU.mult, ALU.mult,
                                    [B_raw, B_rs, B_od], [B_nrm])

                        latent_raw(32, QCH, cq_s, B_cq, 2)
                        latent_raw(32 + QR, KVCH, ckv_s, B_ckv, 3)
                        small_proj(32 + QR + KVR, 64, 1)
                        rope_fm(1, 64, rope_t[0:64, 0, :], rope_t[0:64, 1, :], PM, krT[0:64, tsl])
                        latent_norm(QCH, QR, cq_s, B_cq, cqn, B_cqn, qg_s, 2, rstd, B_rstd)
                        latent_norm(KVCH, KVR, ckv_s, B_ckv, ckvn, B_ckvn, kvg_s, 3, rstd2, B_rstd2)

                        sect_fm(w2d, o_qc, CH * 128, h_plain(qcT))
                        sect_fm(w2d, o_kc, CH * 128, h_plain(kcT))
                        proj_tm(w2d, o_vc, CH * 256, vc, t0)
                        sect_fm(w2d, o_rc, CH * 256, h_silu2)

                        for h in range(DH):
                            mm(S, PS[0][:, 0:T], [(wuq_s[0:rws, ci, h * 192:h * 192 + 128], cqn[0:rws, ci, :]) for ci, (o, rws) in enumerate(QCH)],
                               [B_od, B_cqn], [BPS[0]])
                            si = get_stg()
                            cp(S, "scalar", stg[si][:], PS[0][:, 0:T], [BPS[0]], [B_stg[si]])
                            dma(S, "sync", qnT[h * 128:(h + 1) * 128, tsl], stg[si][:], [B_stg[si]], ())
                            mm(S, PS[1][0:64, 0:T], [(wuq_s[0:rws, ci, h * 192 + 128:h * 192 + 192], cqn[0:rws, ci, :]) for ci, (o, rws) in enumerate(QCH)],
                               [B_od, B_cqn], [BPS[1]])
                            rope_fm(1, 64, rope_t[0:64, 0, :], rope_t[0:64, 1, :], PM, qrT[h * 64:(h + 1) * 64, tsl])
                        for h in range(DH):
                            pk = 2 + h % 2
                            mm(S, PS[pk][:, 0:T], [(wukv_s[0:rws, ci, h * 256:h * 256 + 128], ckvn[0:rws, ci, :]) for ci, (o, rws) in enumerate(KVCH)],
                               [B_od, B_ckvn], [BPS[pk]])
                            si = get_stg()
                            cp(S, "scalar", stg[si][:], PS[pk][:, 0:T], [BPS[pk]], [B_stg[si]])
                            dma(S, "sync", knT[h * 128:(h + 1) * 128, tsl], stg[si][:], [B_stg[si]], ())
                        for b in range(TB):
                            for h0 in range(0, DH, 4):
                                nh = min(4, DH - h0)
                                pb = 4 + b % 2
                                for hh in range(nh):
                                    h = h0 + hh
                                    mm(S, PS[pb][:, hh * 128:(hh + 1) * 128],
                                       [(ckvn[0:rws, ci, b * 128:(b + 1) * 128], wukv_s[0:rws, ci, h * 256 + 128:h * 256 + 256]) for ci, (o, rws) in enumerate(KVCH)],
                                       [B_od, B_ckvn], [BPS[pb]])
                                si = get_stg()
                                cp(S, "scalar", stg[si][:, 0:nh * 128], PS[pb][:, 0:nh * 128], [BPS[pb]], [B_stg[si]])
                                dma(S, "sync", vm[t0 + b * 128:t0 + (b + 1) * 128, h0 * 128:(h0 + nh) * 128], stg[si][:, 0:nh * 128], [B_stg[si]], ())
                        for d_ in range(2):
                            dstT = lfT if d_ == 0 else lbT
                            for hc in range(CH):
                                pgt = (d_ * CH + hc) % 2
                                mm(S, PS[pgt][:, 0:T], [(w2[:, d_, hc * 128:(hc + 1) * 128], lowf[:, d_, :])], [B_od, B_low], [BPS[pgt]])
                                k = cnt["stgf"] = cnt["stgf"] + 1
                                k %= 2
                                act(S, stgf[k][:], PS[pgt][:, 0:T], AF.Exp, [BPS[pgt], B_od], [B_stgf[k]], scale=-1.0, bias=nbias[:, d_, hc:hc + 1])
                                act(S, stgf[k][:], stgf[k][:], AF.Ln, [B_stgf[k]], [B_stgf[k]], scale=1.0, bias=1.0)
                                dma(S, "sync", dstT[hc * 128:(hc + 1) * 128, tsl], stgf[k][:], [B_stgf[k]], ())
                S.barrier()
                S.flush()

        def mixer_even():
            scale = 128.0 ** -0.5
            with contextlib.ExitStack() as st:
                NB = S_ // 128
                qT2 = [sb(st, "a_q%d" % j, [128, S_], BF16) for j in range(2)]
                kT2 = [sb(st, "a_k%d" % j, [128, S_], BF16) for j in range(2)]
                vbr2 = [[sb(st, "a_v%d_%d" % (i, j), [128, NB, 128], BF16) for i in range(3)] for j in range(2)]
                B_q2, B_k2, B_v2 = bufs(2), bufs(2), [bufs(3) for _ in range(2)]
                accn2 = [sb(st, "a_n%d" % j, [128, S_], F32) for j in range(2)]
                accd2 = [sb(st, "a_d%d" % j, [128, S_], F32) for j in range(2)]
                B_n2, B_d2 = bufs(2), bufs(2)
                eb = [sb(st, "a_e%d" % i, [128, 256], BF16) for i in range(3)]
                em = [sb(st, "a_em%d" % i, [128, 256], BF16) for i in range(3)]
                obf = sb(st, "a_o", [128, S_], BF16)
                junk = sb(st, "a_junk", [128, 4], F32)
                B_junk = Buf()
                B_o = Buf()
                B_e, B_em = bufs(3), bufs(3)
                MB = cb[:, C_MB:C_MB + 256]
                SB_ = (0, 1, 6)
                it = 0
                def load_head(h):
                    j = h % 2
                    hs = slice(h * 128, (h + 1) * 128)
                    dma(S, "sync", qT2[j][:], qaT[hs, :], (), [B_q2[j]])
                    dma(S, "sync", kT2[j][:], kaT[hs, :], (), [B_k2[j]])
                    for bi, dil in enumerate((1, 4, 16)):
                        L = S_ // dil
                        nb = L // 128
                        for r in range(dil):
                            dma(S, "sync", vbr2[j][bi][:, r * nb:(r + 1) * nb, :],
                                va[r::dil, hs].rearrange("(b p) e -> p b e", p=128), (), [B_v2[j][bi]])

                load_head(0)
                for h in range(AH):
                    hs = slice(h * 128, (h + 1) * 128)
                    qT, kT, vbr = qT2[h % 2], kT2[h % 2], vbr2[h % 2]
                    B_q, B_k, B_v = B_q2[h % 2], B_k2[h % 2], B_v2[h % 2]
                    accn, accd, B_n, B_d = accn2[h % 2], accd2[h % 2], B_n2[h % 2], B_d2[h % 2]
                    if h + 1 < AH:
                        load_head(h + 1)
                    B_nr = {(bi, r): Buf() for bi, dil in enumerate((1, 4, 16)) for r in range(dil)}
                    B_dr = {(bi, r): Buf() for bi, dil in enumerate((1, 4, 16)) for r in range(dil)}
                    memset(S, "gpsimd", accn[:], 0.0, [B_n] + [B_nr[(0, 0)]])
                    memset(S, "gpsimd", accd[:], 0.0, [B_d] + [B_dr[(0, 0)]])
                    blocks = []
                    for bi, dil in enumerate((1, 4, 16)):
                        L = S_ // dil
                        nb = L // 128
                        for b in range(nb):
                            for r in range(dil):
                                blocks.append((bi, dil, L, nb, r, b))

                    def geom(blk):
                        bi, dil, L, nb, r, b = blk
                        j0 = 128 * b
                        qlo, qhi = max(0, j0 - 64), min(L, j0 + 192)
                        return j0, qlo, qhi, qhi - qlo, qlo - (j0 - 64)

                    def issue_scores(x):
                        bi, dil, L, nb, r, b = blocks[x]
                        j0, qlo, qhi, n, flo = geom(blocks[x])
                        kc = kT[:, r + dil * j0:r + dil * (j0 + 127) + 1:dil]
                        qc = qT[:, r + dil * qlo:r + dil * (qhi - 1) + 1:dil]
                        sbk = SB_[(it + x) % 3]
                        mm(S, PS[sbk][:, 0:n], [(kc, qc)], [B_k, B_q], [BPS[sbk]])

                    issue_scores(0)
                    issue_scores(1)
                    for x, blk in enumerate(blocks):
                        bi, dil, L, nb, r, b = blk
                        j0, qlo, qhi, n, flo = geom(blk)
                        k3 = (it + x) % 3
                        sbk = SB_[k3]
                        k = (it + x) % 2
                        if x + 2 < len(blocks):
                            issue_scores(x + 2)
                        act(S, eb[k3][:, 0:n], PS[sbk][:, 0:n], AF.Exp, [BPS[sbk]], [B_e[k3]], scale=scale)
                        tt(S, "gpsimd", em[k3][:, 0:n], eb[k3][:, 0:n], MB[:, flo:flo + n], ALU.mult, [B_e[k3], B_c], [B_em[k3]])
                        mm(S, PS[2 + k][:, 0:n], [(vbr[bi][:, r * nb + b, :], em[k3][:, 0:n])], [B_v[bi], B_em[k3]], [BPS[2 + k]])
                        mm(S, PS[4 + k][:, 0:n], [(ones_b, em[k3][:, 0:n])], [B_c, B_em[k3]], [BPS[4 + k]])
                        cs = slice(r + dil * qlo, r + dil * (qhi - 1) + 1, dil)
                        if x > 0 and blocks[x - 1][0] != bi:
                            pbi = blocks[x - 1][0]
                            pdil = (1, 4, 16)[pbi]
                            S.op("vector", lambda e: e.memset(junk[:, 0:1], 0.0),
                                 [B_nr[(pbi, rr)] for rr in range(pdil)] + [B_dr[(pbi, rr)] for rr in range(pdil)],
                                 [B_nr[(bi, rr)] for rr in range(dil)] + [B_dr[(bi, rr)] for rr in range(dil)] + [B_junk])
                        tt(S, "vector", accn[:, cs], accn[:, cs], PS[2 + k][:, 0:n], ALU.add, [BPS[2 + k]], [B_nr[(bi, r)]])
                        tt(S, "vector", accd[:, cs], accd[:, cs], PS[4 + k][:, 0:n], ALU.add, [BPS[4 + k]], [B_dr[(bi, r)]])
                    it += len(blocks)
                    S.op("vector", lambda e: e.memset(junk[:, 0:1], 0.0),
                         [B_nr[(2, rr)] for rr in range(16)] + [B_dr[(2, rr)] for rr in range(16)], [B_n, B_d, B_junk])
                    act(S, accd[:], accd[:], AF.Ln, [B_d], [B_d])
                    act(S, accd[:], accd[:], AF.Exp, [B_d], [B_d], scale=-1.0)
                    tt(S, "gpsimd", obf[:], accn[:], accd[:], ALU.mult, [B_n, B_d], [B_o])
                    dma(S, "sync", mixT[hs, :], obf[:], [B_o], ())
                S.barrier()
                S.flush()

        def mixer_ret():
            lns = math.log(128.0 ** -0.5)
            NGR = S_ // 512
            with contextlib.ExitStack() as st:
                PSb = [PS[i][:].bitcast(BF16) for i in range(8)]

                class Slot:
                    pass
                slots = []
                for si in range(2):
                    L = Slot()
                    n = "b%d_" % si
                    L.qT = sb(st, n + "q", [128, S_], BF16)
                    L.kT = sb(st, n + "k", [128, S_], BF16)
                    L.gT = sb(st, n + "g", [128, S_], BF16)
                    L.v = sb(st, n + "v", [128, NCH, 128], BF16)
                    L.SfA = sb(st, n + "sf", [128, NCH, 128], BF16)
                    L.SbA = sb(st, n + "sb", [128, NCH, 128], BF16)
                    L.Sf2 = [sb(st, n + "sfr%d" % i, [128, 128], F32) for i in range(2)]
                    L.kdall = [sb(st, n + "kdall%d" % i, [128, NCH, 128], BF16) for i in range(2)]
                    L.DT = sb(st, n + "dt", [128, 128], F32)
                    L.DT2 = sb(st, n + "dt2", [128, 128], F32)
                    L.qdf = sb(st, n + "qdf", [128, 512], F32)
                    L.qdb = sb(st, n + "qdb", [128, 512], F32)
                    L.kd = sb(st, n + "kd", [128, 4], F32)
                    L.qf = [sb(st, n + "qf%d" % i, [128, 512], BF16) for i in range(2)]
                    L.qb_ = [sb(st, n + "qb%d" % i, [128, 512], BF16) for i in range(2)]
                    L.A = [sb(st, n + "A%d" % i, [128, 128], BF16) for i in range(2)]
                    L.o = sb(st, n + "o", [128, 512], F32)
                    L.o2 = sb(st, n + "o2", [128, 512], F32)
                    L.msq = sb(st, n + "msq", [128, 512], F32)
                    L.var = sb(st, n + "var", [128, 512], F32)
                    L.res = sb(st, n + "res", [128, 512], F32)
                    L.ob = [sb(st, n + "ob%d" % i, [128, 512], BF16) for i in range(2)]
                    (L.B_q, L.B_k, L.B_g, L.B_v, L.B_sfa, L.B_sba, L.B_hc, L.B_o, L.B_o2, L.B_msq, L.B_var, L.B_res) = (Buf() for _ in range(12))
                    L.B_sf2, L.B_kdall, L.B_A, L.B_ob, L.B_qf, L.B_qb = bufs(2), bufs(2), bufs(2), bufs(2), bufs(2), bufs(2)
                    L.pt, L.po = si, (2 + 2 * si, 3 + 2 * si)
                    L.pk = L.po[0]
                    slots.append(L)

                def head_gen(h, L):
                    hs = slice(h * 128, (h + 1) * 128)
                    lgf, lgb = lg[:, h:h + 1], lg[:, BH + h:BH + h + 1]
                    dma(S, "sync", L.qT[:], qbT[hs, :], (), [L.B_q])
                    dma(S, "sync", L.kT[:], kbT[hs, :], (), [L.B_k])
                    dma(S, "sync", L.gT[:], gbT[hs, :], (), [L.B_g])
                    dma(S, "sync", L.v[:], vb[:, hs].rearrange("(b p) e -> p b e", p=128), (), [L.B_v])
                    yield
                    act(S, L.DT[:], cf[:, C_POS:C_POS + 128], AF.Exp, [B_c], [L.B_hc], scale=lgf, bias=lns)
                    act(S, L.DT2[:], cf[:, C_NEG:C_NEG + 128], AF.Exp, [B_c], [L.B_hc], scale=lgb)
                    tt(S, "vector", L.DT[:], L.DT[:], L.DT2[:], ALU.mult, [L.B_hc], [L.B_hc])
                    act(S, L.qdf[:], cf[:, C_IDX:C_IDX + 512], AF.Exp, [B_c], [L.B_hc], scale=lgf, bias=lns)
                    act(S, L.qdb[:], cf[:, C_CMA:C_CMA + 512], AF.Exp, [B_c], [L.B_hc], scale=lgb, bias=lns)
                    act(S, L.kd[:, 0:1], cf[:, C_COLF:C_COLF + 1], AF.Exp, [B_c], [L.B_hc], scale=lgf)
                    act(S, L.kd[:, 1:2], cf[:, C_COLB:C_COLB + 1], AF.Exp, [B_c], [L.B_hc], scale=lgb)
                    act(S, L.kd[:, 2:3], cf[:, C_CC:C_CC + 1], AF.Exp, [B_c], [L.B_hc], scale=lgf)
                    act(S, L.kd[:, 3:4], cf[:, C_CC:C_CC + 1], AF.Exp, [B_c], [L.B_hc], scale=lgb)
                    yield
                    for direction in (0, 1):
                        SA, B_sa = (L.SfA, L.B_sfa) if direction == 0 else (L.SbA, L.B_sba)
                        first = 0 if direction == 0 else NCH - 1
                        for g8 in range(NCH // 8):
                            for j in range(8):
                                i = g8 * 8 + j
                                tr(S, PSb[L.pt][:, j * 128:(j + 1) * 128], L.kT[:, i * 128:(i + 1) * 128], ident_b, [L.B_k, B_c], [BPS[L.pt]])
                            ts(S, "vector", L.kdall[direction][:, g8 * 8:(g8 + 1) * 8, :], PSb[L.pt][:, 0:1024].rearrange("p (c d) -> p c d", d=128),
                               L.kd[:, direction:direction + 1], ALU.mult, [BPS[L.pt], L.B_hc], [L.B_kdall[direction]])
                            yield
                        memset(S, "vector", L.Sf2[0][:], 0.0, [L.B_sf2[0]])
                        memset(S, "gpsimd", SA[:, first, :], 0.0, [B_sa])
                        order = range(0, NCH - 1) if direction == 0 else range(NCH - 1, 0, -1)
                        for s_, i in enumerate(order):
                            mm(S, PS[L.pk][:, 0:128], [(L.kdall[direction][:, i, :], L.v[:, i, :])], [L.B_kdall[direction], L.B_v], [BPS[L.pk]])
                            stt(S, L.Sf2[(s_ + 1) % 2][:], L.Sf2[s_ % 2][:], L.kd[:, 2 + direction:3 + direction], PS[L.pk][:, 0:128], ALU.mult, ALU.add,
                                [BPS[L.pk], L.B_hc, L.B_sf2[s_ % 2]], [L.B_sf2[(s_ + 1) % 2]])
                            nxt = i + 1 if direction == 0 else i - 1
                            cp(S, "gpsimd", SA[:, nxt, :], L.Sf2[(s_ + 1) % 2][:], [L.B_sf2[(s_ + 1) % 2]], [B_sa])
                            yield

                    def front(g):
                        gs = slice(g * 512, (g + 1) * 512)
                        q2 = g % 2
                        tt(S, "gpsimd", L.qf[q2][:], L.qT[:, gs], L.qdf[:], ALU.mult, [L.B_q, L.B_hc], [L.B_qf[q2]])
                        tt(S, "gpsimd", L.qb_[q2][:], L.qT[:, gs], L.qdb[:], ALU.mult, [L.B_q, L.B_hc], [L.B_qb[q2]])
                        po = L.po[g % 2]
                        for ci in range(4):
                            i = g * 4 + ci
                            cs = slice(i * 128, (i + 1) * 128)
                            k = i % 2
                            mm(S, PS[L.pt][:, 0:128], [(L.kT[:, cs], L.qT[:, cs])], [L.B_k, L.B_q], [BPS[L.pt]])
                            tt(S, "vector", L.A[k][:], PS[L.pt][:, 0:128], L.DT[:], ALU.mult, [BPS[L.pt], L.B_hc], [L.B_A[k]])
                            mm(S, PS[po][:, ci * 128:(ci + 1) * 128],
                               [(L.SfA[:, i, :], L.qf[q2][:, ci * 128:(ci + 1) * 128]), (L.SbA[:, i, :], L.qb_[q2][:, ci * 128:(ci + 1) * 128])],
                               [L.B_sfa, L.B_sba, L.B_qf[q2], L.B_qb[q2]], [BPS[po]], start=True, stop=False)
                            mm(S, PS[po][:, ci * 128:(ci + 1) * 128], [(L.v[:, i, :], L.A[k][:])], [L.B_v, L.B_A[k]], [BPS[po]],
                               start=False, stop=True)
                            yield

                    def post(g):
                        gs = slice(g * 512, (g + 1) * 512)
                        po = L.po[g % 2]
                        cp(S, "scalar", L.o[:], PS[po][:], [BPS[po]], [L.B_o])
                        act(S, L.o2[:], PS[po][:], AF.Square, [BPS[po]], [L.B_o2])
                        yield
                        mm(S, PS[6][:], [(mean_f, L.o[:])], [L.B_o, B_c], [BPS[6]])
                        mm(S, PS[7][:], [(mean_f, L.o2[:])], [L.B_o2, B_c], [BPS[7]])
                        act(S, L.msq[:], PS[6][:], AF.Square, [BPS[6]], [L.B_msq])
                        tt(S, "vector", L.var[:], PS[7][:], L.msq[:], ALU.subtract, [BPS[7], L.B_msq], [L.B_var])
                        tt(S, "vector", L.res[:], L.o[:], PS[6][:], ALU.subtract, [L.B_o, BPS[6]], [L.B_res])
                        yield
                        act(S, L.var[:], L.var[:], AF.Ln, [L.B_var], [L.B_var], scale=1.0, bias=EPS)
                        act(S, L.var[:], L.var[:], AF.Exp, [L.B_var], [L.B_var], scale=-0.5)
                        tt(S, "gpsimd", L.res[:], L.res[:], L.var[:], ALU.mult, [L.B_res, L.B_var], [L.B_res])
                        kk = g % 2
                        tt(S, "gpsimd", L.ob[kk][:], L.res[:], L.gT[:, gs], ALU.mult, [L.B_res, L.B_g], [L.B_ob[kk]])
                        dma(S, "sync", mixT[AH * 128 + h * 128:AH * 128 + (h + 1) * 128, gs], L.ob[kk][:], [L.B_ob[kk]], ())
                        yield

                    yield from front(0)
                    for g in range(NGR):
                        if g + 1 < NGR:
                            yield from front(g + 1)
                        yield from post(g)

                for h0 in range(0, BH, 2):
                    gens = [head_gen(h0 + j, slots[j]) for j in range(min(2, BH - h0))]
                    while gens:
                        for g_ in list(gens):
                            try:
                                next(g_)
                            except StopIteration:
                                gens.remove(g_)
                S.barrier()
                S.flush()

        def mixer_gla():
            sc = 128.0 ** -0.5
            with contextlib.ExitStack() as st:
                NG = S_ // 512
                v = sb(st, "c_v", [128, NCH, 256], BF16)
                qF = sb(st, "c_qF", [128, S_], BF16)
                kF = sb(st, "c_kF", [128, S_], BF16)
                qB = sb(st, "c_qB", [128, S_], BF16)
                qB2 = sb(st, "c_qB2", [128, S_], BF16)
                kB = sb(st, "c_kB", [128, S_], BF16)
                eFl = sb(st, "c_eFl", [128, NCH], F32)
                eEt = sb(st, "c_eEt", [128, NCH], F32)
                SfA = sb(st, "c_sf", [128, NCH, 256], BF16)
                SbA = sb(st, "c_sb", [128, NCH, 256], BF16)
                Sr2 = [sb(st, "c_sr%d" % i, [128, 256], F32) for i in range(2)]
                kve = [sb(st, "c_kve%d" % i, [128, 256], F32) for i in range(2)]
                kdall = [sb(st, "c_kdall%d" % i, [128, NCH, 128], BF16) for i in range(2)]
                B_sr2, B_kve, B_kdall = bufs(2), bufs(2), bufs(2)
                qt = [sb(st, "c_qt%d" % i, [128, 512], BF16) for i in range(2)]
                kt = [sb(st, "c_kt%d" % i, [128, 512], BF16) for i in range(2)]
                lt = [sb(st, "c_lt%d" % i, [128, 2, 512], F32) for i in range(2)]
                Lc = sb(st, "c_Lc", [128, 2, 512], F32)
                X1 = sb(st, "c_X1", [128, 512], F32)
                X2 = sb(st, "c_X2", [128, 512], F32)
                X3 = sb(st, "c_X3", [128, 512], F32)
                kdec = [sb(st, "c_kdec%d" % i, [128, 128], BF16) for i in range(2)]
                A2 = [sb(st, "c_A%d" % i, [128, 256], BF16) for i in range(2)]
                rt = sb(st, "c_rt", [128, 2, 512], BF16)
                o = sb(st, "c_o", [128, 2, 512], F32)
                o2 = sb(st, "c_o2", [128, 512], F32)
                rs = sb(st, "c_rs", [128, 512], F32)
                ob = [sb(st, "c_ob%d" % i, [128, 512], BF16) for i in range(2)]
                gn = sb(st, "c_gn", [128, 2], F32)
                B_v, B_qF, B_kF, B_qB, B_qB2, B_kB, B_eFl, B_eEt = (Buf() for _ in range(8))
                B_sfa, B_sba, B_sr, B_st, B_L, B_X1, B_X2, B_X3, B_rt, B_o, B_o2, B_rs, B_gn = (Buf() for _ in range(13))
                B_qt, B_kt, B_lt, B_kdec, B_A2, B_ob = bufs(2), bufs(2), bufs(2), bufs(2), bufs(2), bufs(2)
                PSb = [PS[i][:].bitcast(BF16) for i in range(8)]
                MG = cb[:, C_MG:C_MG + 256]
                dma(S, "sync", gn[:], gla_g[0].rearrange("(c p) -> p c", p=128), (), [B_gn], slow=True)
                it = 0
                for hc in range(CH):
                    hs = slice(hc * 128, (hc + 1) * 128)
                    dma(S, "sync", v[:], vc[:, hc * 256:(hc + 1) * 256].rearrange("(b p) e -> p b e", p=128), (), [B_v])
                    for g in range(NG):
                        gs = slice(g * 512, (g + 1) * 512)
                        k = g % 2
                        dma(S, "sync", qt[k][:], qcT[hs, gs], (), [B_qt[k]])
                        dma(S, "sync", kt[k][:], kcT[hs, gs], (), [B_kt[k]])
                        dma(S, "sync", lt[k][:, 0, :], lfT[hs, gs], (), [B_lt[k]])
                        dma(S, "sync", lt[k][:, 1, :], lbT[hs, gs], (), [B_lt[k]])
                        for d_ in range(2):
                            for ci in range(4):
                                cs = slice(ci * 128, (ci + 1) * 128)
                                S.op("vector", (lambda e, d_=d_, cs=cs, k=k: e.tensor_tensor_scan(
                                    out=Lc[:, d_, cs], data0=cf[:, C_ONE:C_ONE + 128], data1=lt[k][:, d_, cs], initial=0.0,
                                    op0=ALU.mult, op1=ALU.add)), [B_lt[k], B_c], [B_L])
                        act(S, X1[:], Lc[:, 0, :], AF.Exp, [B_L], [B_X1], scale=-1.0 / 16)
                        cp(S, "gpsimd", eFl[:, g * 4:(g + 1) * 4], X1[:, 127::128], [B_X1], [B_eFl])
                        stt(S, qF[:, gs], qt[k][:], sc, X1[:], ALU.mult, ALU.mult, [B_qt[k], B_X1], [B_qF])
                        act(S, X2[:], Lc[:, 0, :], AF.Exp, [B_L], [B_X2], scale=1.0 / 16)
                        tt(S, "gpsimd", kF[:, gs], kt[k][:], X2[:], ALU.mult, [B_kt[k], B_X2], [B_kF])
                        act(S, eEt[:, g * 4:(g + 1) * 4], Lc[:, 1, 127::128], AF.Exp, [B_L], [B_eEt], scale=-1.0 / 16)
                        tt(S, "gpsimd", X3[:], Lc[:, 1, :], lt[k][:, 1, :], ALU.subtract, [B_L, B_lt[k]], [B_X3])
                        act(S, X1[:], X3[:], AF.Exp, [B_X3], [B_X1], scale=-1.0 / 16)
                        tt(S, "gpsimd", kB[:, gs], kt[k][:], X1[:], ALU.mult, [B_kt[k], B_X1], [B_kB])
                        act(S, X2[:], X3[:], AF.Exp, [B_X3], [B_X2], scale=1.0 / 16)
                        stt(S, qB[:, gs], qt[k][:], sc, X2[:], ALU.mult, ALU.mult, [B_qt[k], B_X2], [B_qB])
                        for ci in range(4):
                            cs = slice(ci * 128, (ci + 1) * 128)
                            ts(S, "vector", X3[:, cs], X3[:, cs], -1.0, ALU.mult, [B_X3, B_L], [B_X3],
                               s2=Lc[:, 1, ci * 128 + 127:ci * 128 + 128], op1=ALU.add)
                        act(S, X1[:], X3[:], AF.Exp, [B_X3], [B_X1], scale=-1.0 / 16)
                        stt(S, qB2[:, gs], qt[k][:], sc, X1[:], ALU.mult, ALU.mult, [B_qt[k], B_X1], [B_qB2])
                    for direction in (0, 1):
                        SA, B_sa = (SfA, B_sfa) if direction == 0 else (SbA, B_sba)
                        KX, B_kx = (kF, B_kF) if direction == 0 else (kB, B_kB)
                        first = 0 if direction == 0 else NCH - 1
                        for g8 in range(NCH // 8):
                            k = it % 2
                            it += 1
                            for j in range(8):
                                i = g8 * 8 + j
                                tr(S, PSb[k][:, j * 128:(j + 1) * 128], KX[:, i * 128:(i + 1) * 128], ident_b, [B_kx, B_c], [BPS[k]])
                            cp(S, "scalar", kdall[direction][:, g8 * 8:(g8 + 1) * 8, :], PSb[k][:, 0:1024].rearrange("p (c d) -> p c d", d=128),
                               [BPS[k]], [B_kdall[direction]])
                        memset(S, "vector", Sr2[0][:], 0.0, [B_sr2[0]])
                        memset(S, "gpsimd", SA[:, first, :], 0.0, [B_sa])
                        order = range(0, NCH - 1) if direction == 0 else range(NCH - 1, 0, -1)
                        for s_, i in enumerate(order):
                            k = it % 2
                            it += 1
                            a_, b_ = s_ % 2, (s_ + 1) % 2
                            mm(S, PS[2 + k][:, 0:256], [(kdall[direction][:, i, :], v[:, i, :])], [B_kdall[direction], B_v], [BPS[2 + k]])
                            if direction == 0:
                                act(S, kve[k][:], PS[2 + k][:, 0:256], AF.Copy, [BPS[2 + k], B_eFl], [B_kve[k]], scale=eFl[:, i:i + 1])
                                stt(S, Sr2[b_][:], Sr2[a_][:], eFl[:, i:i + 1], kve[k][:], ALU.mult, ALU.add, [B_sr2[a_], B_eFl, B_kve[k]], [B_sr2[b_]])
                                nxt = i + 1
                            else:
                                stt(S, Sr2[b_][:], Sr2[a_][:], eEt[:, i:i + 1], PS[2 + k][:, 0:256], ALU.mult, ALU.add,
                                    [B_sr2[a_], BPS[2 + k], B_eEt], [B_sr2[b_]])
                                nxt = i - 1
                            cp(S, "gpsimd", SA[:, nxt, :], Sr2[b_][:], [B_sr2[b_]], [B_sa])
                    def front(g):
                        nonlocal it
                        for ci in range(4):
                            i = g * 4 + ci
                            cs = slice(i * 128, (i + 1) * 128)
                            k = it % 2
                            it += 1
                            mm(S, PS[k][:, 0:128], [(kF[:, cs], qF[:, cs])], [B_kF, B_qF], [BPS[k]])
                            mm(S, PS[k][:, 128:256], [(kB[:, cs], qB[:, cs])], [B_kB, B_qB], [BPS[k]])
                            tt(S, "vector", A2[k][:], PS[k][:, 0:256], MG, ALU.mult, [BPS[k], B_c], [B_A2[k]])
                            for ec in range(2):
                                es = slice(ec * 128, (ec + 1) * 128)
                                pb = 2 + 2 * (g % 2) + ec
                                mm(S, PS[pb][:, ci * 128:(ci + 1) * 128],
                                   [(SfA[:, i, es], qF[:, cs]), (SbA[:, i, es], qB2[:, cs])],
                                   [B_sfa, B_sba, B_qF, B_qB2], [BPS[pb]], start=True, stop=False)
                            for ec in range(2):
                                es = slice(ec * 128, (ec + 1) * 128)
                                pb = 2 + 2 * (g % 2) + ec
                                mm(S, PS[pb][:, ci * 128:(ci + 1) * 128],
                                   [(v[:, i, es], A2[k][:, 0:128]), (v[:, i, es], A2[k][:, 128:256])],
                                   [B_v, B_A2[k]], [BPS[pb]], start=False, stop=True)

                    def post(g):
                        gs = slice(g * 512, (g + 1) * 512)
                        dma(S, "sync", rt[:], rcT[hc * 256:(hc + 1) * 256, gs].rearrange("(c p) t -> p c t", p=128), (), [B_rt])
                        for ec in range(2):
                            pb = 2 + 2 * (g % 2) + ec
                            cp(S, "scalar", o[:, ec, :], PS[pb][:], [BPS[pb]], [B_o])
                            act(S, o2[:], PS[pb][:], AF.Square, [BPS[pb]], [B_o2])
                            mm(S, PS[6][:], [(ones_f, o2[:])], [B_o2, B_c], [BPS[6]], start=(ec == 0), stop=(ec == 1))
                        act(S, rs[:], PS[6][:], AF.Ln, [BPS[6]], [B_rs], scale=1.0 / 256, bias=EPS)
                        act(S, rs[:], rs[:], AF.Exp, [B_rs], [B_rs], scale=-0.5)
                        for ec in range(2):
                            stt(S, o[:, ec, :], o[:, ec, :], gn[:, ec:ec + 1], rs[:], ALU.mult, ALU.mult, [B_rs, B_gn], [B_o])
                            kk = (g * 2 + ec) % 2
                            tt(S, "gpsimd", ob[kk][:], o[:, ec, :], rt[:, ec, :], ALU.mult, [B_o, B_rt], [B_ob[kk]])
                            dma(S, "sync", mixT[hc * 256 + ec * 128:hc * 256 + (ec + 1) * 128, gs], ob[kk][:], [B_ob[kk]], ())

                    front(0)
                    for g in range(NG):
                        if g + 1 < NG:
                            front(g + 1)
                        post(g)
                S.barrier()
                S.flush()

        def mixer_mla():
            scale = 192.0 ** -0.5
            with contextlib.ExitStack() as st:
                NG = S_ // 512
                kr = sb(st, "d_kr", [64, S_], BF16)
                qn2 = [sb(st, "d_qn%d" % j, [128, S_], BF16) for j in range(2)]
                kn2 = [sb(st, "d_kn%d" % j, [128, S_], BF16) for j in range(2)]
                qr2 = [sb(st, "d_qr%d" % j, [64, S_], BF16) for j in range(2)]
                v2 = [sb(st, "d_v%d" % j, [128, NCH, 128], BF16) for j in range(2)]
                B_qn2, B_kn2, B_qr2, B_v2 = bufs(2), bufs(2), bufs(2), bufs(2)
                NE = 6
                E = [sb(st, "d_E%d" % i, [128, 512], BF16) for i in range(NE)]
                rd = sb(st, "d_rd", [128, 512], F32)
                ob = [sb(st, "d_ob%d" % i, [128, 512], BF16) for i in range(2)]
                B_kr, B_rd = Buf(), Buf()
                B_E, B_ob = bufs(NE), bufs(2)
                dma(S, "sync", kr[:], krT[:, :], (), [B_kr])
                it = 0
                def load_head(h):
                    j = h % 2
                    hs = slice(h * 128, (h + 1) * 128)
                    dma(S, "sync", qn2[j][:], qnT[hs, :], (), [B_qn2[j]])
                    dma(S, "sync", kn2[j][:], knT[hs, :], (), [B_kn2[j]])
                    dma(S, "sync", qr2[j][:], qrT[h * 64:(h + 1) * 64, :], (), [B_qr2[j]])
                    dma(S, "sync", v2[j][:], vm[:, hs].rearrange("(b p) e -> p b e", p=128), (), [B_v2[j]])

                load_head(0)
                for h in range(DH):
                    hs = slice(h * 128, (h + 1) * 128)
                    qn, kn, qr, v = qn2[h % 2], kn2[h % 2], qr2[h % 2], v2[h % 2]
                    B_qn, B_kn, B_qr, B_v = B_qn2[h % 2], B_kn2[h % 2], B_qr2[h % 2], B_v2[h % 2]
                    if h + 1 < DH:
                        load_head(h + 1)
                    blocks = [(g, kb_) for g in range(NG) for kb_ in range(NCH)]

                    def issue_scores(bi_):
                        g, kb_ = blocks[bi_]
                        gs = slice(g * 512, (g + 1) * 512)
                        ks = slice(kb_ * 128, (kb_ + 1) * 128)
                        sbk = (it0 + bi_) % 4
                        mm(S, PS[sbk][:], [(kn[:, ks], qn[:, gs]), (kr[:, ks], qr[:, gs])], [B_kn, B_qn, B_kr, B_qr], [BPS[sbk]])

                    it0 = it
                    issue_scores(0)
                    issue_scores(1)
                    for bi_, (g, kb_) in enumerate(blocks):
                        gs = slice(g * 512, (g + 1) * 512)
                        po, pd = 4 + g % 2, 6 + g % 2
                        sbk = (it0 + bi_) % 4
                        k = (it0 + bi_) % NE
                        if bi_ + 2 < len(blocks):
                            issue_scores(bi_ + 2)
                        act(S, E[k][:], PS[sbk][:], AF.Exp, [BPS[sbk]], [B_E[k]], scale=scale)
                        mm(S, PS[po][:], [(v[:, kb_, :], E[k][:])], [B_v, B_E[k]], [BPS[po]], start=(kb_ == 0), stop=(kb_ == NCH - 1))
                        mm(S, PS[pd][:], [(ones_b, E[k][:])], [B_c, B_E[k]], [BPS[pd]], start=(kb_ == 0), stop=(kb_ == NCH - 1))
                        if kb_ == NCH - 1:
                            S.op("vector", (lambda e, pd=pd: e.reciprocal(out=rd[:], in_=PS[pd][:])), [BPS[pd]], [B_rd])
                            kk = g % 2
                            tt(S, "vector", ob[kk][:], PS[po][:], rd[:], ALU.mult, [BPS[po], B_rd], [B_ob[kk]])
                            dma(S, "sync", mixT[CH * 256 + h * 128:CH * 256 + (h + 1) * 128, gs], ob[kk][:], [B_ob[kk]], ())
                    it += len(blocks)
                S.barrier()
                S.flush()

        S.barrier()
        S.flush()
        stages = [lambda: token_phase(0), mixer_even, mixer_ret, lambda: token_phase(1), mixer_gla, mixer_mla, lambda: token_phase(2)]
        for f in stages[:cfg.get("stages", 7)]:
            f()
        S.finish()
    return nc


def _consts(S_):
    c = np.zeros((128, NCOLS), np.float32)
    p = np.arange(128)[:, None].astype(np.float32)
    f = np.arange(128)[None, :].astype(np.float32)
    c[:, C_ID:C_ID + 128] = np.eye(128, dtype=np.float32)
    c[:, C_ONE:C_ONE + 128] = 1.0
    c[:, C_MEAN:C_MEAN + 128] = 1.0 / 128
    c[:, C_POS:C_POS + 128] = np.maximum(f - p, 0)
    c[:, C_NEG:C_NEG + 128] = np.maximum(p - f, 0)
    a4 = np.tile(np.arange(128, dtype=np.float32), 4)[None, :]
    c[:, C_IDX:C_IDX + 512] = a4 + 1.0
    c[:, C_CMA:C_CMA + 512] = 128.0 - a4
    c[:, C_COLF] = 127.0 - p[:, 0]
    c[:, C_COLB] = p[:, 0]
    c[:, C_CC] = 128.0

    def perm(n, half, rot):
        m = np.zeros((128, 128), np.float32)
        for i in range(half):
            m[i, i + half] = 1.0
            m[i + half, i] = 1.0
        return m
    c[:, C_PA:C_PA + 128] = perm(128, 16, 32)
    c[:, C_PB:C_PB + 128] = perm(128, 64, 128)
    c[:, C_PM:C_PM + 128] = perm(128, 32, 64)
    f2 = np.arange(256)[None, :].astype(np.float32)
    c[:, C_MB:C_MB + 256] = ((f2 - p >= 0) & (f2 - p <= 128)).astype(np.float32)
    c[:, C_MG:C_MG + 128] = (p <= f).astype(np.float32)
    c[:, C_MG + 128:C_MG + 256] = (p > f).astype(np.float32)

    def rope_tab(theta, rot, rows):
        half = rot // 2
        inv = np.power(np.float32(theta), -np.arange(half, dtype=np.float32) * np.float32(2.0 / rot)).astype(np.float32)
        pos = np.arange(S_, dtype=np.float32)
        ang = (pos[:, None] * inv[None, :]).astype(np.float32)
        cs, sn = np.cos(ang).astype(np.float32), np.sin(ang).astype(np.float32)
        C = np.ones((rows, S_), np.float32)
        Sn = np.zeros((rows, S_), np.float32)
        C[0:half] = cs.T
        C[half:rot] = cs.T
        Sn[0:half] = -sn.T
        Sn[half:rot] = sn.T
        return np.ascontiguousarray(np.stack([C, Sn]))
    return c, rope_tab(500000.0, 32, 128), rope_tab(10000.0, 128, 128), rope_tab(10000.0, 64, 64)


_NC_CACHE = {}


def run_cfg(cfg, seqs, weights, n_cores):
    key = tuple(sorted(cfg.items()))
    if key not in _NC_CACHE:
        _NC_CACHE[key] = build(cfg)
    nc = _NC_CACHE[key]
    cst, rA, rB, rM = _consts(cfg["S"])
    base = {k: np.ascontiguousarray(np.asarray(v, dtype=np.float32)) for k, v in weights.items()}
    base.update(cst=cst, ropeA=rA, ropeB=rB, ropeM=rM)
    in_maps = []
    for i in range(n_cores):
        m = dict(base)
        m["x"] = np.ascontiguousarray(seqs[i % len(seqs)])
        in_maps.append(m)
    res = run_bass_kernel_spmd(nc, in_maps, core_ids=list(range(n_cores)))
    return [res.results[i]["y"] for i in range(len(seqs))]


def kernel(x_prompt, x_sample, **weights):
    xp = np.asarray(x_prompt, dtype=np.float32)
    xs = np.asarray(x_sample, dtype=np.float32)
    seqs = [xp[i] for i in range(xp.shape[0])] + [xs[i] for i in range(xs.shape[0])]
    outs = run_cfg(FULL_CFG, seqs, weights, 8)
    yp = np.stack(outs[:xp.shape[0]]).astype(np.float32)
    ys = np.stack(outs[xp.shape[0]:]).astype(np.float32)
    return (yp, ys)
```
